# Optimizing a Trainium2 kernel written in Bass

```python
import jax, jax.numpy as jnp
from jax import lax
import numpy as np

D_MODEL = 1024
BATCH = 2
SEQ = 16384
DEPTH = 1
DEC_BATCH = 128
DEC_SEQ = 1
PAST_LEN = 8192
PAGE_SIZE = 128

DIL_CONFIGS = ((128, 1), (512, 4), (2048, 16))
N_GROUPS = 3
HEADS_PER_GROUP = 4
N_HEADS_A = N_GROUPS * HEADS_PER_GROUP
HEAD_DIM_A = 64
DIL_BLOCK = 128
N_HEADS_B = 4
HEAD_K_B = 128
HEAD_V_B = 256
GATE_RANK = 16
GATE_TEMP = 16.0
GLA_CHUNK = 64
PEER_HEADS = 8
PEER_KEYS = 128
N_EXPERTS = PEER_KEYS * PEER_KEYS
PEER_QDIM = 256
PEER_TOPK = 16
PEER_BLOCK = 128
W_A = N_HEADS_A * HEAD_DIM_A
W_AO = HEADS_PER_GROUP * HEAD_DIM_A
W_BK = N_HEADS_B * HEAD_K_B
W_BV = N_HEADS_B * HEAD_V_B
IN_SIZES = (W_A, W_A, W_A, W_BK, W_BK, W_BV, W_BV, GATE_RANK, D_MODEL, D_MODEL)
IN_COLS = sum(IN_SIZES)
IN_SPLITS = tuple(int(o) for o in np.cumsum(IN_SIZES)[:-1])
DN_ALPHA = (2 * DEPTH) ** 0.25
DN_BETA = (8 * DEPTH) ** -0.25
LN_EPS = 1e-5

kernel_name = "hybrid_dilated_gla_peer_step"


def layer_norm(x, w=None, b=None):
    xf = x.astype(jnp.float32)
    mu = jnp.mean(xf, axis=-1, keepdims=True)
    var = jnp.mean(jnp.square(xf - mu), axis=-1, keepdims=True)
    y = (xf - mu) * lax.rsqrt(var + LN_EPS)
    if w is not None:
        y = y * w.astype(jnp.float32) + b.astype(jnp.float32)
    return y.astype(x.dtype)


def adaln(c, w_ada, b_ada):
    mod = jax.nn.silu(c) @ w_ada + b_ada
    return [m[:, None, :] for m in jnp.split(mod, 6, axis=-1)]


def modulate(x, shift, scale):
    return layer_norm(x) * (1 + scale) + shift


def alibi_slopes():
    h = jnp.arange(1, N_HEADS_A + 1, dtype=jnp.float32)
    return jnp.exp2(-8.0 * h / N_HEADS_A).reshape(N_GROUPS, HEADS_PER_GROUP)


def softmax_stats(s, valid):
    s = jnp.where(valid, s, -jnp.inf)
    m = jnp.max(s, axis=-1, keepdims=True)
    p = jnp.exp(s - m)
    z = jnp.sum(p, axis=-1, keepdims=True)
    return p / z, (m + jnp.log(z))[..., 0]


def dilated_prompt(q, k, v, slopes, window, dil):
    B, T, H, hd = q.shape
    span = dil * DIL_BLOCK
    t_pad = -(-T // span) * span
    L = t_pad // dil
    nb = L // DIL_BLOCK

    def to_phase(a):
        a = jnp.pad(a, ((0, 0), (0, t_pad - T), (0, 0), (0, 0)))
        a = jnp.transpose(a.reshape(B, L, dil, H, hd), (0, 2, 1, 3, 4))
        return a.reshape(B, dil, nb, DIL_BLOCK, H, hd)

    def with_prev(a):
        prev = jnp.pad(a, ((0, 0), (0, 0), (1, 0), (0, 0), (0, 0), (0, 0)))[:, :, :-1]
        return jnp.concatenate([prev, a], axis=3)

    qb = to_phase(q)
    kc = with_prev(to_phase(k))
    vc = with_prev(to_phase(v))
    s = jnp.einsum('brnqhd,brnkhd->brnhqk', qb, kc).astype(jnp.float32) * (hd ** -0.5)
    qi = jnp.arange(DIL_BLOCK)[:, None]
    ki = jnp.arange(2 * DIL_BLOCK)[None, :]
    steps = qi + DIL_BLOCK - ki
    in_band = (steps >= 0) & (steps <= window // dil)
    has_prev = jnp.arange(nb)[:, None, None] > 0
    valid = in_band[None] & (has_prev | (ki >= DIL_BLOCK)[None])
    bias = -slopes[:, None, None] * (steps * dil).astype(jnp.float32)[None]
    p, lse = softmax_stats(s + bias[None, None, None], valid[None, None, :, None])
    o = jnp.einsum('brnhqk,brnkhd->brnqhd', p.astype(v.dtype), vc)
    o = jnp.transpose(o.reshape(B, dil, L, H, hd), (0, 2, 1, 3, 4)).reshape(B, t_pad, H, hd)[:, :T]
    lse = jnp.transpose(lse, (0, 2, 4, 1, 3)).reshape(B, t_pad, H)[:, :T]
    return o, lse


def dilated_sample(q, k, v, buf, slopes, window, dil):
    S, hd = q.shape[1], q.shape[-1]
    W = buf.shape[1]
    kv_all = jnp.concatenate([buf, jnp.stack([k, v], axis=2)], axis=1)
    steps = jnp.arange(window // dil + 1)
    idx = W + jnp.arange(S)[:, None] - steps[None, :] * dil
    g = jnp.take(kv_all, jnp.maximum(idx, 0), axis=1)
    s = jnp.einsum('bshd,bskhd->bshk', q, g[:, :, :, 0]).astype(jnp.float32) * (hd ** -0.5)
    bias = -slopes[:, None] * (steps * dil).astype(jnp.float32)[None, :]
    p, lse = softmax_stats(s + bias, (idx >= 0)[None, :, None, :])
    o = jnp.einsum('bshk,bskhd->bshd', p.astype(v.dtype), g[:, :, :, 1])
    return o, lse, kv_all[:, S:]


def gla_chunk(S, q, k, v, g):
    q, k, v, g = (a.astype(jnp.float32) for a in (q, k, v, g))
    C = q.shape[1]
    b = jnp.cumsum(g, axis=1)
    causal = jnp.tril(jnp.ones((C, C), dtype=bool))
    diff = b[:, :, None] - b[:, None, :]
    decay = jnp.exp(jnp.where(causal[None, :, :, None, None], diff, -jnp.inf))
    scores = jnp.einsum('bthk,btshk,bshk->bhts', q, decay, k)
    o = jnp.einsum('bthk,bhkv->bthv', q * jnp.exp(b), S) + jnp.einsum('bhts,bshv->bthv', scores, v)
    b_last = b[:, -1]
    S_new = jnp.exp(b_last)[..., None] * S + jnp.einsum('bshk,bshv->bhkv', k * jnp.exp(b_last[:, None] - b), v)
    return o, S_new


def gla_prompt(q, k, v, g):
    B, T, H = q.shape[:3]
    n_chunks = T // GLA_CHUNK

    def to_chunks(a):
        return jnp.moveaxis(a.reshape(B, n_chunks, GLA_CHUNK, *a.shape[2:]), 1, 0)

    S0 = jnp.zeros((B, H, HEAD_K_B, HEAD_V_B), jnp.float32)

    def step(S, xs):
        o, S = gla_chunk(S, *xs)
        return S, o

    S_fin, o = lax.scan(step, S0, (to_chunks(q), to_chunks(k), to_chunks(v), to_chunks(g)))
    return jnp.moveaxis(o, 0, 1).reshape(B, T, H, HEAD_V_B), S_fin


def mixer_project(h, w_in, w_gla_up, b_gla):
    B, T = h.shape[:2]
    qa, ka, va, qb, kb, vb, rb, glr, ga, gb = jnp.split(h @ w_in, IN_SPLITS, axis=-1)
    heads_a = lambda a: a.reshape(B, T, N_HEADS_A, HEAD_DIM_A)
    qb = qb.reshape(B, T, N_HEADS_B, HEAD_K_B) * (HEAD_K_B ** -0.5)
    kb = kb.reshape(B, T, N_HEADS_B, HEAD_K_B)
    vb = vb.reshape(B, T, N_HEADS_B, HEAD_V_B)
    gdec = jax.nn.log_sigmoid((glr @ w_gla_up + b_gla).astype(jnp.float32)) / GATE_TEMP
    gdec = gdec.reshape(B, T, N_HEADS_B, HEAD_K_B)
    return heads_a(qa), heads_a(ka), heads_a(va), qb, kb, vb, gdec, rb, ga, gb


def mixer_merge(o_groups, lse_groups, o_gla, rb, ga, gb, gla_norm_w, w_br_a, w_br_b, w_out):
    B, T = rb.shape[:2]
    dt = rb.dtype
    wts = jax.nn.softmax(jnp.stack(lse_groups), axis=0)
    oa = jnp.sum(wts[..., None] * jnp.stack(o_groups).astype(jnp.float32), axis=0).reshape(B, T, W_AO)
    ob = o_gla * lax.rsqrt(jnp.mean(jnp.square(o_gla), axis=-1, keepdims=True) + LN_EPS)
    ob = ob.reshape(B, T, W_BV).astype(dt) * gla_norm_w * jax.nn.silu(rb)
    merged = jax.nn.sigmoid(ga) * (oa.astype(dt) @ w_br_a) + jax.nn.sigmoid(gb) * (ob @ w_br_b)
    return merged @ w_out


def mixer_prompt(h, w_in, w_gla_up, b_gla, gla_norm_w, w_br_a, w_br_b, w_out):
    qa, ka, va, qb, kb, vb, gdec, rb, ga, gb = mixer_project(h, w_in, w_gla_up, b_gla)
    slopes = alibi_slopes()
    T = h.shape[1]
    o_groups, lse_groups, bufs = [], [], []
    for gi, (window, dil) in enumerate(DIL_CONFIGS):
        hs = slice(gi * HEADS_PER_GROUP, (gi + 1) * HEADS_PER_GROUP)
        o, lse = dilated_prompt(qa[:, :, hs], ka[:, :, hs], va[:, :, hs], slopes[gi], window, dil)
        o_groups.append(o)
        lse_groups.append(lse)
        keep = min(window, T)
        bufs.append(jnp.stack([ka[:, T - keep:, hs], va[:, T - keep:, hs]], axis=2))
    o_gla, S_fin = gla_prompt(qb, kb, vb, gdec)
    out = mixer_merge(o_groups, lse_groups, o_gla, rb, ga, gb, gla_norm_w, w_br_a, w_br_b, w_out)
    return out, bufs, S_fin


def mixer_sample(h, bufs_in, S0, w_in, w_gla_up, b_gla, gla_norm_w, w_br_a, w_br_b, w_out):
    qa, ka, va, qb, kb, vb, gdec, rb, ga, gb = mixer_project(h, w_in, w_gla_up, b_gla)
    slopes = alibi_slopes()
    o_groups, lse_groups, bufs = [], [], []
    for gi, (window, dil) in enumerate(DIL_CONFIGS):
        hs = slice(gi * HEADS_PER_GROUP, (gi + 1) * HEADS_PER_GROUP)
        o, lse, nbuf = dilated_sample(qa[:, :, hs], ka[:, :, hs], va[:, :, hs], bufs_in[gi], slopes[gi], window, dil)
        o_groups.append(o)
        lse_groups.append(lse)
        bufs.append(nbuf)
    o_gla, S_new = gla_chunk(S0.astype(jnp.float32), qb, kb, vb, gdec)
    out = mixer_merge(o_groups, lse_groups, o_gla, rb, ga, gb, gla_norm_w, w_br_a, w_br_b, w_out)
    return out, bufs, S_new


def peer(h, w_pq, peer_k1, peer_k2, peer_u, peer_v):
    shp = h.shape
    hf = h.reshape(-1, D_MODEL)
    n = hf.shape[0]
    n_pad = -(-n // PEER_BLOCK) * PEER_BLOCK
    hb = jnp.pad(hf, ((0, n_pad - n), (0, 0))).reshape(-1, PEER_BLOCK, D_MODEL)
    half = PEER_QDIM // 2

    def one_block(xb):
        qv = (xb @ w_pq).reshape(PEER_BLOCK, PEER_HEADS, PEER_QDIM)
        s1 = jnp.einsum('thd,hkd->thk', qv[..., :half], peer_k1).astype(jnp.float32)
        s2 = jnp.einsum('thd,hkd->thk', qv[..., half:], peer_k2).astype(jnp.float32)
        v1, i1 = lax.top_k(s1, PEER_TOPK)
        v2, i2 = lax.top_k(s2, PEER_TOPK)
        cand = (v1[..., :, None] + v2[..., None, :]).reshape(PEER_BLOCK, PEER_HEADS, PEER_TOPK * PEER_TOPK)
        sc, ci = lax.top_k(cand, PEER_TOPK)
        e = (jnp.take_along_axis(i1, ci // PEER_TOPK, axis=-1) * PEER_KEYS
             + jnp.take_along_axis(i2, ci % PEER_TOPK, axis=-1))
        gw = jax.nn.softmax(sc, axis=-1)
        act = jax.nn.gelu(jnp.einsum('td,thkd->thk', xb, peer_u[e]), approximate=False)
        return jnp.einsum('thk,thkd->td', (gw * act).astype(xb.dtype), peer_v[e])

    y = lax.map(one_block, hb)
    return y.reshape(n_pad, D_MODEL)[:n].reshape(shp)


def decoder_layer(x, c, mixer_fn, w_ada, b_ada, ln1_w, ln1_b, w_pq, peer_k1, peer_k2, peer_u, peer_v, ln2_w, ln2_b):
    sh1, sc1, g1, sh2, sc2, g2 = adaln(c, w_ada, b_ada)
    mix, bufs, S = mixer_fn(modulate(x, sh1, sc1))
    x = layer_norm(DN_ALPHA * x + g1 * mix, ln1_w, ln1_b)
    ff = peer(modulate(x, sh2, sc2), w_pq, peer_k1, peer_k2, peer_u, peer_v)
    x = layer_norm(DN_ALPHA * x + g2 * ff, ln2_w, ln2_b)
    return x, bufs, S


def setup_inputs(seed: int = 0) -> dict:
    key = jax.random.key(seed)
    ks = list(jax.random.split(key, 40))

    def nrm(shape, scale):
        return jax.random.normal(ks.pop(), shape, jnp.float32) * scale

    def kv_cache(window):
        return nrm((DEPTH, DEC_BATCH, min(window, PAST_LEN), 2, HEADS_PER_GROUP, HEAD_DIM_A), 1.0)

    D = D_MODEL
    return {
        "x_prompt": nrm((BATCH, SEQ, D), 1.0),
        "x_sample": nrm((DEC_BATCH, DEC_SEQ, D), 1.0),
        "c_prompt": nrm((BATCH, D), 1.0),
        "c_sample": nrm((DEC_BATCH, D), 1.0),
        "cache_kv_w128": kv_cache(DIL_CONFIGS[0][0]),
        "cache_kv_w512": kv_cache(DIL_CONFIGS[1][0]),
        "cache_kv_w2048": kv_cache(DIL_CONFIGS[2][0]),
        "state_gla": nrm((DEPTH, DEC_BATCH, N_HEADS_B, HEAD_K_B, HEAD_V_B), 0.3),
        "w_ada": nrm((DEPTH, D, 6 * D), 0.5 * D ** -0.5),
        "b_ada": nrm((DEPTH, 6 * D), 0.02),
        "w_in": nrm((DEPTH, D, IN_COLS), D ** -0.5),
        "w_gla_up": nrm((DEPTH, GATE_RANK, W_BK), GATE_RANK ** -0.5),
        "b_gla": nrm((DEPTH, W_BK), 0.1),
        "gla_norm_w": 1.0 + nrm((DEPTH, W_BV), 0.02),
        "w_br_a": nrm((DEPTH, W_AO, D), W_AO ** -0.5),
        "w_br_b": nrm((DEPTH, W_BV, D), W_BV ** -0.5),
        "w_out": nrm((DEPTH, D, D), DN_BETA * D ** -0.5),
        "ln1_w": 1.0 + nrm((DEPTH, D), 0.02),
        "ln1_b": nrm((DEPTH, D), 0.02),
        "w_pq": nrm((DEPTH, D, PEER_HEADS * PEER_QDIM), D ** -0.5),
        "peer_k1": nrm((DEPTH, PEER_HEADS, PEER_KEYS, PEER_QDIM // 2), (PEER_QDIM // 2) ** -0.5),
        "peer_k2": nrm((DEPTH, PEER_HEADS, PEER_KEYS, PEER_QDIM // 2), (PEER_QDIM // 2) ** -0.5),
        "peer_u": nrm((DEPTH, N_EXPERTS, D), D ** -0.5),
        "peer_v": nrm((DEPTH, N_EXPERTS, D), DN_BETA * (PEER_HEADS * PEER_TOPK) ** -0.5),
        "ln2_w": 1.0 + nrm((DEPTH, D), 0.02),
        "ln2_b": nrm((DEPTH, D), 0.02),
    }


def reference(x_prompt, x_sample, c_prompt, c_sample, cache_kv_w128, cache_kv_w512, cache_kv_w2048, state_gla,
              w_ada, b_ada, w_in, w_gla_up, b_gla, gla_norm_w, w_br_a, w_br_b, w_out, ln1_w, ln1_b,
              w_pq, peer_k1, peer_k2, peer_u, peer_v, ln2_w, ln2_b):
    yp, ys = x_prompt, x_sample
    p128, p512, p2048, pgla = [], [], [], []
    s128, s512, s2048, sgla = [], [], [], []
    for l in range(DEPTH):
        mix_w = (w_in[l], w_gla_up[l], b_gla[l], gla_norm_w[l], w_br_a[l], w_br_b[l], w_out[l])
        rest_w = (w_ada[l], b_ada[l], ln1_w[l], ln1_b[l], w_pq[l], peer_k1[l], peer_k2[l],
                  peer_u[l], peer_v[l], ln2_w[l], ln2_b[l])
        yp, bufs_p, S_p = decoder_layer(yp, c_prompt, lambda h: mixer_prompt(h, *mix_w), *rest_w)
        caches_l = (cache_kv_w128[l], cache_kv_w512[l], cache_kv_w2048[l])
        ys, bufs_s, S_s = decoder_layer(ys, c_sample, lambda h: mixer_sample(h, caches_l, state_gla[l], *mix_w), *rest_w)
        p128.append(bufs_p[0]); p512.append(bufs_p[1]); p2048.append(bufs_p[2]); pgla.append(S_p)
        s128.append(bufs_s[0]); s512.append(bufs_s[1]); s2048.append(bufs_s[2]); sgla.append(S_s)
    kv_w128_prompt = jnp.stack(p128)
    kv_w512_prompt = jnp.stack(p512)
    kv_w2048_prompt = jnp.stack(p2048)
    gla_prompt_state = jnp.stack(pgla)
    kv_w128_sample = jnp.stack(s128)
    kv_w512_sample = jnp.stack(s512)
    kv_w2048_sample = jnp.stack(s2048)
    gla_sample_state = jnp.stack(sgla)
    return (yp, ys, kv_w128_prompt, kv_w512_prompt, kv_w2048_prompt, gla_prompt_state,
            kv_w128_sample, kv_w512_sample, kv_w2048_sample, gla_sample_state)
```

```python
import numpy as np
from contextlib import ExitStack
import concourse.bass as bass
import concourse.mybir as mybir
from concourse.bass_utils import run_bass_kernel_spmd

F32 = mybir.dt.float32
BF16 = mybir.dt.bfloat16
I32 = mybir.dt.int32
U32 = mybir.dt.uint32
AF = mybir.ActivationFunctionType
ALU = mybir.AluOpType
AX = mybir.AxisListType

D = 1024
SEG = 4096
NPRE = 12288
NDS = 72
DN_ALPHA = 2.0 ** 0.25
LN_EPS = 1e-5
C_QA, C_KA, C_VA, C_QB, C_KB, C_VB, C_RB, C_GLR, C_GA, C_GB = 0, 768, 1536, 2304, 2816, 3328, 4352, 5376, 5392, 6416
DILS = (1, 4, 16)
WINS = (128, 512, 2048)


class Buf:
    __slots__ = ("w", "r")

    def __init__(self):
        self.w = None
        self.r = {}


class T:
    def __init__(self, t, n=1):
        self.t = t
        self.bs = [Buf() for _ in range(n)]
        self.b = self.bs[0]


class KB:
    def __init__(self, nc):
        self.nc = nc
        self.E = {"pe": nc.tensor, "act": nc.scalar, "dve": nc.vector, "pool": nc.gpsimd, "sp": nc.sync}
        self.es = ExitStack()
        self.semobj = {}
        self.cnt = {}
        self.known = {}
        for n in self.E:
            self.semobj["e:" + n] = self.es.enter_context(nc.semaphore("s_" + n))
            self.cnt[n] = 0
            self.known[n] = {}
        for j in range(NDS):
            self.semobj["d:%d" % j] = self.es.enter_context(nc.semaphore("d%d" % j))
        self.duse = [0] * NDS
        self.drr = 0
        self.uid = 0

    def sb(self, st, name, shape, dt, n=1):
        self.uid += 1
        return T(st.enter_context(self.nc.sbuf_tensor("%s_%d" % (name, self.uid), shape, dt)), n)

    def ps(self, st, name, shape, dt, n=1):
        self.uid += 1
        return T(st.enter_context(self.nc.psum_tensor("%s_%d" % (name, self.uid), shape, dt)), n)

    def _wait(self, en, deps):
        need = {}
        for kk, v in deps:
            if v > need.get(kk, 0):
                need[kk] = v
        kn = self.known[en]
        for kk, v in need.items():
            if kn.get(kk, 0) >= v:
                continue
            self.E[en].wait_ge(self.semobj[kk], v)
            kn[kk] = v

    def _deps(self, en, reads, writes, is_dma):
        own = None if is_dma else "e:" + en
        deps = []
        for b in reads:
            if b.w is not None:
                deps.append(b.w)
        for b in writes:
            if b.w is not None and b.w[0] != own:
                deps.append(b.w)
            for kk, v in b.r.items():
                if kk != own:
                    deps.append((kk, v))
        return deps

    def _commit(self, ev, reads, writes):
        for b in reads:
            if ev[1] > b.r.get(ev[0], 0):
                b.r[ev[0]] = ev[1]
        for b in writes:
            b.w = ev
            b.r = {}

    @staticmethod
    def _flat(items):
        out = []
        for x in items:
            if isinstance(x, T):
                out.extend(x.bs)
            else:
                out.append(x)
        return out

    def op(self, en, fn, reads=(), writes=()):
        reads = self._flat(reads)
        writes = self._flat(writes)
        self._wait(en, self._deps(en, reads, writes, False))
        self.cnt[en] += 1
        ev = ("e:" + en, self.cnt[en])
        fn(self.E[en]).then_inc(self.semobj[ev[0]], 1)
        self._commit(ev, reads, writes)

    def dma(self, q, out, in_, reads=(), writes=(), indirect=None, **kw):
        reads = self._flat(reads)
        writes = self._flat(writes)
        j = self.drr
        self.drr = (self.drr + 1) % NDS
        key = "d:%d" % j
        deps = self._deps(q, reads, writes, True)
        if self.duse[j] > 0:
            deps.append((key, 16 * self.duse[j]))
        self._wait(q, deps)
        self.duse[j] += 1
        ev = (key, 16 * self.duse[j])
        if indirect is None:
            ins = self.E[q].dma_start(out=out, in_=in_, **kw)
        else:
            ins = self.E[q].indirect_dma_start(out=out, out_offset=None, in_=in_,
                                               in_offset=bass.IndirectOffsetOnAxis(ap=indirect, axis=0))
        ins.then_inc(self.semobj[key], 16)
        self._commit(ev, reads, writes)

    def barrier(self):
        deps = [("e:" + n, c) for n, c in self.cnt.items() if c > 0]
        deps += [("d:%d" % j, 16 * u) for j, u in enumerate(self.duse) if u > 0]
        for en in self.E:
            self._wait(en, [d for d in deps if d[0] != "e:" + en])


def bcast_rows(ap_row, p):
    n = ap_row.shape[-1]
    return bass.AP(ap_row.tensor, ap_row.offset, [[0, p], [1, n]])


def build():
    nc = bass.Bass("TRN2", target_bir_lowering=False)
    k = KB(nc)

    def din(name, shape, dt=F32):
        return nc.dram_tensor(name, shape, dt, kind="ExternalInput").ap()

    def dout(name, shape, dt=F32):
        return nc.dram_tensor(name, shape, dt, kind="ExternalOutput").ap()

    def dscr(name, shape, dt=F32):
        return T(nc.dram_tensor(name, shape, dt, kind="Internal").ap(), 1)

    I = dict(
        xo=din("xo", [SEG, D]), xp=din("xp", [NPRE, D]), segf=din("segf", [128, 1]),
        cP=din("cP", [1, D]), cS=din("cS", [128, D]), xs=din("xs", [128, D]),
        c128=din("c128", [16, 128, 512]), c512=din("c512", [16, 512, 512]), c2048=din("c2048", [16, 2048, 512]),
        sgla=din("sgla", [16, 4, 128, 256]),
        w_ada=din("w_ada", [D, 6 * D]), b_ada=din("b_ada", [1, 6 * D]), w_in=din("w_in", [D, 7440]),
        w_up=din("w_up", [16, 512]), b_gla=din("b_gla", [1, 512]), gnw=din("gnw", [1, D]),
        w_br_a=din("w_br_a", [256, D]), w_br_b=din("w_br_b", [D, D]), w_out=din("w_out", [D, D]),
        ln1_w=din("ln1_w", [1, D]), ln1_b=din("ln1_b", [1, D]), w_pq=din("w_pq", [D, 2048]),
        pk1=din("pk1", [8, 128, 128]), pk2=din("pk2", [8, 128, 128]),
        pu=din("pu", [16384, D]), pv=din("pv", [16384, D]),
        ln2_w=din("ln2_w", [1, D]), ln2_b=din("ln2_b", [1, D]),
        identf=din("identf", [128, 128]), tri=din("tri", [128, 128]), tris=din("tris", [128, 128]),
        biasT=din("biasT", [128, 12, 256]), iota16=din("iota16", [128, 16]), biasS=din("biasS", [1, 3 * 128 * 4]),
    )
    O = dict(
        yo=dout("yo", [SEG, D]), ys=dout("ys", [16, D]),
        kvp128=dout("kvp128", [128, 512]), kvp512=dout("kvp512", [512, 512]), kvp2048=dout("kvp2048", [2048, 512]),
        glap=dout("glap", [4, 128, 256]),
        kvs128=dout("kvs128", [16, 128, 512]), kvs512=dout("kvs512", [16, 512, 512]),
        kvs2048=dout("kvs2048", [16, 2048, 512]), glas=dout("glas", [16, 4, 128, 256]),
    )
    NT = SEG // 128 + 1
    modS = dscr("modS", [128, 6 * D])
    modP = dscr("modP", [128, 6 * D])
    OAT = dscr("OAT", [4, 64, NT * 128], BF16)
    OB = dscr("OB", [NT * 128, D], BF16)
    X1 = dscr("X1", [NT * 128, D], F32)
    PUV = T(nc.dram_tensor("PUV", [16384, 2 * D], BF16, kind="Internal").ap(), 64)
    dram_out = T(None, 1)

    g = ExitStack()
    identf = k.sb(g, "identf", [128, 128], F32)
    identb = k.sb(g, "identb", [128, 128], BF16)
    trif = k.sb(g, "trif", [128, 128], F32)
    trisf = k.sb(g, "trisf", [128, 128], F32)
    onesf = k.sb(g, "onesf", [128, 128], F32)
    onesb = k.sb(g, "onesb", [128, 128], BF16)
    segc = k.sb(g, "segc", [128, 1], F32)
    S = k.sb(g, "S", [128, 4, 256], F32)
    Sb = k.sb(g, "Sb", [128, 4, 256], BF16)
    pbank = [k.ps(g, "pf%d" % i, [128, 512], F32) for i in range(6)]
    pbb = [k.ps(g, "pb%d" % i, [128, 1024], BF16) for i in range(2)]

    k.dma("sp", identf.t[:], I["identf"], [], [identf])
    k.dma("pool", identb.t[:], I["identf"], [], [identb])
    k.dma("sp", trif.t[:], I["tri"], [], [trif])
    k.dma("sp", trisf.t[:], I["tris"], [], [trisf])
    k.dma("sp", segc.t[:], I["segf"], [], [segc])
    k.op("pool", lambda e: e.memset(onesf.t[:], 1.0), [], [onesf])
    k.op("pool", lambda e: e.memset(onesb.t[:], 1.0), [], [onesb])
    k.op("pool", lambda e: e.memset(S.t[:], 0.0), [], [S])
    k.op("pool", lambda e: e.memset(Sb.t[:], 0.0), [], [Sb])

    w_in_v = I["w_in"].rearrange("(c p) n -> p c n", p=128)

    def layer_norm(*a):
        for _ in layer_norm_g(*a):
            pass

    def layer_norm_g(st_tmp, src_ap, src_bufs, A_ap, B_ap, ab_bufs, out_ap, out_bufs, tag):
        st_tmp["n"] += 1
        sel = st_tmp["n"] % 2
        junk, s1, s2, t1 = st_tmp["junk"][sel], st_tmp["s1"][sel], st_tmp["s2"][sel], st_tmp["t1"][sel]
        k.op("act", lambda e: e.activation(out=junk.t[:], in_=src_ap, func=AF.Identity, accum_out=s1.t[:, 0:1]),
             src_bufs, [junk, s1])
        k.op("act", lambda e: e.activation(out=junk.t[:], in_=src_ap, func=AF.Square, accum_out=s2.t[:, 0:1]),
             src_bufs, [junk, s2])
        yield
        k.op("dve", lambda e: e.tensor_scalar(out=t1.t[:, 0:1], in0=s1.t[:, 0:1], scalar1=1.0 / D, scalar2=None,
                                              op0=ALU.mult), [s1], [t1])
        k.op("dve", lambda e: e.tensor_tensor(out=t1.t[:, 1:2], in0=t1.t[:, 0:1], in1=t1.t[:, 0:1], op=ALU.mult),
             [t1], [t1])
        k.op("dve", lambda e: e.scalar_tensor_tensor(out=t1.t[:, 2:3], in0=s2.t[:, 0:1], scalar=1.0 / D,
                                                     in1=t1.t[:, 1:2], op0=ALU.mult, op1=ALU.subtract), [s2, t1], [t1])
        k.op("dve", lambda e: e.tensor_scalar(out=t1.t[:, 2:3], in0=t1.t[:, 2:3], scalar1=LN_EPS, scalar2=None,
                                              op0=ALU.add), [t1], [t1])
        yield
        k.op("act", lambda e: e.activation(out=t1.t[:, 3:4], in_=t1.t[:, 2:3], func=AF.Ln), [t1], [t1])
        k.op("act", lambda e: e.activation(out=t1.t[:, 4:5], in_=t1.t[:, 3:4], func=AF.Exp, scale=-0.5), [t1], [t1])
        k.op("dve", lambda e: e.tensor_scalar(out=t1.t[:, 5:6], in0=t1.t[:, 0:1], scalar1=t1.t[:, 4:5], scalar2=-1.0,
                                              op0=ALU.mult, op1=ALU.mult), [t1], [t1])
        k.op("act", lambda e: e.activation(out=junk.t[:], in_=src_ap, func=AF.Identity, bias=t1.t[:, 5:6],
                                           scale=t1.t[:, 4:5]), src_bufs + [t1], [junk])
        yield
        k.op("dve", lambda e: e.tensor_tensor(out=junk.t[:], in0=junk.t[:], in1=A_ap, op=ALU.mult),
             [junk] + ab_bufs, [junk])
        k.op("dve", lambda e: e.tensor_tensor(out=out_ap, in0=junk.t[:], in1=B_ap, op=ALU.add),
             [junk] + ab_bufs, out_bufs)

    def ln_scratch(st):
        return dict(n=0, junk=[k.sb(st, "lnjunk", [128, D], F32) for _ in range(2)],
                    s1=[k.sb(st, "lns1", [128, 1], F32) for _ in range(2)],
                    s2=[k.sb(st, "lns2", [128, 1], F32) for _ in range(2)],
                    t1=[k.sb(st, "lnt1", [128, 8], F32) for _ in range(2)])

    def transpose8(src, dst, pb, rows=128):
        for c in range(8):
            k.op("pe", lambda e, c=c: e.transpose(out=pb.t[:, c * 128:(c + 1) * 128], in_=src.t[:, c * 128:(c + 1) * 128],
                                                  identity=identb.t[:]), [src, identb], [pb])
        k.op("act", lambda e: e.activation(out=dst.t[:].rearrange("p c t -> p (c t)"), in_=pb.t[:], func=AF.Copy),
             [pb], [dst])

    def load_w(st, name, dram_view, c0, ncols, q="pool"):
        npc = (ncols + 511) // 512
        w = k.sb(st, name, [128, 8, ncols], BF16, n=8 * npc)
        for c in range(8):
            for pi, j0 in enumerate(range(0, ncols, 512)):
                j1 = min(ncols, j0 + 512)
                k.dma(q, w.t[:, c, j0:j1], dram_view[:, c, c0 + j0:c0 + j1], [], [w.bs[c * npc + pi]])
        return w

    def phase0():
        st = ExitStack()
        cs = k.sb(st, "cs", [128, D], F32)
        cp = k.sb(st, "cp", [1, D], F32)
        csT = k.sb(st, "csT", [128, 8, 128], BF16)
        cpT = k.sb(st, "cpT", [128, 8], BF16)
        bada = k.sb(st, "bada", [128, 512], F32)
        wblk = [k.sb(st, "wblk%d" % i, [128, 8, 512], BF16, n=8) for i in range(2)]
        mrow = k.sb(st, "mrow", [1, 512], F32)
        stg = [k.sb(st, "stg%d" % i, [128, 512], F32) for i in range(2)]
        stg2 = [k.sb(st, "stgp%d" % i, [128, 512], F32) for i in range(2)]
        k.dma("sp", cs.t[:], I["cS"], [], [cs])
        k.dma("sp", cp.t[:], I["cP"], [], [cp])
        k.op("act", lambda e: e.activation(out=cs.t[:], in_=cs.t[:], func=AF.Silu), [cs], [cs])
        k.op("act", lambda e: e.activation(out=cp.t[:], in_=cp.t[:], func=AF.Silu), [cp], [cp])
        pf = pbank[0]
        for c in range(8):
            k.op("pe", lambda e, c=c: e.transpose(out=pf.t[:, 0:128], in_=cs.t[:, c * 128:(c + 1) * 128],
                                                  identity=identf.t[:]), [cs, identf], [pf])
            k.op("pe", lambda e, c=c: e.matmul(pf.t[:, 128:129], lhsT=cp.t[0:1, c * 128:(c + 1) * 128],
                                               rhs=onesf.t[0:1, 0:1], start=True, stop=True), [cp, onesf], [pf])
            k.op("dve", lambda e, c=c: e.tensor_copy(out=csT.t[:, c, :], in_=pf.t[:, 0:128]), [pf], [csT])
            k.op("dve", lambda e, c=c: e.tensor_copy(out=cpT.t[:, c:c + 1], in_=pf.t[:, 128:129]), [pf], [cpT])
        w_ada_v = I["w_ada"].rearrange("(c p) n -> p c n", p=128)
        for blk in range(12):
            wb = wblk[blk % 2]
            cols = slice(blk * 512, (blk + 1) * 512)
            for c in range(8):
                k.dma("pool", wb.t[:, c, :], w_ada_v[:, c, cols], [], [wb.bs[c]])
            k.dma("sp", bada.t[:], bcast_rows(I["b_ada"][:, cols], 128), [], [bada])
            p1, p2, p3 = pbank[1], pbank[2], pbank[3]
            for c in range(8):
                k.op("pe", lambda e, c=c: e.matmul(p1.t[:, :], lhsT=csT.t[:, c, :], rhs=wb.t[:, c, :], start=(c == 0),
                                                   stop=(c == 7)), [csT, wb], [p1])
            for c in range(8):
                k.op("pe", lambda e, c=c: e.matmul(p2.t[0:1, :], lhsT=cpT.t[:, c:c + 1], rhs=wb.t[:, c, :], start=(c == 0),
                                                   stop=(c == 7)), [cpT, wb], [p2])
            k.op("act", lambda e: e.activation(out=mrow.t[:], in_=p2.t[0:1, :], func=AF.Copy), [p2], [mrow])
            k.op("pe", lambda e: e.matmul(p3.t[:, :], lhsT=onesf.t[0:1, :], rhs=mrow.t[0:1, :], start=True, stop=True),
                 [onesf, mrow], [p3])
            sa, sp_ = stg[blk % 2], stg2[blk % 2]
            k.op("dve", lambda e: e.tensor_tensor(out=sa.t[:], in0=p1.t[:], in1=bada.t[:], op=ALU.add), [p1, bada], [sa])
            k.op("dve", lambda e: e.tensor_tensor(out=sp_.t[:], in0=p3.t[:], in1=bada.t[:], op=ALU.add), [p3, bada], [sp_])
            if blk in (2, 3, 8, 9):
                k.op("pool", lambda e: e.tensor_scalar_add(out=sa.t[:], in0=sa.t[:], scalar1=1.0), [sa], [sa])
                k.op("pool", lambda e: e.tensor_scalar_add(out=sp_.t[:], in0=sp_.t[:], scalar1=1.0), [sp_], [sp_])
            k.dma("sp", modS.t[:, cols], sa.t[:], [sa], [modS])
            k.dma("sp", modP.t[:, cols], sp_.t[:], [sp_], [modP])
        k.barrier()
        st.close()

    def phaseG():
        st = ExitStack()
        A1 = k.sb(st, "A1", [128, D], F32)
        B1 = k.sb(st, "B1", [128, D], F32)
        k.dma("sp", A1.t[:], modP.t[:, 1 * D:2 * D], [modP], [A1])
        k.dma("sp", B1.t[:], modP.t[:, 0:D], [modP], [B1])
        Wq = load_w(st, "Wq", w_in_v, C_QB, 512)
        Wk = load_w(st, "Wk", w_in_v, C_KB, 512)
        Wv = load_w(st, "Wv", w_in_v, C_VB, 1024)
        Wg = load_w(st, "Wg", w_in_v, C_GLR, 16)
        Wka = load_w(st, "Wka", w_in_v, C_KA, 768)
        Wva = load_w(st, "Wva", w_in_v, C_VA, 768)
        wup = k.sb(st, "wup", [16, 512], BF16)
        bgl = k.sb(st, "bgl", [1, 512], BF16)
        gnw = k.sb(st, "gnw", [128, D], F32)
        k.dma("pool", wup.t[:], I["w_up"], [], [wup])
        k.dma("pool", bgl.t[:], I["b_gla"], [], [bgl])
        k.dma("sp", gnw.t[:], bcast_rows(I["gnw"], 128), [], [gnw])
        lns = ln_scratch(st)
        xt = [k.sb(st, "xt%d" % i, [128, D], F32) for i in range(2)]
        hbs = [k.sb(st, "hb%d" % i, [128, D], BF16) for i in range(2)]
        hTs_ = [k.sb(st, "hT%d" % i, [128, 8, 128], BF16) for i in range(2)]
        ee = k.sb(st, "ee", [128, 512], F32)
        ll = k.sb(st, "ll", [128, 512], F32)
        e3 = k.sb(st, "e3", [128, 512], F32)
        e1 = k.sb(st, "e1", [128, 512], F32)
        khat = k.sb(st, "khat", [128, 512], BF16)
        kchk = k.sb(st, "kchk", [128, 512], BF16)
        qtil = k.sb(st, "qtil", [128, 512], BF16)
        qkT = k.sb(st, "qkT", [128, 8, 128], BF16)
        dec = k.sb(st, "dec", [128, 4], F32)
        scT = k.sb(st, "scT", [128, 4, 128], BF16)
        og = k.sb(st, "og", [128, 4, 256], F32)
        ssq = k.sb(st, "ssq", [128, 8], F32)
        obt = k.sb(st, "obt", [128, D], BF16)
        kvst = [k.sb(st, "kvst%d" % i, [128, 512], F32) for i in range(2)]
        keep = k.sb(st, "keep", [128, 4], F32)
        for j in range(3):
            k.op("dve", lambda e, j=j: e.tensor_scalar(out=keep.t[:, j:j + 1], in0=segc.t[:, 0:1], scalar1=float(j - 2),
                                                       scalar2=0.0, op0=ALU.add, op1=ALU.max), [segc], [keep])
            k.op("dve", lambda e, j=j: e.tensor_scalar(out=keep.t[:, j:j + 1], in0=keep.t[:, j:j + 1], scalar1=1.0,
                                                       scalar2=None, op0=ALU.min), [keep], [keep])
        cvb = [k.sb(st, "cvb%d" % i, [128, 4, D], BF16) for i in range(4)]

        def convert_chunk(ci):
            src, half = (I["pu"], 0) if ci < 32 else (I["pv"], 1)
            cj = ci % 32
            cv = cvb[ci % 4]
            rows = slice(cj * 512, (cj + 1) * 512)
            k.dma("pool", cv.t[:], src[rows, :].rearrange("(p j) n -> p j n", j=4), [], [cv])
            k.dma("sp", PUV.t[rows, half * D:(half + 1) * D].rearrange("(p j) n -> p j n", j=4), cv.t[:], [cv], [PUV.bs[ci]])
        ntile_pre = NPRE // 128
        ntile = ntile_pre + SEG // 128
        pK, pV0, pV1, pM = pbank[1], pbank[2], pbank[3], pbank[5]
        pX, pD = pbank[4], pbank[0]
        pTb, pQK = pbb[0], pbb[1]
        glrTs = [k.sb(st, "glrT%d" % i, [16, 128], BF16) for i in range(2)]
        vbfs = [k.sb(st, "vbf%d" % i, [128, 1024], BF16) for i in range(2)]
        ksbs = [k.sb(st, "ksb%d" % i, [128, 512], F32) for i in range(2)]
        qsbs = [k.sb(st, "qsb%d" % i, [128, 512], F32) for i in range(2)]

        def front1(ti):
            own = ti >= ntile_pre
            to = ti - ntile_pre
            src = I["xo"][to * 128:(to + 1) * 128, :] if own else I["xp"][ti * 128:(ti + 1) * 128, :]
            x = xt[ti % 2]
            hb, hT = hbs[ti % 2], hTs_[ti % 2]
            glrT, vbf, ksb, qsb = glrTs[ti % 2], vbfs[ti % 2], ksbs[ti % 2], qsbs[ti % 2]
            k.dma("sp", x.t[:], src, [], [x])
            yield from layer_norm_g(lns, x.t[:], [x], A1.t[:], B1.t[:], [A1, B1], hb.t[:], [hb], "g")
            if ti % 2 == 0:
                convert_chunk(ti // 2)
            yield

        def front2(ti):
            own = ti >= ntile_pre
            to = ti - ntile_pre
            hb, hT = hbs[ti % 2], hTs_[ti % 2]
            glrT, vbf, ksb, qsb = glrTs[ti % 2], vbfs[ti % 2], ksbs[ti % 2], qsbs[ti % 2]
            transpose8(hb, hT, pTb)
            yield
            for c in range(8):
                k.op("pe", lambda e, c=c: e.matmul(pK.t[:, :], lhsT=hT.t[:, c, :], rhs=Wk.t[:, c, :], start=(c == 0),
                                                   stop=(c == 7)), [hT, Wk], [pK])
            k.op("act", lambda e: e.activation(out=ksb.t[:], in_=pK.t[:], func=AF.Copy), [pK], [ksb])
            yield
            for half, pv in ((0, pV0), (1, pV1)):
                for c in range(8):
                    k.op("pe", lambda e, c=c, half=half, pv=pv: e.matmul(
                        pv.t[:, :], lhsT=hT.t[:, c, :], rhs=Wv.t[:, c, half * 512:(half + 1) * 512], start=(c == 0),
                        stop=(c == 7)), [hT, Wv], [pv])
                yield
            for c in range(8):
                k.op("pe", lambda e, c=c: e.matmul(pM.t[0:16, 0:128], lhsT=Wg.t[:, c, :], rhs=hT.t[:, c, :], start=(c == 0),
                                                   stop=(c == 7)), [hT, Wg], [pM])
            k.op("act", lambda e: e.activation(out=glrT.t[:], in_=pM.t[0:16, 0:128], func=AF.Copy), [pM], [glrT])
            k.op("act", lambda e: e.activation(out=vbf.t[:, 0:512], in_=pV0.t[:], func=AF.Copy), [pV0], [vbf])
            k.op("act", lambda e: e.activation(out=vbf.t[:, 512:1024], in_=pV1.t[:], func=AF.Copy), [pV1], [vbf])
            yield
            if own:
                for c in range(8):
                    k.op("pe", lambda e, c=c: e.matmul(pK.t[:, :], lhsT=hT.t[:, c, :], rhs=Wq.t[:, c, :], start=(c == 0),
                                                       stop=(c == 7)), [hT, Wq], [pK])
                k.op("act", lambda e: e.activation(out=qsb.t[:], in_=pK.t[:], func=AF.Copy, scale=128.0 ** -0.5), [pK], [qsb])
                yield
                for gi in range(3):
                    if SEG - (to + 1) * 128 < WINS[gi]:
                        kv = kvst[gi % 2]
                        for c in range(8):
                            k.op("pe", lambda e, c=c, gi=gi: e.matmul(pM.t[:, 0:256], lhsT=hT.t[:, c, :],
                                                                      rhs=Wka.t[:, c, gi * 256:(gi + 1) * 256],
                                                                      start=(c == 0), stop=(c == 7)), [hT, Wka], [pM])
                        for c in range(8):
                            k.op("pe", lambda e, c=c, gi=gi: e.matmul(pM.t[:, 256:512], lhsT=hT.t[:, c, :],
                                                                      rhs=Wva.t[:, c, gi * 256:(gi + 1) * 256],
                                                                      start=(c == 0), stop=(c == 7)), [hT, Wva], [pM])
                        k.op("act", lambda e, kv=kv: e.activation(out=kv.t[:], in_=pM.t[:], func=AF.Copy), [pM], [kv])
                        r0 = (to + 1) * 128 - (SEG - WINS[gi]) - 128
                        k.dma("sp", O["kvp%d" % WINS[gi]][r0:r0 + 128, :], kv.t[:], [kv], [dram_out])
                        yield

        def back(ti):
            own = ti >= ntile_pre
            to = ti - ntile_pre
            glrT, vbf, ksb, qsb = glrTs[ti % 2], vbfs[ti % 2], ksbs[ti % 2], qsbs[ti % 2]
            k.op("pe", lambda e: e.matmul(pX.t[:, :], lhsT=glrT.t[:, :], rhs=wup.t[:, :], start=True, stop=False),
                 [glrT, wup], [pX])
            k.op("pe", lambda e: e.matmul(pX.t[:, :], lhsT=onesb.t[0:1, :], rhs=bgl.t[0:1, :], start=False, stop=True),
                 [onesb, bgl], [pX])
            k.op("act", lambda e: e.activation(out=ee.t[:], in_=pX.t[:], func=AF.Exp, scale=-1.0), [pX], [ee])
            k.op("act", lambda e: e.activation(out=ll.t[:], in_=ee.t[:], func=AF.Ln, bias=1.0), [ee], [ll])
            yield
            k.op("pe", lambda e: e.matmul(pX.t[:, :], lhsT=trisf.t[:, :], rhs=ll.t[:, :], start=True, stop=True),
                 [trisf, ll], [pX])
            for h in range(4):
                k.op("pe", lambda e, h=h: e.matmul(pD.t[:, h:h + 1], lhsT=ll.t[:, h * 128:(h + 1) * 128],
                                                   rhs=onesf.t[:, 0:1], start=True, stop=True), [ll, onesf], [pD])
            k.op("act", lambda e: e.activation(out=e3.t[:], in_=pX.t[:], func=AF.Exp, scale=-1.0 / 16), [pX], [e3])
            k.op("act", lambda e: e.activation(out=dec.t[:], in_=pD.t[:, 0:4], func=AF.Exp, scale=-1.0 / 16), [pD], [dec])
            yield
            k.op("dve", lambda e: e.tensor_tensor(out=khat.t[:], in0=ksb.t[:], in1=e3.t[:], op=ALU.mult), [ksb, e3], [khat])
            yield
            if own:
                k.op("pe", lambda e: e.matmul(pX.t[:, :], lhsT=trif.t[:, :], rhs=ll.t[:, :], start=True, stop=True),
                     [trif, ll], [pX])
                k.op("act", lambda e: e.activation(out=e1.t[:], in_=pX.t[:], func=AF.Exp, scale=-1.0 / 16), [pX], [e1])
                k.op("act", lambda e: e.activation(out=e3.t[:], in_=pX.t[:], func=AF.Exp, scale=1.0 / 16), [pX], [e3])
                k.op("dve", lambda e: e.tensor_tensor(out=kchk.t[:], in0=ksb.t[:], in1=e3.t[:], op=ALU.mult), [ksb, e3], [kchk])
                k.op("dve", lambda e: e.tensor_tensor(out=qtil.t[:], in0=qsb.t[:], in1=e1.t[:], op=ALU.mult), [qsb, e1], [qtil])
                yield
                for h in range(4):
                    k.op("pe", lambda e, h=h: e.transpose(out=pQK.t[:, h * 128:(h + 1) * 128],
                                                          in_=qtil.t[:, h * 128:(h + 1) * 128], identity=identb.t[:]),
                         [qtil, identb], [pQK])
                    k.op("pe", lambda e, h=h: e.transpose(out=pQK.t[:, (4 + h) * 128:(5 + h) * 128],
                                                          in_=kchk.t[:, h * 128:(h + 1) * 128], identity=identb.t[:]),
                         [kchk, identb], [pQK])
                k.op("act", lambda e: e.activation(out=qkT.t[:].rearrange("p c t -> p (c t)"), in_=pQK.t[:], func=AF.Copy),
                     [pQK], [qkT])
                yield
                for h in range(4):
                    k.op("pe", lambda e, h=h: e.matmul(pX.t[:, h * 128:(h + 1) * 128], lhsT=qkT.t[:, 4 + h, :],
                                                       rhs=qkT.t[:, h, :], start=True, stop=True), [qkT], [pX])
                k.op("dve", lambda e: e.tensor_tensor(
                    out=scT.t[:], in0=pX.t[:].rearrange("p (h t) -> p h t", h=4),
                    in1=trif.t[:].unsqueeze(1).to_broadcast([128, 4, 128]), op=ALU.mult), [pX, trif], [scT])
                yield
                for hp in range(2):
                    for h in (2 * hp, 2 * hp + 1):
                        cs_ = slice((h % 2) * 256, (h % 2) * 256 + 256)
                        k.op("pe", lambda e, h=h, cs_=cs_: e.matmul(pD.t[:, cs_], lhsT=scT.t[:, h, :],
                                                                    rhs=vbf.t[:, h * 256:(h + 1) * 256], start=True, stop=False),
                             [scT, vbf], [pD])
                        k.op("pe", lambda e, h=h, cs_=cs_: e.matmul(pD.t[:, cs_], lhsT=qkT.t[:, h, :], rhs=Sb.t[:, h, :],
                                                                    start=False, stop=True), [qkT, Sb], [pD])
                    k.op("act", lambda e, hp=hp: e.activation(out=og.t[:, 2 * hp:2 * hp + 2, :].rearrange("p h v -> p (h v)"),
                                                              in_=pD.t[:], func=AF.Copy), [pD], [og])
                    yield
                for h in range(4):
                    k.op("dve", lambda e, h=h: e.scalar_tensor_tensor(out=e1.t[:, 0:256], in0=og.t[:, h, :], scalar=1.0,
                                                                      in1=og.t[:, h, :], op0=ALU.mult, op1=ALU.mult,
                                                                      accum_out=ssq.t[:, h:h + 1]), [og], [e1, ssq])
                k.op("dve", lambda e: e.tensor_scalar(out=ssq.t[:, 0:4], in0=ssq.t[:, 0:4], scalar1=1.0 / 256, scalar2=LN_EPS,
                                                      op0=ALU.mult, op1=ALU.add), [ssq], [ssq])
                k.op("act", lambda e: e.activation(out=ssq.t[:, 0:4], in_=ssq.t[:, 0:4], func=AF.Sqrt), [ssq], [ssq])
                k.op("dve", lambda e: e.reciprocal(out=ssq.t[:, 4:8], in_=ssq.t[:, 0:4]), [ssq], [ssq])
                for h in range(4):
                    k.op("dve", lambda e, h=h: e.scalar_tensor_tensor(
                        out=obt.t[:, h * 256:(h + 1) * 256], in0=og.t[:, h, :], scalar=ssq.t[:, 4 + h:5 + h],
                        in1=gnw.t[:, h * 256:(h + 1) * 256], op0=ALU.mult, op1=ALU.mult), [og, ssq, gnw], [obt])
                k.dma("sp", OB.t[to * 128:(to + 1) * 128, :], obt.t[:], [obt], [OB])
                yield
            for hp in range(2):
                for h in (2 * hp, 2 * hp + 1):
                    cs_ = slice((h % 2) * 256, (h % 2) * 256 + 256)
                    k.op("pe", lambda e, h=h, cs_=cs_: e.matmul(pD.t[:, cs_], lhsT=khat.t[:, h * 128:(h + 1) * 128],
                                                                rhs=vbf.t[:, h * 256:(h + 1) * 256], start=True, stop=True),
                         [khat, vbf], [pD])
                for h in (2 * hp, 2 * hp + 1):
                    cs_ = slice((h % 2) * 256, (h % 2) * 256 + 256)
                    k.op("dve", lambda e, h=h, cs_=cs_: e.scalar_tensor_tensor(
                        out=S.t[:, h, :], in0=S.t[:, h, :], scalar=dec.t[:, h:h + 1], in1=pD.t[:, cs_], op0=ALU.mult,
                        op1=ALU.add), [S, dec, pD], [S])
                yield
            if (not own) and (ti + 1) % 32 == 0:
                j = (ti + 1) // 32 - 1
                k.op("dve", lambda e, j=j: e.tensor_scalar(out=S.t[:].rearrange("p h v -> p (h v)"),
                                                           in0=S.t[:].rearrange("p h v -> p (h v)"),
                                                           scalar1=keep.t[:, j:j + 1], scalar2=None, op0=ALU.mult), [S, keep], [S])
            if ti >= ntile_pre - 1:
                k.op("act", lambda e: e.activation(out=Sb.t[:].rearrange("p h v -> p (h v)"),
                                                   in_=S.t[:].rearrange("p h v -> p (h v)"), func=AF.Copy), [S], [Sb])

        def interleave(gens):
            gens = [g_ for g_ in gens if g_ is not None]
            while gens:
                for g_ in list(gens):
                    try:
                        next(g_)
                    except StopIteration:
                        gens.remove(g_)

        interleave([front1(0)])
        interleave([front1(1), front2(0)])
        for ti in range(ntile):
            interleave([front1(ti + 2) if ti + 2 < ntile else None, front2(ti + 1) if ti + 1 < ntile else None, back(ti)])
        k.dma("sp", O["glap"].rearrange("h k v -> k h v"), S.t[:], [S], [dram_out])
        k.barrier()
        st.close()


    def phaseA():
        st = ExitStack()
        A1 = k.sb(st, "A1", [128, D], F32)
        B1 = k.sb(st, "B1", [128, D], F32)
        k.dma("sp", A1.t[:], modP.t[:, 1 * D:2 * D], [modP], [A1])
        k.dma("sp", B1.t[:], modP.t[:, 0:D], [modP], [B1])
        lns = ln_scratch(st)
        xt = [k.sb(st, "xt%d" % i, [128, D], F32) for i in range(2)]
        hbs = [k.sb(st, "hb%d" % i, [128, D], BF16) for i in range(2)]
        hTs = k.sb(st, "hTs", [128, 8, 2048], BF16, n=16)
        HTS = [dscr("HTS%d" % i, [128, 8 * 2048], BF16) for i in range(3)]
        KT = [[k.sb(st, "KT%d%d" % (gi, pr), [128, 2048], BF16) for pr in range(2)] for gi in range(3)]
        VA = [[k.sb(st, "VA%d%d" % (gi, pr), [128, 16, 2, 65], BF16) for pr in range(2)] for gi in range(3)]
        QT = k.sb(st, "QT", [128, 2048], BF16)
        acc = k.sb(st, "acc", [128, 2, 2048], F32)
        oaN = k.sb(st, "oaN", [64, 2, 2048], BF16)
        bT = k.sb(st, "bT", [128, 12, 256], F32)
        bF = k.sb(st, "bF", [128, 12, 128], F32)
        negc = k.sb(st, "negc", [128, 1], F32)
        Tt = [k.sb(st, "Tt%d" % i, [128, 256], F32) for i in range(2)]
        PT = [k.sb(st, "PT%d" % i, [128, 2, 128], BF16) for i in range(2)]
        k.dma("sp", bT.t[:], I["biasT"], [], [bT])
        k.op("dve", lambda e: e.tensor_scalar(out=negc.t[:], in0=segc.t[:], scalar1=1.0, scalar2=-1.0, op0=ALU.min,
                                              op1=ALU.add), [segc], [negc])
        k.op("dve", lambda e: e.tensor_scalar(out=negc.t[:], in0=negc.t[:], scalar1=30000.0, scalar2=None, op0=ALU.mult),
             [negc], [negc])
        k.op("dve", lambda e: e.tensor_scalar(out=bF.t[:], in0=bT.t[:, :, 0:128], scalar1=negc.t[:, 0:1], scalar2=None,
                                              op0=ALU.add), [bT, negc], [bF])
        for gi in range(3):
            for pr in range(2):
                k.op("pool", lambda e, gi=gi, pr=pr: e.memset(VA[gi][pr].t[:], 1.0), [], [VA[gi][pr]])
        pP, pVp = pbank[0], pbank[1]
        pS = [pbank[2], pbank[3]]
        pO = [pbank[4], pbank[5]]
        cnt = 0
        for cp_ in range(2):
            Wq = k.sb(st, "Wqa%d" % cp_, [128, 8, 384], BF16, n=24)
            Wk = k.sb(st, "Wka%d" % cp_, [128, 8, 384], BF16, n=24)
            Wv = k.sb(st, "Wva%d" % cp_, [128, 8, 384], BF16, n=24)
            for gi in range(3):
                c0 = gi * 256 + cp_ * 128
                for c in range(8):
                    k.dma("pool", Wq.t[:, c, gi * 128:(gi + 1) * 128], w_in_v[:, c, C_QA + c0:C_QA + c0 + 128], [], [Wq.bs[gi * 8 + c]])
                    k.dma("pool", Wk.t[:, c, gi * 128:(gi + 1) * 128], w_in_v[:, c, C_KA + c0:C_KA + c0 + 128], [], [Wk.bs[gi * 8 + c]])
                    k.dma("pool", Wv.t[:, c, gi * 128:(gi + 1) * 128], w_in_v[:, c, C_VA + c0:C_VA + c0 + 128], [], [Wv.bs[gi * 8 + c]])
            for span in range(3):
                par = span % 2
                if cp_ == 1:
                    k.dma("sp", hTs.t[:].rearrange("p c t -> p (c t)"), HTS[span].t[:, :], [HTS[span]], hTs.bs)
                for tl in range(16 if cp_ == 0 else 0):
                    hb, pTb = hbs[tl % 2], pbb[tl % 2]
                    if span == 0:
                        src = I["xp"][NPRE - 2048 + tl * 128:NPRE - 2048 + (tl + 1) * 128, :]
                    else:
                        src = I["xo"][(span - 1) * 2048 + tl * 128:(span - 1) * 2048 + (tl + 1) * 128, :]
                    x = xt[tl % 2]
                    k.dma("sp", x.t[:], src, [], [x])
                    layer_norm(lns, x.t[:], [x], A1.t[:], B1.t[:], [A1, B1], hb.t[:], [hb], "a")
                    for c in range(8):
                        k.op("pe", lambda e, c=c: e.transpose(out=pTb.t[:, c * 128:(c + 1) * 128],
                                                              in_=hb.t[:, c * 128:(c + 1) * 128], identity=identb.t[:]),
                             [hb, identb], [pTb])
                    k.op("act", lambda e, tl=tl: e.activation(out=hTs.t[:, :, tl * 128:(tl + 1) * 128],
                                                              in_=pTb.t[:].rearrange("p (c t) -> p c t", c=8), func=AF.Copy),
                         [pTb], [hTs.bs[tl]])
                if cp_ == 0:
                    k.dma("sp", HTS[span].t[:, :], hTs.t[:].rearrange("p c t -> p (c t)"), hTs.bs, [HTS[span]])
                for gi in range(3):
                    dil = DILS[gi]
                    nb = 16 // dil
                    kt, va = KT[gi][par], VA[gi][par]
                    ktp, vap = KT[gi][1 - par], VA[gi][1 - par]
                    wc = slice(gi * 128, (gi + 1) * 128)
                    for tg in range(4):
                        ts_ = slice(tg * 512, (tg + 1) * 512)
                        for c in range(8):
                            k.op("pe", lambda e, c=c, ts_=ts_, wc=wc: e.matmul(pP.t[:, :], lhsT=Wk.t[:, c, wc], rhs=hTs.t[:, c, ts_],
                                                                                 start=(c == 0), stop=(c == 7)),
                                 [Wk] + hTs.bs[tg * 4:tg * 4 + 4], [pP])
                        k.op("act", lambda e, ts_=ts_, kt=kt: e.activation(out=kt.t[:, ts_], in_=pP.t[:], func=AF.Copy), [pP], [kt])
                        if span > 0:
                            for c in range(8):
                                k.op("pe", lambda e, c=c, ts_=ts_, wc=wc: e.matmul(pP.t[:, :], lhsT=Wq.t[:, c, wc],
                                                                                     rhs=hTs.t[:, c, ts_], start=(c == 0),
                                                                                     stop=(c == 7)),
                                     [Wq] + hTs.bs[tg * 4:tg * 4 + 4], [pP])
                            k.op("act", lambda e, ts_=ts_: e.activation(out=QT.t[:, ts_], in_=pP.t[:], func=AF.Copy, scale=0.125),
                                 [pP], [QT])

                    def cols(r, n, dil=dil):
                        st0 = n * 128 * dil + r
                        return slice(st0, st0 + 127 * dil + 1, dil)
                    for b4 in range(4):
                        for bi in range(4):
                            blk = b4 * 4 + bi
                            r, n = blk // nb, blk % nb
                            for c in range(8):
                                k.op("pe", lambda e, c=c, bi=bi, r=r, n=n, wc=wc: e.matmul(
                                    pVp.t[:, bi * 128:(bi + 1) * 128], lhsT=hTs.t[:, c, cols(r, n)], rhs=Wv.t[:, c, wc],
                                    start=(c == 0), stop=(c == 7)), [Wv] + hTs.bs, [pVp])
                        k.op("act", lambda e, b4=b4, va=va: e.activation(
                            out=va.t[:, b4 * 4:(b4 + 1) * 4, :, 0:64],
                            in_=pVp.t[:].rearrange("p (b h d) -> p b h d", b=4, h=2), func=AF.Copy), [pVp], [va])
                    if span == 0:
                        continue
                    for blk in range(16):
                        r, n = blk // nb, blk % nb
                        first = (span == 1 and n == 0)
                        for hh in range(2):
                            gh = gi * 4 + cp_ * 2 + hh
                            hp = slice(hh * 64, (hh + 1) * 64)
                            ps_, po_ = pS[cnt % 2], pO[cnt % 2]
                            tt, pt = Tt[cnt % 2], PT[cnt % 2]
                            cnt += 1
                            if n > 0:
                                kprev, vprev, bprev = kt, va, blk - 1
                                kpc = cols(r, n - 1)
                            else:
                                kprev, vprev, bprev = ktp, vap, r * nb + nb - 1
                                kpc = cols(r, nb - 1)
                            k.op("pe", lambda e, ps_=ps_, kprev=kprev, kpc=kpc, hp=hp, r=r, n=n: e.matmul(
                                ps_.t[:, 0:128], lhsT=kprev.t[hp, kpc], rhs=QT.t[hp, cols(r, n)], start=True, stop=True),
                                 [kprev, QT], [ps_])
                            k.op("pe", lambda e, ps_=ps_, kt=kt, hp=hp, r=r, n=n: e.matmul(
                                ps_.t[:, 128:256], lhsT=kt.t[hp, cols(r, n)], rhs=QT.t[hp, cols(r, n)], start=True, stop=True),
                                 [kt, QT], [ps_])
                            if first:
                                k.op("dve", lambda e, tt=tt, ps_=ps_, gh=gh: e.tensor_tensor(
                                    out=tt.t[:, 0:128], in0=ps_.t[:, 0:128], in1=bF.t[:, gh, :], op=ALU.add), [ps_, bF], [tt])
                                k.op("dve", lambda e, tt=tt, ps_=ps_, gh=gh: e.tensor_tensor(
                                    out=tt.t[:, 128:256], in0=ps_.t[:, 128:256], in1=bT.t[:, gh, 128:256], op=ALU.add),
                                     [ps_, bT], [tt])
                            else:
                                k.op("dve", lambda e, tt=tt, ps_=ps_, gh=gh: e.tensor_tensor(
                                    out=tt.t[:], in0=ps_.t[:, 0:256], in1=bT.t[:, gh, :], op=ALU.add), [ps_, bT], [tt])
                            k.op("act", lambda e, tt=tt, pt=pt: e.activation(out=pt.t[:].rearrange("p a q -> p (a q)"),
                                                                             in_=tt.t[:], func=AF.Exp), [tt], [pt])
                            k.op("pe", lambda e, po_=po_, vprev=vprev, bprev=bprev, hh=hh, pt=pt: e.matmul(
                                po_.t[0:65, 0:128], lhsT=vprev.t[:, bprev, hh, :], rhs=pt.t[:, 0, :], start=True, stop=False),
                                 [vprev, pt], [po_])
                            k.op("pe", lambda e, po_=po_, va=va, blk=blk, hh=hh, pt=pt: e.matmul(
                                po_.t[0:65, 0:128], lhsT=va.t[:, blk, hh, :], rhs=pt.t[:, 1, :], start=False, stop=True),
                                 [va, pt], [po_])
                            if gi == 0:
                                k.op("dve", lambda e, po_=po_, hh=hh, r=r, n=n: e.tensor_copy(
                                    out=acc.t[0:65, hh, cols(r, n)], in_=po_.t[0:65, 0:128]), [po_], [acc])
                            else:
                                k.op("dve", lambda e, po_=po_, hh=hh, r=r, n=n: e.tensor_tensor(
                                    out=acc.t[0:65, hh, cols(r, n)], in0=acc.t[0:65, hh, cols(r, n)], in1=po_.t[0:65, 0:128],
                                    op=ALU.add), [po_, acc], [acc])
                if span == 0:
                    continue
                k.op("dve", lambda e: e.reciprocal(out=acc.t[64:65, :, :], in_=acc.t[64:65, :, :]), [acc], [acc])
                for hh in range(2):
                    for tg in range(4):
                        ts_ = slice(tg * 512, (tg + 1) * 512)
                        k.op("pe", lambda e, hh=hh, ts_=ts_: e.matmul(pP.t[0:64, :], lhsT=onesf.t[64:65, 0:64],
                                                                      rhs=acc.t[64:65, hh, ts_], start=True, stop=True),
                             [onesf, acc], [pP])
                        k.op("dve", lambda e, hh=hh, ts_=ts_: e.tensor_tensor(out=oaN.t[0:64, hh, ts_], in0=acc.t[0:64, hh, ts_],
                                                                              in1=pP.t[0:64, :], op=ALU.mult), [acc, pP], [oaN])
                    t0 = (span - 1) * 2048
                    k.dma("sp", OAT.t[cp_ * 2 + hh, :, t0:t0 + 2048], oaN.t[0:64, hh, :], [oaN], [OAT])
        k.barrier()
        st.close()

    def phaseS():
        st = ExitStack()
        A1 = k.sb(st, "A1", [128, D], F32)
        B1 = k.sb(st, "B1", [128, D], F32)
        k.dma("sp", A1.t[:], modS.t[:, 1 * D:2 * D], [modS], [A1])
        k.dma("sp", B1.t[:], modS.t[:, 0:D], [modS], [B1])
        lns = ln_scratch(st)
        x = k.sb(st, "xs", [128, D], F32)
        hb = k.sb(st, "hb", [128, D], BF16)
        hT = k.sb(st, "hT", [128, 8, 128], BF16)
        pj = k.sb(st, "pj", [128, 4352], F32)
        wb = [k.sb(st, "wb%d" % i, [128, 8, 512], BF16, n=8) for i in range(2)]
        Wg = load_w(st, "Wg", w_in_v, C_GLR, 16)
        wup = k.sb(st, "wup", [16, 512], BF16)
        bgl = k.sb(st, "bgl", [1, 512], BF16)
        gnw = k.sb(st, "gnw", [128, D], F32)
        bS = k.sb(st, "bS", [16, 3, 128, 4], F32)
        k.dma("pool", wup.t[:], I["w_up"], [], [wup])
        k.dma("pool", bgl.t[:], I["b_gla"], [], [bgl])
        k.dma("sp", gnw.t[:], bcast_rows(I["gnw"], 128), [], [gnw])
        k.dma("sp", bS.t[:].rearrange("p g k h -> p (g k h)"), bcast_rows(I["biasS"], 16), [], [bS])
        k.dma("sp", x.t[:], I["xs"], [], [x])
        layer_norm(lns, x.t[:], [x], A1.t[:], B1.t[:], [A1, B1], hb.t[:], [hb], "s")
        transpose8(hb, hT, pbb[0])
        for blk in range(9):
            w = wb[blk % 2]
            n0 = blk * 512
            n1 = min(4352, n0 + 512)
            for c in range(8):
                k.dma("pool", w.t[:, c, 0:n1 - n0], w_in_v[:, c, n0:n1], [], [w.bs[c]])
            pp = pbank[blk % 2]
            for c in range(8):
                k.op("pe", lambda e, c=c, w=w, pp=pp, nn=n1 - n0: e.matmul(pp.t[:, 0:nn], lhsT=hT.t[:, c, :], rhs=w.t[:, c, 0:nn],
                                                                          start=(c == 0), stop=(c == 7)), [hT, w], [pp])
            k.op("act", lambda e, pp=pp, n0=n0, n1=n1: e.activation(out=pj.t[:, n0:n1], in_=pp.t[:, 0:n1 - n0], func=AF.Copy),
                 [pp], [pj])
        newkv = k.sb(st, "newkv", [128, 3, 512], F32)
        for gi in range(3):
            k.op("dve", lambda e, gi=gi: e.tensor_copy(out=newkv.t[:, gi, 0:256], in_=pj.t[:, C_KA + gi * 256:C_KA + (gi + 1) * 256]),
                 [pj], [newkv])
            k.op("dve", lambda e, gi=gi: e.tensor_copy(out=newkv.t[:, gi, 256:512], in_=pj.t[:, C_VA + gi * 256:C_VA + (gi + 1) * 256]),
                 [pj], [newkv])
            W = WINS[gi]
            cin, cout = I["c%d" % W], O["kvs%d" % W]
            k.dma("sp", cout[:, W - 1, :], newkv.t[0:16, gi, :], [newkv], [dram_out])
            for b in range(16):
                k.dma("sp", cout[b, 0:W - 1, :], cin[b, 1:W, :], [], [dram_out])
        oacc = k.sb(st, "oacc", [16, 4, 64], F32)
        zacc = k.sb(st, "zacc", [16, 4], F32)
        pr1 = k.sb(st, "pr1", [16, 12, 64], F32)
        ss = k.sb(st, "ss", [16, 12], F32)
        psf = k.sb(st, "psf", [16, 12], F32)
        qa_v = pj.t[0:16, C_QA:C_QA + 768].rearrange("p (g d) -> p g d", g=12)
        ka_v = pj.t[0:16, C_KA:C_KA + 768].rearrange("p (g d) -> p g d", g=12)
        va_v = pj.t[0:16, C_VA:C_VA + 768].rearrange("p (g d) -> p g d", g=12)
        k.op("dve", lambda e: e.tensor_tensor(out=pr1.t[:], in0=qa_v, in1=ka_v, op=ALU.mult), [pj], [pr1])
        k.op("dve", lambda e: e.tensor_reduce(out=ss.t[:], in_=pr1.t[:], axis=AX.X, op=ALU.add), [pr1], [ss])
        k.op("act", lambda e: e.activation(out=psf.t[:], in_=ss.t[:], func=AF.Exp, scale=0.125), [ss], [psf])
        k.op("dve", lambda e: e.tensor_tensor(out=pr1.t[:], in0=va_v, in1=psf.t[:].unsqueeze(2).to_broadcast([16, 12, 64]),
                                              op=ALU.mult), [pj, psf], [pr1])
        k.op("dve", lambda e: e.tensor_tensor(out=oacc.t[:], in0=pr1.t[:, 0:4, :], in1=pr1.t[:, 4:8, :], op=ALU.add), [pr1], [oacc])
        k.op("dve", lambda e: e.tensor_tensor(out=oacc.t[:], in0=oacc.t[:], in1=pr1.t[:, 8:12, :], op=ALU.add), [pr1, oacc], [oacc])
        k.op("dve", lambda e: e.tensor_tensor(out=zacc.t[:], in0=psf.t[:, 0:4], in1=psf.t[:, 4:8], op=ALU.add), [psf], [zacc])
        k.op("dve", lambda e: e.tensor_tensor(out=zacc.t[:], in0=zacc.t[:], in1=psf.t[:, 8:12], op=ALU.add), [psf, zacc], [zacc])
        Kt = [k.sb(st, "Kt%d" % i, [16, 16, 512], F32) for i in range(2)]
        prd = k.sb(st, "prd", [16, 16, 256], F32)
        scs = k.sb(st, "scs", [16, 64], F32)
        pex = k.sb(st, "pex", [16, 64], F32)
        red = k.sb(st, "red", [16, 256], F32)
        red4 = k.sb(st, "red4", [16, 4], F32)
        ci_ = 0
        for gi in range(3):
            W, dil = WINS[gi], DILS[gi]
            cin = I["c%d" % W]
            for kc in range(8):
                kt = Kt[ci_ % 2]
                ci_ += 1
                r0 = kc * 16 * dil
                k.dma("sp", kt.t[:], cin[:, r0:r0 + 15 * dil + 1:dil, :], [], [kt])
                qg = pj.t[0:16, C_QA + gi * 256:C_QA + (gi + 1) * 256]
                k.op("dve", lambda e, kt=kt, qg=qg: e.tensor_tensor(out=prd.t[:], in0=kt.t[:, :, 0:256],
                                                                    in1=qg.unsqueeze(1).to_broadcast([16, 16, 256]), op=ALU.mult),
                     [kt, pj], [prd])
                k.op("dve", lambda e: e.tensor_reduce(out=scs.t[:], in_=prd.t[:].rearrange("p k (h d) -> p (k h) d", h=4),
                                                      axis=AX.X, op=ALU.add), [prd], [scs])
                k.op("dve", lambda e, gi=gi, kc=kc: e.scalar_tensor_tensor(
                    out=scs.t[:], in0=scs.t[:], scalar=0.125, in1=bS.t[:, gi, kc * 16:(kc + 1) * 16, :].rearrange("p k h -> p (k h)"),
                    op0=ALU.mult, op1=ALU.add), [scs, bS], [scs])
                k.op("act", lambda e: e.activation(out=pex.t[:], in_=scs.t[:], func=AF.Exp), [scs], [pex])
                k.op("dve", lambda e: e.tensor_reduce(out=red4.t[:], in_=pex.t[:].rearrange("p (k h) -> p h k", h=4), axis=AX.X,
                                                      op=ALU.add), [pex], [red4])
                k.op("dve", lambda e: e.tensor_tensor(out=zacc.t[:], in0=zacc.t[:], in1=red4.t[:], op=ALU.add), [zacc, red4], [zacc])
                k.op("dve", lambda e, kt=kt: e.tensor_tensor(
                    out=prd.t[:].rearrange("p k (h d) -> p k h d", h=4),
                    in0=kt.t[:, :, 256:512].rearrange("p k (h d) -> p k h d", h=4),
                    in1=pex.t[:].rearrange("p (k h) -> p k h", h=4).unsqueeze(3).to_broadcast([16, 16, 4, 64]), op=ALU.mult),
                     [kt, pex], [prd])
                k.op("dve", lambda e: e.tensor_reduce(out=red.t[:], in_=prd.t[:].rearrange("p k n -> p n k"), axis=AX.X, op=ALU.add),
                     [prd], [red])
                k.op("dve", lambda e: e.tensor_tensor(out=oacc.t[:].rearrange("p h d -> p (h d)"),
                                                      in0=oacc.t[:].rearrange("p h d -> p (h d)"), in1=red.t[:], op=ALU.add),
                     [oacc, red], [oacc])
        oas = k.sb(st, "oas", [128, 4, 64], BF16)
        oasT = k.sb(st, "oasT", [64, 4, 128], BF16)
        k.op("pool", lambda e: e.memset(oas.t[:], 0.0), [], [oas])
        k.op("dve", lambda e: e.reciprocal(out=zacc.t[:], in_=zacc.t[:]), [zacc], [zacc])
        k.op("dve", lambda e: e.tensor_tensor(out=oas.t[0:16, :, :], in0=oacc.t[:], in1=zacc.t[:].unsqueeze(2).to_broadcast([16, 4, 64]),
                                              op=ALU.mult), [oacc, zacc], [oas])
        pq_ = pbb[1]
        for j in range(4):
            k.op("pe", lambda e, j=j: e.transpose(out=pq_.t[0:64, j * 128:(j + 1) * 128], in_=oas.t[:, j, :], identity=identb.t[:]),
                 [oas, identb], [pq_])
        k.op("act", lambda e: e.activation(out=oasT.t[:].rearrange("p j t -> p (j t)"), in_=pq_.t[0:64, 0:512], func=AF.Copy),
             [pq_], [oasT])
        for j in range(4):
            k.dma("sp", OAT.t[j, :, SEG:SEG + 128], oasT.t[:, j, :], [oasT], [OAT])
        glrT = k.sb(st, "glrT", [16, 128], BF16)
        ee = k.sb(st, "ee", [128, 512], F32)
        pM, pX = pbank[2], pbank[3]
        for c in range(8):
            k.op("pe", lambda e, c=c: e.matmul(pM.t[0:16, 0:128], lhsT=Wg.t[:, c, :], rhs=hT.t[:, c, :], start=(c == 0), stop=(c == 7)),
                 [hT, Wg], [pM])
        k.op("act", lambda e: e.activation(out=glrT.t[:], in_=pM.t[0:16, 0:128], func=AF.Copy), [pM], [glrT])
        k.op("pe", lambda e: e.matmul(pX.t[:, :], lhsT=glrT.t[:, :], rhs=wup.t[:, :], start=True, stop=False), [glrT, wup], [pX])
        k.op("pe", lambda e: e.matmul(pX.t[:, :], lhsT=onesb.t[0:1, :], rhs=bgl.t[0:1, :], start=False, stop=True), [onesb, bgl], [pX])
        k.op("act", lambda e: e.activation(out=ee.t[:], in_=pX.t[:], func=AF.Exp, scale=-1.0), [pX], [ee])
        k.op("act", lambda e: e.activation(out=ee.t[:], in_=ee.t[:], func=AF.Ln, bias=1.0), [ee], [ee])
        k.op("act", lambda e: e.activation(out=ee.t[:], in_=ee.t[:], func=AF.Exp, scale=-1.0 / 16), [ee], [ee])
        qkdT = k.sb(st, "qkdT", [128, 12, 128], F32)
        qsc = k.sb(st, "qsc", [128, 512], F32)
        k.op("dve", lambda e: e.tensor_scalar(out=qsc.t[:], in0=pj.t[:, C_QB:C_QB + 512], scalar1=128.0 ** -0.5, scalar2=None,
                                              op0=ALU.mult), [pj], [qsc])
        for h in range(4):
            for which, srcap, srcb in ((0, qsc.t[:, h * 128:(h + 1) * 128], qsc),
                                       (1, pj.t[:, C_KB + h * 128:C_KB + (h + 1) * 128], pj),
                                       (2, ee.t[:, h * 128:(h + 1) * 128], ee)):
                pt_ = pbank[4 + (h * 3 + which) % 2]
                k.op("pe", lambda e, pt_=pt_, srcap=srcap: e.transpose(out=pt_.t[:, 0:128], in_=srcap, identity=identf.t[:]),
                     [srcb, identf], [pt_])
                k.op("dve", lambda e, pt_=pt_, which=which, h=h: e.tensor_copy(out=qkdT.t[:, which * 4 + h, :], in_=pt_.t[:, 0:128]),
                     [pt_], [qkdT])
        VS = dscr("VS", [128, D], F32)
        k.dma("sp", VS.t[:, :], pj.t[:, C_VB:C_VB + D], [pj], [VS])
        ogs = k.sb(st, "ogs", [128, 4, 256], F32)
        k.op("pool", lambda e: e.memset(ogs.t[:], 0.0), [], [ogs])
        S0 = [k.sb(st, "S0%d" % i, [128, 4, 256], F32) for i in range(2)]
        vbc = [k.sb(st, "vbc%d" % i, [128, 4, 256], F32) for i in range(2)]
        for b in range(16):
            s0, vb_ = S0[b % 2], vbc[b % 2]
            k.dma("sp", s0.t[:], I["sgla"][b].rearrange("h k v -> k h v"), [], [s0])
            k.dma("sp", vb_.t[:].rearrange("p h v -> p (h v)"), bcast_rows(VS.t[b:b + 1, :], 128), [VS], [vb_])
            for h in range(4):
                k.op("dve", lambda e, s0=s0, h=h, b=b: e.tensor_scalar(out=s0.t[:, h, :], in0=s0.t[:, h, :],
                                                                       scalar1=qkdT.t[:, 8 + h, b:b + 1], scalar2=None, op0=ALU.mult),
                     [s0, qkdT], [s0])
                k.op("dve", lambda e, s0=s0, vb_=vb_, h=h, b=b: e.scalar_tensor_tensor(
                    out=s0.t[:, h, :], in0=vb_.t[:, h, :], scalar=qkdT.t[:, 4 + h, b:b + 1], in1=s0.t[:, h, :], op0=ALU.mult,
                    op1=ALU.add), [s0, vb_, qkdT], [s0])
            k.dma("sp", O["glas"][b].rearrange("h k v -> k h v"), s0.t[:], [s0], [dram_out])
            for h in range(4):
                po = pbank[h // 2]
                k.op("pe", lambda e, po=po, h=h, s0=s0: e.matmul(po.t[:, (h % 2) * 256:(h % 2) * 256 + 256], lhsT=qkdT.t[:, h, :],
                                                                 rhs=s0.t[:, h, :], start=True, stop=True), [qkdT, s0], [po])
            for hf in range(2):
                k.op("dve", lambda e, hf=hf, b=b: e.scalar_tensor_tensor(
                    out=ogs.t[:, hf * 2:hf * 2 + 2, :].rearrange("p h v -> p (h v)"), in0=pbank[hf].t[:, :],
                    scalar=identf.t[:, b:b + 1], in1=ogs.t[:, hf * 2:hf * 2 + 2, :].rearrange("p h v -> p (h v)"),
                    op0=ALU.mult, op1=ALU.add), [pbank[hf], identf, ogs], [ogs])
        ssq = k.sb(st, "ssq", [128, 8], F32)
        obt = k.sb(st, "obt", [128, D], BF16)
        for h in range(4):
            k.op("dve", lambda e, h=h: e.scalar_tensor_tensor(out=ee.t[:, 0:256], in0=ogs.t[:, h, :], scalar=1.0, in1=ogs.t[:, h, :],
                                                              op0=ALU.mult, op1=ALU.mult, accum_out=ssq.t[:, h:h + 1]), [ogs], [ee, ssq])
        k.op("dve", lambda e: e.tensor_scalar(out=ssq.t[:, 0:4], in0=ssq.t[:, 0:4], scalar1=1.0 / 256, scalar2=LN_EPS, op0=ALU.mult,
                                              op1=ALU.add), [ssq], [ssq])
        k.op("act", lambda e: e.activation(out=ssq.t[:, 0:4], in_=ssq.t[:, 0:4], func=AF.Sqrt), [ssq], [ssq])
        k.op("dve", lambda e: e.reciprocal(out=ssq.t[:, 4:8], in_=ssq.t[:, 0:4]), [ssq], [ssq])
        for h in range(4):
            k.op("dve", lambda e, h=h: e.scalar_tensor_tensor(out=obt.t[:, h * 256:(h + 1) * 256], in0=ogs.t[:, h, :],
                                                              scalar=ssq.t[:, 4 + h:5 + h], in1=gnw.t[:, h * 256:(h + 1) * 256],
                                                              op0=ALU.mult, op1=ALU.mult), [ogs, ssq, gnw], [obt])
        k.dma("sp", OB.t[SEG:SEG + 128, :], obt.t[:], [obt], [OB])
        k.barrier()
        st.close()

    def phaseB():
        st = ExitStack()
        A1 = k.sb(st, "A1", [128, D], F32)
        B1 = k.sb(st, "B1", [128, D], F32)
        G1 = k.sb(st, "G1", [128, D], F32)
        l1w = k.sb(st, "l1w", [128, D], F32)
        l1b = k.sb(st, "l1b", [128, D], F32)
        k.dma("sp", l1w.t[:], bcast_rows(I["ln1_w"], 128), [], [l1w])
        k.dma("sp", l1b.t[:], bcast_rows(I["ln1_b"], 128), [], [l1b])
        Wrb = load_w(st, "Wrb", w_in_v, C_RB, 1024)
        Wga = load_w(st, "Wga", w_in_v, C_GA, 1024)
        Wgb = load_w(st, "Wgb", w_in_v, C_GB, 1024)
        Wbb = load_w(st, "Wbb", I["w_br_b"].rearrange("(c p) n -> p c n", p=128), 0, 1024)
        Wo = load_w(st, "Wo", I["w_out"].rearrange("(c p) n -> p c n", p=128), 0, 1024)
        Wba = k.sb(st, "Wba", [64, 4, D], BF16, n=8)
        for j in range(4):
            for hf in range(2):
                k.dma("pool", Wba.t[:, j, hf * 512:(hf + 1) * 512], I["w_br_a"][j * 64:(j + 1) * 64, hf * 512:(hf + 1) * 512], [], [Wba.bs[j * 2 + hf]])
        lns = ln_scratch(st)
        xt = [k.sb(st, "xt%d" % i, [128, D], F32) for i in range(3)]
        hbs = [k.sb(st, "hb%d" % i, [128, D], BF16) for i in range(2)]
        hTs_ = [k.sb(st, "hT%d" % i, [128, 8, 128], BF16) for i in range(2)]
        gats = [[k.sb(st, "gat%d_%d" % (pr, i), [128, D], BF16) for i in range(3)] for pr in range(2)]
        obls = [k.sb(st, "obl%d" % i, [128, D], BF16) for i in range(2)]
        oaTs = [k.sb(st, "oaT%d" % i, [64, 4, 128], BF16) for i in range(2)]
        obT = k.sb(st, "obT", [128, 8, 128], BF16)
        t1 = k.sb(st, "t1", [128, D], F32)
        t2 = k.sb(st, "t2", [128, D], F32)
        mg = k.sb(st, "mg", [128, D], BF16)
        x1o = k.sb(st, "x1o", [128, D], F32)

        def f1(ti):
            smp = ti == NT - 1
            if ti == 0 or smp:
                md = modS if smp else modP
                k.dma("sp", A1.t[:], md.t[:, 1 * D:2 * D], [md], [A1])
                k.dma("sp", B1.t[:], md.t[:, 0:D], [md], [B1])
            x, hb = xt[ti % 3], hbs[ti % 2]
            k.dma("sp", x.t[:], I["xs"] if smp else I["xo"][ti * 128:(ti + 1) * 128, :], [], [x])
            yield from layer_norm_g(lns, x.t[:], [x], A1.t[:], B1.t[:], [A1, B1], hb.t[:], [hb], "b")
            yield

        def f2(ti):
            hb, hT, gat, obl, oaT = hbs[ti % 2], hTs_[ti % 2], gats[ti % 2], obls[ti % 2], oaTs[ti % 2]
            k.dma("sp", obl.t[:], OB.t[ti * 128:(ti + 1) * 128, :], [OB], [obl])
            k.dma("sp", oaT.t[:], OAT.t[:, :, ti * 128:(ti + 1) * 128].rearrange("j d t -> d j t"), [OAT], [oaT])
            transpose8(hb, hT, pbb[0])
            yield
            for wi, (W_, fn) in enumerate(((Wrb, AF.Silu), (Wga, AF.Sigmoid), (Wgb, AF.Sigmoid))):
                for hf in range(2):
                    pp = pbank[hf]
                    for c in range(8):
                        k.op("pe", lambda e, c=c, W_=W_, hf=hf, pp=pp: e.matmul(pp.t[:, :], lhsT=hT.t[:, c, :],
                                                                              rhs=W_.t[:, c, hf * 512:(hf + 1) * 512],
                                                                              start=(c == 0), stop=(c == 7)), [hT, W_], [pp])
                    k.op("act", lambda e, wi=wi, hf=hf, pp=pp, fn=fn: e.activation(out=gat[wi].t[:, hf * 512:(hf + 1) * 512],
                                                                                 in_=pp.t[:], func=fn), [pp], [gat[wi]])
                    yield

        def back(ti):
            smp = ti == NT - 1
            if ti == 0 or smp:
                md = modS if smp else modP
                k.dma("sp", G1.t[:], md.t[:, 2 * D:3 * D], [md], [G1])
            x, gat, obl, oaT = xt[ti % 3], gats[ti % 2], obls[ti % 2], oaTs[ti % 2]
            k.op("dve", lambda e: e.tensor_tensor(out=obl.t[:], in0=obl.t[:], in1=gat[0].t[:], op=ALU.mult), [obl, gat[0]], [obl])
            transpose8(obl, obT, pbb[1])
            yield
            for hf in range(2):
                pa, pb_ = pbank[2 + hf], pbank[4 + hf]
                for j in range(4):
                    k.op("pe", lambda e, j=j, hf=hf, pa=pa: e.matmul(pa.t[:, :], lhsT=oaT.t[:, j, :], rhs=Wba.t[:, j, hf * 512:(hf + 1) * 512],
                                                                     start=(j == 0), stop=(j == 3)), [oaT, Wba], [pa])
                for c in range(8):
                    k.op("pe", lambda e, c=c, hf=hf, pb_=pb_: e.matmul(pb_.t[:, :], lhsT=obT.t[:, c, :], rhs=Wbb.t[:, c, hf * 512:(hf + 1) * 512],
                                                                       start=(c == 0), stop=(c == 7)), [obT, Wbb], [pb_])
                hs = slice(hf * 512, (hf + 1) * 512)
                k.op("dve", lambda e, pa=pa, hs=hs: e.tensor_tensor(out=t1.t[:, hs], in0=pa.t[:], in1=gat[1].t[:, hs], op=ALU.mult),
                     [pa, gat[1]], [t1])
                k.op("dve", lambda e, pb_=pb_, hs=hs: e.tensor_tensor(out=t2.t[:, hs], in0=pb_.t[:], in1=gat[2].t[:, hs], op=ALU.mult),
                     [pb_, gat[2]], [t2])
                yield
            k.op("pool", lambda e: e.tensor_tensor(out=mg.t[:], in0=t1.t[:], in1=t2.t[:], op=ALU.add), [t1, t2], [mg])
            transpose8(mg, obT, pbb[1])
            yield
            for hf in range(2):
                pp = pbank[2 + hf]
                hs = slice(hf * 512, (hf + 1) * 512)
                for c in range(8):
                    k.op("pe", lambda e, c=c, hf=hf, pp=pp: e.matmul(pp.t[:, :], lhsT=obT.t[:, c, :], rhs=Wo.t[:, c, hf * 512:(hf + 1) * 512],
                                                                     start=(c == 0), stop=(c == 7)), [obT, Wo], [pp])
                k.op("dve", lambda e, pp=pp, hs=hs: e.tensor_tensor(out=t1.t[:, hs], in0=pp.t[:], in1=G1.t[:, hs], op=ALU.mult),
                     [pp, G1], [t1])
                yield
            k.op("dve", lambda e, x=x: e.scalar_tensor_tensor(out=t2.t[:], in0=x.t[:], scalar=DN_ALPHA, in1=t1.t[:], op0=ALU.mult,
                                                              op1=ALU.add), [x, t1], [t2])
            yield from layer_norm_g(lns, t2.t[:], [t2], l1w.t[:], l1b.t[:], [l1w, l1b], x1o.t[:], [x1o], "b1")
            k.dma("sp", X1.t[ti * 128:(ti + 1) * 128, :], x1o.t[:], [x1o], [X1])
            yield

        def interleave(gens):
            gens = [g_ for g_ in gens if g_ is not None]
            while gens:
                for g_ in list(gens):
                    try:
                        next(g_)
                    except StopIteration:
                        gens.remove(g_)

        interleave([f1(0)])
        interleave([f1(1), f2(0)])
        for ti in range(NT):
            interleave([f1(ti + 2) if ti + 2 < NT else None, f2(ti + 1) if ti + 1 < NT else None, back(ti)])
        k.barrier()
        st.close()

    def phaseC():
        st = ExitStack()
        A2 = k.sb(st, "A2", [128, D], F32)
        B2 = k.sb(st, "B2", [128, D], F32)
        G2 = k.sb(st, "G2", [128, D], F32)
        l2w = k.sb(st, "l2w", [128, D], F32)
        l2b = k.sb(st, "l2b", [128, D], F32)
        k.dma("sp", l2w.t[:], bcast_rows(I["ln2_w"], 128), [], [l2w])
        k.dma("sp", l2b.t[:], bcast_rows(I["ln2_b"], 128), [], [l2b])
        Wpq = load_w(st, "Wpq", I["w_pq"].rearrange("(c p) n -> p c n", p=128), 0, 2048)
        st_kk = ExitStack()
        kkT = k.sb(st, "kkT", [128, 16, 128], BF16)
        kk = k.sb(st_kk, "kk", [128, 16, 128], BF16)
        k.dma("pool", kk.t[:, 0:8, :], I["pk1"].rearrange("h k d -> k h d"), [], [kk])
        k.dma("pool", kk.t[:, 8:16, :], I["pk2"].rearrange("h k d -> k h d"), [], [kk])
        for which in range(2):
            for h in range(8):
                pq_ = pbb[(which * 8 + h) % 2]
                k.op("pe", lambda e, pq_=pq_, which=which, h=h: e.transpose(out=pq_.t[:, 0:128], in_=kk.t[:, which * 8 + h, :],
                                                                            identity=identb.t[:]), [kk, identb], [pq_])
                k.op("dve", lambda e, pq_=pq_, which=which, h=h: e.tensor_copy(out=kkT.t[:, 2 * h + which, :], in_=pq_.t[:, 0:128]),
                     [pq_], [kkT])
        k.barrier()
        st_kk.close()
        io16 = k.sb(st, "io16", [128, 16], F32)
        thr16 = k.sb(st, "thr16", [128, 16], F32)
        k.dma("sp", io16.t[:], I["iota16"], [], [io16])
        k.op("dve", lambda e: e.tensor_scalar(out=thr16.t[:], in0=io16.t[:], scalar1=16.0, scalar2=None, op0=ALU.mult), [io16], [thr16])
        lns = ln_scratch(st)
        x1 = [k.sb(st, "x1_%d" % i, [128, D], F32) for i in range(2)]
        h2s = [k.sb(st, "h2_%d" % i, [128, D], F32) for i in range(2)]
        eis = [k.sb(st, "ei%d" % i, [128, 128], I32) for i in range(2)]
        gws = [k.sb(st, "gw%d" % i, [128, 8, 16], F32) for i in range(2)]
        h2b = k.sb(st, "h2b", [128, D], BF16)
        h2T = k.sb(st, "h2T", [128, 8, 128], BF16)
        qvT = k.sb(st, "qvT", [128, 16, 128], BF16)
        sA = k.sb(st, "sA", [128, 2048], F32)
        sB = k.sb(st, "sB", [128, 2048], F32)
        v16 = k.sb(st, "v16", [128, 16, 16], F32)
        i16u = k.sb(st, "i16u", [128, 16, 16], U32)
        i16f = k.sb(st, "i16f", [128, 16, 16], F32)
        di1 = k.sb(st, "di1", [128, 8, 16], F32)
        sc16 = k.sb(st, "sc16", [128, 8, 16], F32)
        ciu = k.sb(st, "ciu", [128, 8, 16], U32)
        cif = k.sb(st, "cif", [128, 128], F32)
        a1 = k.sb(st, "a1", [128, 128], F32)
        i1s = k.sb(st, "i1s", [128, 128], F32)
        bsel = k.sb(st, "bsel", [128, 128], F32)
        i2s = k.sb(st, "i2s", [128, 128], F32)
        zs = k.sb(st, "zs", [128, 8], F32)
        dots = k.sb(st, "dots", [128, 128], F32)
        actv = k.sb(st, "actv", [128, 128], F32)
        gco = k.sb(st, "gco", [128, 128], F32)
        GS = 4
        NG = 14
        gb_ = [k.sb(st, "gb%d" % i, [128, 2, D], BF16) for i in range(NG)]
        dg = [k.sb(st, "dg%d" % i, [128, 128], BF16) for i in range(8)]
        prodb = [k.sb(st, "prod%d" % i, [128, D], F32) for i in range(2)]
        junk = k.sb(st, "junkc", [128, D], BF16)
        ff = k.sb(st, "ff", [128, D], F32)
        yt = k.sb(st, "yt", [128, D], F32)
        pF = [pbank[4], pbank[5]]

        def top16(vals, vals2, vout, iout, n, w):
            vv = vals.t[:].rearrange("p (c k) -> p c k", k=w)
            v2 = vals2.t[:].rearrange("p (c k) -> p c k", k=w)
            for c_ in range(n):
                k.op("dve", lambda e, c_=c_: e.max(out=vout.t[:, c_, 0:8], in_=vv[:, c_, :]), [vals], [vout])
                k.op("dve", lambda e, c_=c_: e.max_index(out=iout.t[:, c_, 0:8], in_max=vout.t[:, c_, 0:8], in_values=vv[:, c_, :]),
                     [vals, vout], [iout])
                k.op("dve", lambda e, c_=c_: e.match_replace(out=v2[:, c_, :], in_to_replace=vout.t[:, c_, 0:8],
                                                             in_values=vv[:, c_, :], imm_value=-1e30), [vals, vout], [vals2])
                k.op("dve", lambda e, c_=c_: e.max(out=vout.t[:, c_, 8:16], in_=v2[:, c_, :]), [vals2], [vout])
                k.op("dve", lambda e, c_=c_: e.max_index(out=iout.t[:, c_, 8:16], in_max=vout.t[:, c_, 8:16],
                                                         in_values=v2[:, c_, :]), [vals2, vout], [iout])
                if c_ % 2 == 1:
                    yield

        def front(ti):
            smp = ti == NT - 1
            x, h2, ei, gw = x1[ti % 2], h2s[ti % 2], eis[ti % 2], gws[ti % 2]
            if ti == 0 or smp:
                md = modS if smp else modP
                k.dma("sp", A2.t[:], md.t[:, 4 * D:5 * D], [md], [A2])
                k.dma("sp", B2.t[:], md.t[:, 3 * D:4 * D], [md], [B2])
            k.dma("sp", x.t[:], X1.t[ti * 128:(ti + 1) * 128, :], [X1], [x])
            layer_norm(lns, x.t[:], [x], A2.t[:], B2.t[:], [A2, B2], h2.t[:], [h2], "c")
            yield
            k.op("act", lambda e: e.activation(out=h2b.t[:], in_=h2.t[:], func=AF.Copy), [h2], [h2b])
            transpose8(h2b, h2T, pbb[0])
            yield
            for q4 in range(4):
                pp = pbank[q4 % 2]
                for cc in range(4):
                    ch = q4 * 4 + cc
                    for c in range(8):
                        k.op("pe", lambda e, c=c, cc=cc, ch=ch, pp=pp: e.matmul(pp.t[:, cc * 128:(cc + 1) * 128],
                                                                              lhsT=Wpq.t[:, c, ch * 128:(ch + 1) * 128], rhs=h2T.t[:, c, :],
                                                                              start=(c == 0), stop=(c == 7)), [Wpq, h2T], [pp])
                k.op("act", lambda e, q4=q4, pp=pp: e.activation(out=qvT.t[:, q4 * 4:(q4 + 1) * 4, :].rearrange("p c t -> p (c t)"),
                                                                 in_=pp.t[:], func=AF.Copy), [pp], [qvT])
                yield
            for q4 in range(4):
                pp = pbank[2 + q4 % 2]
                for cc in range(4):
                    ch = q4 * 4 + cc
                    k.op("pe", lambda e, cc=cc, ch=ch, pp=pp: e.matmul(pp.t[:, cc * 128:(cc + 1) * 128], lhsT=qvT.t[:, ch, :],
                                                                       rhs=kkT.t[:, ch, :], start=True, stop=True), [qvT, kkT], [pp])
                k.op("act", lambda e, q4=q4, pp=pp: e.activation(out=sA.t[:, q4 * 512:(q4 + 1) * 512], in_=pp.t[:], func=AF.Copy), [pp], [sA])
            yield
            yield from top16(sA, sB, v16, i16u, 16, 128)
            k.op("dve", lambda e: e.tensor_copy(out=i16f.t[:], in_=i16u.t[:]), [i16u], [i16f])
            v4 = v16.t[:].rearrange("p (h w) k -> p h w k", w=2)
            i4 = i16f.t[:].rearrange("p (h w) k -> p h w k", w=2)
            k.op("dve", lambda e: e.tensor_tensor(out=sB.t[:].rearrange("p (h a b) -> p h a b", h=8, a=16),
                                                  in0=v4[:, :, 0, :].unsqueeze(3).to_broadcast([128, 8, 16, 16]),
                                                  in1=v4[:, :, 1, :].unsqueeze(2).to_broadcast([128, 8, 16, 16]), op=ALU.add),
                 [v16], [sB])
            yield
            yield from top16(sB, sA, sc16, ciu, 8, 256)
            k.op("dve", lambda e: e.tensor_copy(out=cif.t[:], in_=ciu.t[:].rearrange("p h k -> p (h k)")), [ciu], [cif])
            k.op("dve", lambda e: e.tensor_copy(out=di1.t[:, :, 0:1], in_=i4[:, :, 0, 0:1]), [i16f], [di1])
            k.op("dve", lambda e: e.tensor_tensor(out=di1.t[:, :, 1:16], in0=i4[:, :, 0, 1:16], in1=i4[:, :, 0, 0:15], op=ALU.subtract),
                 [i16f], [di1])
            bigv = sA.t[:].rearrange("p (h k j) -> p h k j", h=8, k=16)
            bigf = sA.t[:].rearrange("p (m j) -> p m j", j=16)
            cif4 = cif.t[:].rearrange("p (h k) -> p h k", h=8).unsqueeze(3).to_broadcast([128, 8, 16, 16])
            k.op("dve", lambda e: e.tensor_tensor(out=bigv, in0=cif4, in1=thr16.t[:].unsqueeze(1).unsqueeze(1).to_broadcast([128, 8, 16, 16]),
                                                  op=ALU.is_ge), [cif, thr16], [sA])
            k.op("dve", lambda e: e.tensor_reduce(out=a1.t[:], in_=bigf, axis=AX.X, op=ALU.add), [sA], [a1])
            yield
            k.op("dve", lambda e: e.tensor_tensor(out=bigv, in0=bigv, in1=di1.t[:].unsqueeze(2).to_broadcast([128, 8, 16, 16]), op=ALU.mult),
                 [sA, di1], [sA])
            k.op("dve", lambda e: e.tensor_reduce(out=i1s.t[:], in_=bigf, axis=AX.X, op=ALU.add), [sA], [i1s])
            k.op("dve", lambda e: e.scalar_tensor_tensor(out=bsel.t[:], in0=a1.t[:], scalar=-16.0, in1=cif.t[:], op0=ALU.mult, op1=ALU.add),
                 [a1, cif], [bsel])
            k.op("dve", lambda e: e.tensor_scalar_add(out=bsel.t[:], in0=bsel.t[:], scalar1=16.0), [bsel], [bsel])
            yield
            bs4 = bsel.t[:].rearrange("p (h k) -> p h k", h=8).unsqueeze(3).to_broadcast([128, 8, 16, 16])
            k.op("dve", lambda e: e.tensor_tensor(out=bigv, in0=bs4, in1=io16.t[:].unsqueeze(1).unsqueeze(1).to_broadcast([128, 8, 16, 16]),
                                                  op=ALU.is_equal), [bsel, io16], [sA])
            k.op("dve", lambda e: e.tensor_tensor(out=bigv, in0=bigv, in1=i4[:, :, 1, :].unsqueeze(2).to_broadcast([128, 8, 16, 16]),
                                                  op=ALU.mult), [sA, i16f], [sA])
            k.op("dve", lambda e: e.tensor_reduce(out=i2s.t[:], in_=bigf, axis=AX.X, op=ALU.add), [sA], [i2s])
            yield
            k.op("dve", lambda e: e.scalar_tensor_tensor(out=i1s.t[:], in0=i1s.t[:], scalar=128.0, in1=i2s.t[:], op0=ALU.mult, op1=ALU.add),
                 [i1s, i2s], [i1s])
            k.op("dve", lambda e: e.tensor_copy(out=ei.t[:], in_=i1s.t[:]), [i1s], [ei])
            k.op("dve", lambda e: e.tensor_tensor(out=gw.t[:], in0=sc16.t[:], in1=sc16.t[:, :, 0:1].to_broadcast([128, 8, 16]), op=ALU.subtract),
                 [sc16], [gw])
            k.op("act", lambda e: e.activation(out=gw.t[:], in_=gw.t[:], func=AF.Exp), [gw], [gw])
            k.op("dve", lambda e: e.tensor_reduce(out=zs.t[:], in_=gw.t[:], axis=AX.X, op=ALU.add), [gw], [zs])
            k.op("dve", lambda e: e.reciprocal(out=zs.t[:], in_=zs.t[:]), [zs], [zs])
            k.op("dve", lambda e: e.tensor_tensor(out=gw.t[:], in0=gw.t[:], in1=zs.t[:].unsqueeze(2).to_broadcast([128, 8, 16]), op=ALU.mult),
                 [gw, zs], [gw])
            yield

        def drain(gen):
            if gen is not None:
                for _ in gen:
                    pass

        gcnt = 0
        dcnt = 0
        drain(front(0))
        for ti in range(NT):
            smp = ti == NT - 1
            x, h2, ei, gw = x1[ti % 2], h2s[ti % 2], eis[ti % 2], gws[ti % 2]
            nxt = front(ti + 1) if ti + 1 < NT else None
            if ti == 0 or smp:
                md = modS if smp else modP
                k.dma("sp", G2.t[:], md.t[:, 5 * D:6 * D], [md], [G2])
            ngrp = 128 // GS
            tiles = {}

            def tail(gi_):
                nonlocal dcnt
                cs_ = slice(gi_ * GS, (gi_ + 1) * GS)
                k.op("act", lambda e: e.activation(out=actv.t[:, cs_], in_=dots.t[:, cs_], func=AF.Gelu), [dots], [actv])
                k.op("dve", lambda e: e.tensor_tensor(out=gco.t[:, cs_], in0=actv.t[:, cs_], in1=gw.t[:].rearrange("p h k -> p (h k)")[:, cs_],
                                                      op=ALU.mult), [actv, gw], [gco])
                for c_ in range(gi_ * GS, (gi_ + 1) * GS):
                    gt = tiles.pop(c_)
                    d_ = dg[dcnt % 8]
                    dcnt += 1
                    k.op("act", lambda e, d_=d_, c_=c_: e.activation(out=d_.t[:], in_=identb.t[:], func=AF.Identity, scale=gco.t[:, c_:c_ + 1]),
                         [identb, gco], [d_])
                    for hf in range(2):
                        k.op("pe", lambda e, d_=d_, gt=gt, hf=hf, c_=c_: e.matmul(pF[hf].t[:, :], lhsT=d_.t[:], rhs=gt.t[:, 1, hf * 512:(hf + 1) * 512],
                                                                               start=(c_ == 0), stop=(c_ == 127)), [d_, gt], [pF[hf]])

            for gi_ in range(ngrp):
                for c_ in range(gi_ * GS, (gi_ + 1) * GS):
                    gt = gb_[gcnt % NG]
                    gcnt += 1
                    tiles[c_] = gt
                    k.dma("pool", gt.t[:].rearrange("p a n -> p (a n)"), PUV.t, [ei] + PUV.bs, [gt], indirect=ei.t[:, c_:c_ + 1])
                    pb2 = prodb[c_ % 2]
                    k.op("dve", lambda e, gt=gt, pb2=pb2: e.tensor_tensor(out=pb2.t[:], in0=gt.t[:, 0, :], in1=h2.t[:], op=ALU.mult),
                         [gt, h2], [pb2])
                    k.op("act", lambda e, pb2=pb2, c_=c_: e.activation(out=junk.t[:], in_=pb2.t[:], func=AF.Copy,
                                                                       accum_out=dots.t[:, c_:c_ + 1]), [pb2], [junk, dots])
                if gi_ > 0:
                    tail(gi_ - 1)
                if nxt is not None:
                    next(nxt, None)
            tail(ngrp - 1)
            drain(nxt)
            for hf in range(2):
                hs = slice(hf * 512, (hf + 1) * 512)
                k.op("dve", lambda e, hf=hf, hs=hs: e.tensor_tensor(out=ff.t[:, hs], in0=pF[hf].t[:], in1=G2.t[:, hs], op=ALU.mult),
                     [pF[hf], G2], [ff])
            k.op("dve", lambda e, x=x: e.scalar_tensor_tensor(out=ff.t[:], in0=x.t[:], scalar=DN_ALPHA, in1=ff.t[:], op0=ALU.mult, op1=ALU.add),
                 [x, ff], [ff])
            layer_norm(lns, ff.t[:], [ff], l2w.t[:], l2b.t[:], [l2w, l2b], yt.t[:], [yt], "c2")
            if smp:
                k.dma("sp", O["ys"], yt.t[0:16, :], [yt], [dram_out])
            else:
                k.dma("sp", O["yo"][ti * 128:(ti + 1) * 128, :], yt.t[:], [yt], [dram_out])
        k.barrier()
        st.close()

    phase0()
    phaseS()
    phaseA()
    phaseG()
    phaseB()
    phaseC()
    k.barrier()
    g.close()
    k.es.close()
    return nc


_NC = None


def _consts():
    identf = np.eye(128, dtype=np.float32)
    s = np.arange(128)[:, None]
    t = np.arange(128)[None, :]
    tri = (s <= t).astype(np.float32)
    tris = (s > t).astype(np.float32)
    slopes = np.exp2(-8.0 * np.arange(1, 13, dtype=np.float32) / 12).astype(np.float32)
    biasT = np.zeros((128, 12, 256), np.float32)
    ki = np.arange(128)[:, None]
    qi = np.arange(128)[None, :]
    for gi, dil in enumerate(DILS):
        for j in range(4):
            sl = slopes[gi * 4 + j]
            st_prev = qi + 128 - ki
            st_cur = qi - ki
            bp = np.where(st_prev <= 128, -sl * st_prev * dil, -30000.0)
            bc = np.where(st_cur >= 0, -sl * st_cur * dil, -30000.0)
            biasT[:, gi * 4 + j, 0:128] = bp
            biasT[:, gi * 4 + j, 128:256] = bc
    iota16 = np.tile(np.arange(16, dtype=np.float32)[None, :], (128, 1))
    biasS = np.zeros((3, 128, 4), np.float32)
    for gi, dil in enumerate(DILS):
        for j in range(4):
            biasS[gi, :, j] = -slopes[gi * 4 + j] * (128 - np.arange(128)) * dil
    return dict(identf=identf, tri=tri, tris=tris, biasT=biasT, iota16=iota16, biasS=biasS.reshape(1, -1))


def kernel(x_prompt, x_sample, c_prompt, c_sample, cache_kv_w128, cache_kv_w512, cache_kv_w2048, state_gla,
           w_ada, b_ada, w_in, w_gla_up, b_gla, gla_norm_w, w_br_a, w_br_b, w_out, ln1_w, ln1_b,
           w_pq, peer_k1, peer_k2, peer_u, peer_v, ln2_w, ln2_b):
    global _NC
    f = lambda a: np.ascontiguousarray(np.asarray(a, dtype=np.float32))
    if _NC is None:
        _NC = build()
    nc = _NC
    cst = _consts()
    shared = dict(w_ada=f(w_ada[0]), b_ada=f(b_ada), w_in=f(w_in[0]), w_up=f(w_gla_up[0]), b_gla=f(b_gla),
                  gnw=f(gla_norm_w), w_br_a=f(w_br_a[0]), w_br_b=f(w_br_b[0]), w_out=f(w_out[0]), ln1_w=f(ln1_w),
                  ln1_b=f(ln1_b), w_pq=f(w_pq[0]), pk1=f(peer_k1[0]), pk2=f(peer_k2[0]), pu=f(peer_u[0]),
                  pv=f(peer_v[0]), ln2_w=f(ln2_w), ln2_b=f(ln2_b), **cst)
    xpr = f(x_prompt)
    in_maps = []
    for c in range(8):
        b, s = c // 4, c % 4
        xp = np.zeros((NPRE, D), np.float32)
        if s > 0:
            xp[NPRE - s * SEG:] = xpr[b, :s * SEG]
        sq = slice(c * 16, (c + 1) * 16)
        cS = np.zeros((128, D), np.float32)
        cS[:16] = c_sample[sq]
        xs = np.zeros((128, D), np.float32)
        xs[:16] = x_sample[sq, 0]
        m = dict(shared)
        m.update(xo=np.ascontiguousarray(xpr[b, s * SEG:(s + 1) * SEG]), xp=xp,
                 segf=np.full((128, 1), float(s), np.float32), cP=f(c_prompt[b:b + 1]), cS=cS, xs=xs,
                 c128=f(cache_kv_w128[0, sq]).reshape(16, 128, 512), c512=f(cache_kv_w512[0, sq]).reshape(16, 512, 512),
                 c2048=f(cache_kv_w2048[0, sq]).reshape(16, 2048, 512), sgla=f(state_gla[0, sq]))
        in_maps.append(m)
    res = run_bass_kernel_spmd(nc, in_maps, core_ids=list(range(8))).results
    yp = np.stack([np.concatenate([res[b * 4 + s]["yo"] for s in range(4)], 0) for b in range(2)])
    ys = np.concatenate([res[c]["ys"] for c in range(8)], 0).reshape(128, 1, D)
    kvp = [np.stack([res[b * 4 + 3]["kvp%d" % w].reshape(w, 2, 4, 64) for b in range(2)])[None] for w in WINS]
    glap = np.stack([res[b * 4 + 3]["glap"] for b in range(2)])[None]
    kvs = [np.concatenate([res[c]["kvs%d" % w] for c in range(8)], 0).reshape(128, w, 2, 4, 64)[None] for w in WINS]
    glas = np.concatenate([res[c]["glas"] for c in range(8)], 0)[None]
    return (yp, ys, kvp[0], kvp[1], kvp[2], glap, kvs[0], kvs[1], kvs[2], glas)
```

```python
import numpy as np
from contextlib import ExitStack
import concourse.bass as bass
import concourse.mybir as mybir
from concourse.bass_utils import run_bass_kernel_spmd

F32 = mybir.dt.float32
BF16 = mybir.dt.bfloat16
I32 = mybir.dt.int32
U32 = mybir.dt.uint32
AF = mybir.ActivationFunctionType
ALU = mybir.AluOpType
AX = mybir.AxisListType

D = 1024
SEG = 4096
NPRE = 12288
NDS = 72
DN_ALPHA = 2.0 ** 0.25
LN_EPS = 1e-5
C_QA, C_KA, C_VA, C_QB, C_KB, C_VB, C_RB, C_GLR, C_GA, C_GB = 0, 768, 1536, 2304, 2816, 3328, 4352, 5376, 5392, 6416
DILS = (1, 4, 16)
WINS = (128, 512, 2048)


class Buf:
    __slots__ = ("w", "r")

    def __init__(self):
        self.w = None
        self.r = {}


class T:
    def __init__(self, t, n=1):
        self.t = t
        self.bs = [Buf() for _ in range(n)]
        self.b = self.bs[0]


class KB:
    def __init__(self, nc):
        self.nc = nc
        self.E = {"pe": nc.tensor, "act": nc.scalar, "dve": nc.vector, "pool": nc.gpsimd, "sp": nc.sync}
        self.es = ExitStack()
        self.semobj = {}
        self.cnt = {}
        self.known = {}
        for n in self.E:
            self.semobj["e:" + n] = self.es.enter_context(nc.semaphore("s_" + n))
            self.cnt[n] = 0
            self.known[n] = {}
        for j in range(NDS):
            self.semobj["d:%d" % j] = self.es.enter_context(nc.semaphore("d%d" % j))
        self.duse = [0] * NDS
        self.drr = 0
        self.uid = 0
        self.nbg = 0
        for j in range(12):
            self.semobj["b:%d" % j] = self.es.enter_context(nc.semaphore("b%d" % j))

    def sb(self, st, name, shape, dt, n=1):
        self.uid += 1
        return T(st.enter_context(self.nc.sbuf_tensor("%s_%d" % (name, self.uid), shape, dt)), n)

    def ps(self, st, name, shape, dt, n=1):
        self.uid += 1
        return T(st.enter_context(self.nc.psum_tensor("%s_%d" % (name, self.uid), shape, dt)), n)

    def _wait(self, en, deps):
        need = {}
        for kk, v in deps:
            if v > need.get(kk, 0):
                need[kk] = v
        kn = self.known[en]
        for kk, v in need.items():
            if kn.get(kk, 0) >= v:
                continue
            self.E[en].wait_ge(self.semobj[kk], v)
            kn[kk] = v

    def _deps(self, en, reads, writes, is_dma):
        own = None if is_dma else "e:" + en
        deps = []
        for b in reads:
            if b.w is not None:
                deps.append(b.w)
        for b in writes:
            if b.w is not None and b.w[0] != own:
                deps.append(b.w)
            for kk, v in b.r.items():
                if kk != own:
                    deps.append((kk, v))
        return deps

    def _commit(self, ev, reads, writes):
        for b in reads:
            if ev[1] > b.r.get(ev[0], 0):
                b.r[ev[0]] = ev[1]
        for b in writes:
            b.w = ev
            b.r = {}

    @staticmethod
    def _flat(items):
        out = []
        for x in items:
            if isinstance(x, T):
                out.extend(x.bs)
            else:
                out.append(x)
        return out

    def op(self, en, fn, reads=(), writes=()):
        reads = self._flat(reads)
        writes = self._flat(writes)
        self._wait(en, self._deps(en, reads, writes, False))
        self.cnt[en] += 1
        ev = ("e:" + en, self.cnt[en])
        fn(self.E[en]).then_inc(self.semobj[ev[0]], 1)
        self._commit(ev, reads, writes)

    def dma(self, q, out, in_, reads=(), writes=(), indirect=None, **kw):
        reads = self._flat(reads)
        writes = self._flat(writes)
        j = self.drr
        self.drr = (self.drr + 1) % NDS
        key = "d:%d" % j
        deps = self._deps(q, reads, writes, True)
        if self.duse[j] > 0:
            deps.append((key, 16 * self.duse[j]))
        self._wait(q, deps)
        self.duse[j] += 1
        ev = (key, 16 * self.duse[j])
        if indirect is None:
            ins = self.E[q].dma_start(out=out, in_=in_, **kw)
        else:
            ins = self.E[q].indirect_dma_start(out=out, out_offset=None, in_=in_,
                                               in_offset=bass.IndirectOffsetOnAxis(ap=indirect, axis=0))
        ins.then_inc(self.semobj[key], 16)
        self._commit(ev, reads, writes)

    def dma_bg(self, q, out, in_):
        key = "b:%d" % self.nbg
        self.nbg += 1
        self.E[q].dma_start(out=out, in_=in_).then_inc(self.semobj[key], 16)

    def barrier(self, final=False):
        deps = [("e:" + n, c) for n, c in self.cnt.items() if c > 0]
        if final:
            deps += [("b:%d" % j, 16) for j in range(self.nbg)]
        deps += [("d:%d" % j, 16 * u) for j, u in enumerate(self.duse) if u > 0]
        for en in self.E:
            self._wait(en, [d for d in deps if d[0] != "e:" + en])


def bcast_rows(ap_row, p):
    n = ap_row.shape[-1]
    return bass.AP(ap_row.tensor, ap_row.offset, [[0, p], [1, n]])


def build():
    nc = bass.Bass("TRN2", target_bir_lowering=False)
    k = KB(nc)

    def din(name, shape, dt=F32):
        return nc.dram_tensor(name, shape, dt, kind="ExternalInput").ap()

    def dout(name, shape, dt=F32):
        return nc.dram_tensor(name, shape, dt, kind="ExternalOutput").ap()

    def dscr(name, shape, dt=F32):
        return T(nc.dram_tensor(name, shape, dt, kind="Internal").ap(), 1)

    I = dict(
        xo=din("xo", [SEG, D]), xp=din("xp", [NPRE, D]), segf=din("segf", [128, 1]),
        cP=din("cP", [1, D]), cS=din("cS", [128, D]), xs=din("xs", [128, D]),
        c128=din("c128", [16, 128, 512]), c512=din("c512", [16, 512, 512]), c2048=din("c2048", [16, 2048, 512]),
        sgla=din("sgla", [16, 4, 128, 256]),
        w_ada=din("w_ada", [D, 6 * D]), b_ada=din("b_ada", [1, 6 * D]), w_in=din("w_in", [D, 7440]),
        w_up=din("w_up", [16, 512]), b_gla=din("b_gla", [1, 512]), gnw=din("gnw", [1, D]),
        w_br_a=din("w_br_a", [256, D]), w_br_b=din("w_br_b", [D, D]), w_out=din("w_out", [D, D]),
        ln1_w=din("ln1_w", [1, D]), ln1_b=din("ln1_b", [1, D]), w_pq=din("w_pq", [D, 2048]),
        pk1=din("pk1", [8, 128, 128]), pk2=din("pk2", [8, 128, 128]),
        pu=din("pu", [16384, D]), pv=din("pv", [16384, D]),
        ln2_w=din("ln2_w", [1, D]), ln2_b=din("ln2_b", [1, D]),
        identf=din("identf", [128, 128]), tri=din("tri", [128, 128]), tris=din("tris", [128, 128]),
        biasT=din("biasT", [128, 12, 256]), iota16=din("iota16", [128, 16]), biasS=din("biasS", [1, 3 * 128 * 4]),
    )
    O = dict(
        yo=dout("yo", [SEG, D]), ys=dout("ys", [16, D]),
        kvp128=dout("kvp128", [128, 512]), kvp512=dout("kvp512", [512, 512]), kvp2048=dout("kvp2048", [2048, 512]),
        glap=dout("glap", [4, 128, 256]),
        kvs128=dout("kvs128", [16, 128, 512]), kvs512=dout("kvs512", [16, 512, 512]),
        kvs2048=dout("kvs2048", [16, 2048, 512]), glas=dout("glas", [16, 4, 128, 256]),
    )
    NT = SEG // 128 + 1
    modS = dscr("modS", [128, 6 * D])
    modP = dscr("modP", [128, 6 * D])
    OAT = dscr("OAT", [4, 64, NT * 128], BF16)
    OB = dscr("OB", [NT * 128, D], BF16)
    X1 = dscr("X1", [NT * 128, D], F32)
    PUV = T(nc.dram_tensor("PUV", [16384, 2 * D], BF16, kind="Internal").ap(), 64)
    dram_out = T(None, 1)

    g = ExitStack()
    identf = k.sb(g, "identf", [128, 128], F32)
    identb = k.sb(g, "identb", [128, 128], BF16)
    trif = k.sb(g, "trif", [128, 128], F32)
    trisf = k.sb(g, "trisf", [128, 128], F32)
    onesf = k.sb(g, "onesf", [128, 128], F32)
    onesb = k.sb(g, "onesb", [128, 128], BF16)
    segc = k.sb(g, "segc", [128, 1], F32)
    S = k.sb(g, "S", [128, 4, 256], F32)
    Sb = k.sb(g, "Sb", [128, 4, 256], BF16)
    pbank = [k.ps(g, "pf%d" % i, [128, 512], F32) for i in range(6)]
    pbb = [k.ps(g, "pb%d" % i, [128, 1024], BF16) for i in range(2)]

    k.dma("sp", identf.t[:], I["identf"], [], [identf])
    k.dma("pool", identb.t[:], I["identf"], [], [identb])
    k.dma("sp", trif.t[:], I["tri"], [], [trif])
    k.dma("sp", trisf.t[:], I["tris"], [], [trisf])
    k.dma("sp", segc.t[:], I["segf"], [], [segc])
    k.op("pool", lambda e: e.memset(onesf.t[:], 1.0), [], [onesf])
    k.op("pool", lambda e: e.memset(onesb.t[:], 1.0), [], [onesb])
    k.op("pool", lambda e: e.memset(S.t[:], 0.0), [], [S])
    k.op("pool", lambda e: e.memset(Sb.t[:], 0.0), [], [Sb])

    w_in_v = I["w_in"].rearrange("(c p) n -> p c n", p=128)

    def layer_norm(*a):
        for _ in layer_norm_g(*a):
            pass

    def layer_norm_g(st_tmp, src_ap, src_bufs, A_ap, B_ap, ab_bufs, out_ap, out_bufs, tag):
        st_tmp["n"] += 1
        sel = st_tmp["n"] % 2
        junk, s1, s2, t1 = st_tmp["junk"][sel], st_tmp["s1"][sel], st_tmp["s2"][sel], st_tmp["t1"][sel]
        k.op("act", lambda e: e.activation(out=junk.t[:], in_=src_ap, func=AF.Identity, accum_out=s1.t[:, 0:1]),
             src_bufs, [junk, s1])
        k.op("act", lambda e: e.activation(out=junk.t[:], in_=src_ap, func=AF.Square, accum_out=s2.t[:, 0:1]),
             src_bufs, [junk, s2])
        yield
        k.op("dve", lambda e: e.tensor_scalar(out=t1.t[:, 0:1], in0=s1.t[:, 0:1], scalar1=1.0 / D, scalar2=None,
                                              op0=ALU.mult), [s1], [t1])
        k.op("dve", lambda e: e.tensor_tensor(out=t1.t[:, 1:2], in0=t1.t[:, 0:1], in1=t1.t[:, 0:1], op=ALU.mult),
             [t1], [t1])
        k.op("dve", lambda e: e.scalar_tensor_tensor(out=t1.t[:, 2:3], in0=s2.t[:, 0:1], scalar=1.0 / D,
                                                     in1=t1.t[:, 1:2], op0=ALU.mult, op1=ALU.subtract), [s2, t1], [t1])
        k.op("dve", lambda e: e.tensor_scalar(out=t1.t[:, 2:3], in0=t1.t[:, 2:3], scalar1=LN_EPS, scalar2=None,
                                              op0=ALU.add), [t1], [t1])
        yield
        k.op("act", lambda e: e.activation(out=t1.t[:, 3:4], in_=t1.t[:, 2:3], func=AF.Ln), [t1], [t1])
        k.op("act", lambda e: e.activation(out=t1.t[:, 4:5], in_=t1.t[:, 3:4], func=AF.Exp, scale=-0.5), [t1], [t1])
        k.op("dve", lambda e: e.tensor_scalar(out=t1.t[:, 5:6], in0=t1.t[:, 0:1], scalar1=t1.t[:, 4:5], scalar2=-1.0,
                                              op0=ALU.mult, op1=ALU.mult), [t1], [t1])
        k.op("act", lambda e: e.activation(out=junk.t[:], in_=src_ap, func=AF.Identity, bias=t1.t[:, 5:6],
                                           scale=t1.t[:, 4:5]), src_bufs + [t1], [junk])
        yield
        k.op("dve", lambda e: e.tensor_tensor(out=junk.t[:], in0=junk.t[:], in1=A_ap, op=ALU.mult),
             [junk] + ab_bufs, [junk])
        k.op("dve", lambda e: e.tensor_tensor(out=out_ap, in0=junk.t[:], in1=B_ap, op=ALU.add),
             [junk] + ab_bufs, out_bufs)

    def ln_scratch(st):
        return dict(n=0, junk=[k.sb(st, "lnjunk", [128, D], F32) for _ in range(2)],
                    s1=[k.sb(st, "lns1", [128, 1], F32) for _ in range(2)],
                    s2=[k.sb(st, "lns2", [128, 1], F32) for _ in range(2)],
                    t1=[k.sb(st, "lnt1", [128, 8], F32) for _ in range(2)])

    def transpose8(src, dst, pb, rows=128):
        for c in range(8):
            k.op("pe", lambda e, c=c: e.transpose(out=pb.t[:, c * 128:(c + 1) * 128], in_=src.t[:, c * 128:(c + 1) * 128],
                                                  identity=identb.t[:]), [src, identb], [pb])
        k.op("act", lambda e: e.activation(out=dst.t[:].rearrange("p c t -> p (c t)"), in_=pb.t[:], func=AF.Copy),
             [pb], [dst])

    def load_w(st, name, dram_view, c0, ncols, q="pool"):
        npc = (ncols + 511) // 512
        w = k.sb(st, name, [128, 8, ncols], BF16, n=8 * npc)
        for c in range(8):
            for pi, j0 in enumerate(range(0, ncols, 512)):
                j1 = min(ncols, j0 + 512)
                k.dma(q, w.t[:, c, j0:j1], dram_view[:, c, c0 + j0:c0 + j1], [], [w.bs[c * npc + pi]])
        return w

    def phase0():
        st = ExitStack()
        cs = k.sb(st, "cs", [128, D], F32)
        cp = k.sb(st, "cp", [1, D], F32)
        csT = k.sb(st, "csT", [128, 8, 128], BF16)
        cpT = k.sb(st, "cpT", [128, 8], BF16)
        bada = k.sb(st, "bada", [128, 512], F32)
        wblk = [k.sb(st, "wblk%d" % i, [128, 8, 512], BF16, n=8) for i in range(2)]
        mrow = k.sb(st, "mrow", [1, 512], F32)
        stg = [k.sb(st, "stg%d" % i, [128, 512], F32) for i in range(2)]
        stg2 = [k.sb(st, "stgp%d" % i, [128, 512], F32) for i in range(2)]
        k.dma("sp", cs.t[:], I["cS"], [], [cs])
        k.dma("sp", cp.t[:], I["cP"], [], [cp])
        k.op("act", lambda e: e.activation(out=cs.t[:], in_=cs.t[:], func=AF.Silu), [cs], [cs])
        k.op("act", lambda e: e.activation(out=cp.t[:], in_=cp.t[:], func=AF.Silu), [cp], [cp])
        pf = pbank[0]
        for c in range(8):
            k.op("pe", lambda e, c=c: e.transpose(out=pf.t[:, 0:128], in_=cs.t[:, c * 128:(c + 1) * 128],
                                                  identity=identf.t[:]), [cs, identf], [pf])
            k.op("pe", lambda e, c=c: e.matmul(pf.t[:, 128:129], lhsT=cp.t[0:1, c * 128:(c + 1) * 128],
                                               rhs=onesf.t[0:1, 0:1], start=True, stop=True), [cp, onesf], [pf])
            k.op("dve", lambda e, c=c: e.tensor_copy(out=csT.t[:, c, :], in_=pf.t[:, 0:128]), [pf], [csT])
            k.op("dve", lambda e, c=c: e.tensor_copy(out=cpT.t[:, c:c + 1], in_=pf.t[:, 128:129]), [pf], [cpT])
        w_ada_v = I["w_ada"].rearrange("(c p) n -> p c n", p=128)
        for blk in range(12):
            wb = wblk[blk % 2]
            cols = slice(blk * 512, (blk + 1) * 512)
            for c in range(8):
                k.dma("pool", wb.t[:, c, :], w_ada_v[:, c, cols], [], [wb.bs[c]])
            k.dma("sp", bada.t[:], bcast_rows(I["b_ada"][:, cols], 128), [], [bada])
            p1, p2, p3 = pbank[1], pbank[2], pbank[3]
            for c in range(8):
                k.op("pe", lambda e, c=c: e.matmul(p1.t[:, :], lhsT=csT.t[:, c, :], rhs=wb.t[:, c, :], start=(c == 0),
                                                   stop=(c == 7)), [csT, wb], [p1])
            for c in range(8):
                k.op("pe", lambda e, c=c: e.matmul(p2.t[0:1, :], lhsT=cpT.t[:, c:c + 1], rhs=wb.t[:, c, :], start=(c == 0),
                                                   stop=(c == 7)), [cpT, wb], [p2])
            k.op("act", lambda e: e.activation(out=mrow.t[:], in_=p2.t[0:1, :], func=AF.Copy), [p2], [mrow])
            k.op("pe", lambda e: e.matmul(p3.t[:, :], lhsT=onesf.t[0:1, :], rhs=mrow.t[0:1, :], start=True, stop=True),
                 [onesf, mrow], [p3])
            sa, sp_ = stg[blk % 2], stg2[blk % 2]
            k.op("dve", lambda e: e.tensor_tensor(out=sa.t[:], in0=p1.t[:], in1=bada.t[:], op=ALU.add), [p1, bada], [sa])
            k.op("dve", lambda e: e.tensor_tensor(out=sp_.t[:], in0=p3.t[:], in1=bada.t[:], op=ALU.add), [p3, bada], [sp_])
            if blk in (2, 3, 8, 9):
                k.op("pool", lambda e: e.tensor_scalar_add(out=sa.t[:], in0=sa.t[:], scalar1=1.0), [sa], [sa])
                k.op("pool", lambda e: e.tensor_scalar_add(out=sp_.t[:], in0=sp_.t[:], scalar1=1.0), [sp_], [sp_])
            k.dma("sp", modS.t[:, cols], sa.t[:], [sa], [modS])
            k.dma("sp", modP.t[:, cols], sp_.t[:], [sp_], [modP])
        k.barrier()
        st.close()

    def phaseG():
        st = ExitStack()
        A1 = k.sb(st, "A1", [128, D], F32)
        B1 = k.sb(st, "B1", [128, D], F32)
        k.dma("sp", A1.t[:], modP.t[:, 1 * D:2 * D], [modP], [A1])
        k.dma("sp", B1.t[:], modP.t[:, 0:D], [modP], [B1])
        Wq = load_w(st, "Wq", w_in_v, C_QB, 512)
        Wk = load_w(st, "Wk", w_in_v, C_KB, 512)
        Wv = load_w(st, "Wv", w_in_v, C_VB, 1024)
        Wg = load_w(st, "Wg", w_in_v, C_GLR, 16)
        Wka = load_w(st, "Wka", w_in_v, C_KA, 768)
        Wva = load_w(st, "Wva", w_in_v, C_VA, 768)
        wup = k.sb(st, "wup", [16, 512], BF16)
        bgl = k.sb(st, "bgl", [1, 512], BF16)
        gnw = k.sb(st, "gnw", [128, D], F32)
        k.dma("pool", wup.t[:], I["w_up"], [], [wup])
        k.dma("pool", bgl.t[:], I["b_gla"], [], [bgl])
        k.dma("sp", gnw.t[:], bcast_rows(I["gnw"], 128), [], [gnw])
        lns = ln_scratch(st)
        xt = [k.sb(st, "xt%d" % i, [128, D], F32) for i in range(2)]
        hbs = [k.sb(st, "hb%d" % i, [128, D], BF16) for i in range(2)]
        hTs_ = [k.sb(st, "hT%d" % i, [128, 8, 128], BF16) for i in range(2)]
        ee = k.sb(st, "ee", [128, 512], F32)
        ll = k.sb(st, "ll", [128, 512], F32)
        e3 = k.sb(st, "e3", [128, 512], F32)
        e1 = k.sb(st, "e1", [128, 512], F32)
        khat = k.sb(st, "khat", [128, 512], BF16)
        kchk = k.sb(st, "kchk", [128, 512], BF16)
        qtil = k.sb(st, "qtil", [128, 512], BF16)
        qkT = k.sb(st, "qkT", [128, 8, 128], BF16)
        dec = k.sb(st, "dec", [128, 4], F32)
        scT = k.sb(st, "scT", [128, 4, 128], BF16)
        og = k.sb(st, "og", [128, 4, 256], F32)
        ssq = k.sb(st, "ssq", [128, 8], F32)
        obt = k.sb(st, "obt", [128, D], BF16)
        kvst = [k.sb(st, "kvst%d" % i, [128, 512], F32) for i in range(2)]
        keep = k.sb(st, "keep", [128, 4], F32)
        for j in range(3):
            k.op("dve", lambda e, j=j: e.tensor_scalar(out=keep.t[:, j:j + 1], in0=segc.t[:, 0:1], scalar1=float(j - 2),
                                                       scalar2=0.0, op0=ALU.add, op1=ALU.max), [segc], [keep])
            k.op("dve", lambda e, j=j: e.tensor_scalar(out=keep.t[:, j:j + 1], in0=keep.t[:, j:j + 1], scalar1=1.0,
                                                       scalar2=None, op0=ALU.min), [keep], [keep])
        cvb = [k.sb(st, "cvb%d" % i, [128, 4, D], BF16) for i in range(4)]

        def convert_chunk(ci):
            src, half = (I["pu"], 0) if ci < 32 else (I["pv"], 1)
            cj = ci % 32
            cv = cvb[ci % 4]
            rows = slice(cj * 512, (cj + 1) * 512)
            k.dma("pool", cv.t[:], src[rows, :].rearrange("(p j) n -> p j n", j=4), [], [cv])
            k.dma("sp", PUV.t[rows, half * D:(half + 1) * D].rearrange("(p j) n -> p j n", j=4), cv.t[:], [cv], [PUV.bs[ci]])
        ntile_pre = NPRE // 128
        ntile = ntile_pre + SEG // 128
        pK, pV0, pV1, pM = pbank[1], pbank[2], pbank[3], pbank[5]
        pX, pD = pbank[4], pbank[0]
        pTb, pQK = pbb[0], pbb[1]
        glrTs = [k.sb(st, "glrT%d" % i, [16, 128], BF16) for i in range(2)]
        vbfs = [k.sb(st, "vbf%d" % i, [128, 1024], BF16) for i in range(2)]
        ksbs = [k.sb(st, "ksb%d" % i, [128, 512], F32) for i in range(2)]
        qsbs = [k.sb(st, "qsb%d" % i, [128, 512], F32) for i in range(2)]

        def front1(ti):
            own = ti >= ntile_pre
            to = ti - ntile_pre
            src = I["xo"][to * 128:(to + 1) * 128, :] if own else I["xp"][ti * 128:(ti + 1) * 128, :]
            x = xt[ti % 2]
            hb, hT = hbs[ti % 2], hTs_[ti % 2]
            glrT, vbf, ksb, qsb = glrTs[ti % 2], vbfs[ti % 2], ksbs[ti % 2], qsbs[ti % 2]
            k.dma("sp", x.t[:], src, [], [x])
            yield from layer_norm_g(lns, x.t[:], [x], A1.t[:], B1.t[:], [A1, B1], hb.t[:], [hb], "g")
            if ti % 2 == 0:
                convert_chunk(ti // 2)
            yield

        def front2(ti):
            own = ti >= ntile_pre
            to = ti - ntile_pre
            hb, hT = hbs[ti % 2], hTs_[ti % 2]
            glrT, vbf, ksb, qsb = glrTs[ti % 2], vbfs[ti % 2], ksbs[ti % 2], qsbs[ti % 2]
            transpose8(hb, hT, pTb)
            yield
            for c in range(8):
                k.op("pe", lambda e, c=c: e.matmul(pK.t[:, :], lhsT=hT.t[:, c, :], rhs=Wk.t[:, c, :], start=(c == 0),
                                                   stop=(c == 7)), [hT, Wk], [pK])
            k.op("act", lambda e: e.activation(out=ksb.t[:], in_=pK.t[:], func=AF.Copy), [pK], [ksb])
            yield
            for half, pv in ((0, pV0), (1, pV1)):
                for c in range(8):
                    k.op("pe", lambda e, c=c, half=half, pv=pv: e.matmul(
                        pv.t[:, :], lhsT=hT.t[:, c, :], rhs=Wv.t[:, c, half * 512:(half + 1) * 512], start=(c == 0),
                        stop=(c == 7)), [hT, Wv], [pv])
                yield
            for c in range(8):
                k.op("pe", lambda e, c=c: e.matmul(pM.t[0:16, 0:128], lhsT=Wg.t[:, c, :], rhs=hT.t[:, c, :], start=(c == 0),
                                                   stop=(c == 7)), [hT, Wg], [pM])
            k.op("act", lambda e: e.activation(out=glrT.t[:], in_=pM.t[0:16, 0:128], func=AF.Copy), [pM], [glrT])
            k.op("act", lambda e: e.activation(out=vbf.t[:, 0:512], in_=pV0.t[:], func=AF.Copy), [pV0], [vbf])
            k.op("act", lambda e: e.activation(out=vbf.t[:, 512:1024], in_=pV1.t[:], func=AF.Copy), [pV1], [vbf])
            yield
            if own:
                for c in range(8):
                    k.op("pe", lambda e, c=c: e.matmul(pK.t[:, :], lhsT=hT.t[:, c, :], rhs=Wq.t[:, c, :], start=(c == 0),
                                                       stop=(c == 7)), [hT, Wq], [pK])
                k.op("act", lambda e: e.activation(out=qsb.t[:], in_=pK.t[:], func=AF.Copy, scale=128.0 ** -0.5), [pK], [qsb])
                yield
                for gi in range(3):
                    if SEG - (to + 1) * 128 < WINS[gi]:
                        kv = kvst[gi % 2]
                        for c in range(8):
                            k.op("pe", lambda e, c=c, gi=gi: e.matmul(pM.t[:, 0:256], lhsT=hT.t[:, c, :],
                                                                      rhs=Wka.t[:, c, gi * 256:(gi + 1) * 256],
                                                                      start=(c == 0), stop=(c == 7)), [hT, Wka], [pM])
                        for c in range(8):
                            k.op("pe", lambda e, c=c, gi=gi: e.matmul(pM.t[:, 256:512], lhsT=hT.t[:, c, :],
                                                                      rhs=Wva.t[:, c, gi * 256:(gi + 1) * 256],
                                                                      start=(c == 0), stop=(c == 7)), [hT, Wva], [pM])
                        k.op("act", lambda e, kv=kv: e.activation(out=kv.t[:], in_=pM.t[:], func=AF.Copy), [pM], [kv])
                        r0 = (to + 1) * 128 - (SEG - WINS[gi]) - 128
                        k.dma("sp", O["kvp%d" % WINS[gi]][r0:r0 + 128, :], kv.t[:], [kv], [dram_out])
                        yield

        def back(ti):
            own = ti >= ntile_pre
            to = ti - ntile_pre
            glrT, vbf, ksb, qsb = glrTs[ti % 2], vbfs[ti % 2], ksbs[ti % 2], qsbs[ti % 2]
            k.op("pe", lambda e: e.matmul(pX.t[:, :], lhsT=glrT.t[:, :], rhs=wup.t[:, :], start=True, stop=False),
                 [glrT, wup], [pX])
            k.op("pe", lambda e: e.matmul(pX.t[:, :], lhsT=onesb.t[0:1, :], rhs=bgl.t[0:1, :], start=False, stop=True),
                 [onesb, bgl], [pX])
            k.op("act", lambda e: e.activation(out=ee.t[:], in_=pX.t[:], func=AF.Exp, scale=-1.0), [pX], [ee])
            k.op("act", lambda e: e.activation(out=ll.t[:], in_=ee.t[:], func=AF.Ln, bias=1.0), [ee], [ll])
            yield
            k.op("pe", lambda e: e.matmul(pX.t[:, :], lhsT=trisf.t[:, :], rhs=ll.t[:, :], start=True, stop=True),
                 [trisf, ll], [pX])
            for h in range(4):
                k.op("pe", lambda e, h=h: e.matmul(pD.t[:, h:h + 1], lhsT=ll.t[:, h * 128:(h + 1) * 128],
                                                   rhs=onesf.t[:, 0:1], start=True, stop=True), [ll, onesf], [pD])
            k.op("act", lambda e: e.activation(out=e3.t[:], in_=pX.t[:], func=AF.Exp, scale=-1.0 / 16), [pX], [e3])
            k.op("act", lambda e: e.activation(out=dec.t[:], in_=pD.t[:, 0:4], func=AF.Exp, scale=-1.0 / 16), [pD], [dec])
            yield
            k.op("dve", lambda e: e.tensor_tensor(out=khat.t[:], in0=ksb.t[:], in1=e3.t[:], op=ALU.mult), [ksb, e3], [khat])
            yield
            if own:
                k.op("pe", lambda e: e.matmul(pX.t[:, :], lhsT=trif.t[:, :], rhs=ll.t[:, :], start=True, stop=True),
                     [trif, ll], [pX])
                k.op("act", lambda e: e.activation(out=e1.t[:], in_=pX.t[:], func=AF.Exp, scale=-1.0 / 16), [pX], [e1])
                k.op("act", lambda e: e.activation(out=e3.t[:], in_=pX.t[:], func=AF.Exp, scale=1.0 / 16), [pX], [e3])
                k.op("dve", lambda e: e.tensor_tensor(out=kchk.t[:], in0=ksb.t[:], in1=e3.t[:], op=ALU.mult), [ksb, e3], [kchk])
                k.op("dve", lambda e: e.tensor_tensor(out=qtil.t[:], in0=qsb.t[:], in1=e1.t[:], op=ALU.mult), [qsb, e1], [qtil])
                yield
                for h in range(4):
                    k.op("pe", lambda e, h=h: e.transpose(out=pQK.t[:, h * 128:(h + 1) * 128],
                                                          in_=qtil.t[:, h * 128:(h + 1) * 128], identity=identb.t[:]),
                         [qtil, identb], [pQK])
                    k.op("pe", lambda e, h=h: e.transpose(out=pQK.t[:, (4 + h) * 128:(5 + h) * 128],
                                                          in_=kchk.t[:, h * 128:(h + 1) * 128], identity=identb.t[:]),
                         [kchk, identb], [pQK])
                k.op("act", lambda e: e.activation(out=qkT.t[:].rearrange("p c t -> p (c t)"), in_=pQK.t[:], func=AF.Copy),
                     [pQK], [qkT])
                yield
                for h in range(4):
                    k.op("pe", lambda e, h=h: e.matmul(pX.t[:, h * 128:(h + 1) * 128], lhsT=qkT.t[:, 4 + h, :],
                                                       rhs=qkT.t[:, h, :], start=True, stop=True), [qkT], [pX])
                k.op("dve", lambda e: e.tensor_tensor(
                    out=scT.t[:], in0=pX.t[:].rearrange("p (h t) -> p h t", h=4),
                    in1=trif.t[:].unsqueeze(1).to_broadcast([128, 4, 128]), op=ALU.mult), [pX, trif], [scT])
                yield
                for hp in range(2):
                    for h in (2 * hp, 2 * hp + 1):
                        cs_ = slice((h % 2) * 256, (h % 2) * 256 + 256)
                        k.op("pe", lambda e, h=h, cs_=cs_: e.matmul(pD.t[:, cs_], lhsT=scT.t[:, h, :],
                                                                    rhs=vbf.t[:, h * 256:(h + 1) * 256], start=True, stop=False),
                             [scT, vbf], [pD])
                        k.op("pe", lambda e, h=h, cs_=cs_: e.matmul(pD.t[:, cs_], lhsT=qkT.t[:, h, :], rhs=Sb.t[:, h, :],
                                                                    start=False, stop=True), [qkT, Sb], [pD])
                    k.op("act", lambda e, hp=hp: e.activation(out=og.t[:, 2 * hp:2 * hp + 2, :].rearrange("p h v -> p (h v)"),
                                                              in_=pD.t[:], func=AF.Copy), [pD], [og])
                    yield
                for h in range(4):
                    k.op("dve", lambda e, h=h: e.scalar_tensor_tensor(out=e1.t[:, 0:256], in0=og.t[:, h, :], scalar=1.0,
                                                                      in1=og.t[:, h, :], op0=ALU.mult, op1=ALU.mult,
                                                                      accum_out=ssq.t[:, h:h + 1]), [og], [e1, ssq])
                k.op("dve", lambda e: e.tensor_scalar(out=ssq.t[:, 0:4], in0=ssq.t[:, 0:4], scalar1=1.0 / 256, scalar2=LN_EPS,
                                                      op0=ALU.mult, op1=ALU.add), [ssq], [ssq])
                k.op("act", lambda e: e.activation(out=ssq.t[:, 0:4], in_=ssq.t[:, 0:4], func=AF.Sqrt), [ssq], [ssq])
                k.op("dve", lambda e: e.reciprocal(out=ssq.t[:, 4:8], in_=ssq.t[:, 0:4]), [ssq], [ssq])
                for h in range(4):
                    k.op("dve", lambda e, h=h: e.scalar_tensor_tensor(
                        out=obt.t[:, h * 256:(h + 1) * 256], in0=og.t[:, h, :], scalar=ssq.t[:, 4 + h:5 + h],
                        in1=gnw.t[:, h * 256:(h + 1) * 256], op0=ALU.mult, op1=ALU.mult), [og, ssq, gnw], [obt])
                k.dma("sp", OB.t[to * 128:(to + 1) * 128, :], obt.t[:], [obt], [OB])
                yield
            for hp in range(2):
                for h in (2 * hp, 2 * hp + 1):
                    cs_ = slice((h % 2) * 256, (h % 2) * 256 + 256)
                    k.op("pe", lambda e, h=h, cs_=cs_: e.matmul(pD.t[:, cs_], lhsT=khat.t[:, h * 128:(h + 1) * 128],
                                                                rhs=vbf.t[:, h * 256:(h + 1) * 256], start=True, stop=True),
                         [khat, vbf], [pD])
                for h in (2 * hp, 2 * hp + 1):
                    cs_ = slice((h % 2) * 256, (h % 2) * 256 + 256)
                    k.op("dve", lambda e, h=h, cs_=cs_: e.scalar_tensor_tensor(
                        out=S.t[:, h, :], in0=S.t[:, h, :], scalar=dec.t[:, h:h + 1], in1=pD.t[:, cs_], op0=ALU.mult,
                        op1=ALU.add), [S, dec, pD], [S])
                yield
            if (not own) and (ti + 1) % 32 == 0:
                j = (ti + 1) // 32 - 1
                k.op("dve", lambda e, j=j: e.tensor_scalar(out=S.t[:].rearrange("p h v -> p (h v)"),
                                                           in0=S.t[:].rearrange("p h v -> p (h v)"),
                                                           scalar1=keep.t[:, j:j + 1], scalar2=None, op0=ALU.mult), [S, keep], [S])
            if ti >= ntile_pre - 1:
                k.op("act", lambda e: e.activation(out=Sb.t[:].rearrange("p h v -> p (h v)"),
                                                   in_=S.t[:].rearrange("p h v -> p (h v)"), func=AF.Copy), [S], [Sb])

        def interleave(gens):
            gens = [g_ for g_ in gens if g_ is not None]
            while gens:
                for g_ in list(gens):
                    try:
                        next(g_)
                    except StopIteration:
                        gens.remove(g_)

        interleave([front1(0)])
        interleave([front1(1), front2(0)])
        for ti in range(ntile):
            interleave([front1(ti + 2) if ti + 2 < ntile else None, front2(ti + 1) if ti + 1 < ntile else None, back(ti)])
        k.dma("sp", O["glap"].rearrange("h k v -> k h v"), S.t[:], [S], [dram_out])
        k.barrier()
        st.close()


    def phaseA():
        st = ExitStack()
        A1 = k.sb(st, "A1", [128, D], F32)
        B1 = k.sb(st, "B1", [128, D], F32)
        k.dma("sp", A1.t[:], modP.t[:, 1 * D:2 * D], [modP], [A1])
        k.dma("sp", B1.t[:], modP.t[:, 0:D], [modP], [B1])
        lns = ln_scratch(st)
        xt = [k.sb(st, "xt%d" % i, [128, D], F32) for i in range(2)]
        hbs = [k.sb(st, "hb%d" % i, [128, D], BF16) for i in range(2)]
        hTs = k.sb(st, "hTs", [128, 8, 2048], BF16, n=16)
        HTS = [dscr("HTS%d" % i, [128, 8 * 2048], BF16) for i in range(3)]
        KT = [[k.sb(st, "KT%d%d" % (gi, pr), [128, 2048], BF16) for pr in range(2)] for gi in range(3)]
        VA = [[k.sb(st, "VA%d%d" % (gi, pr), [128, 16, 2, 65], BF16) for pr in range(2)] for gi in range(3)]
        QT = k.sb(st, "QT", [128, 2048], BF16)
        acc = k.sb(st, "acc", [128, 2, 2048], F32)
        oaN = k.sb(st, "oaN", [64, 2, 2048], BF16)
        bT = k.sb(st, "bT", [128, 12, 256], F32)
        bF = k.sb(st, "bF", [128, 12, 128], F32)
        negc = k.sb(st, "negc", [128, 1], F32)
        Tt = [k.sb(st, "Tt%d" % i, [128, 256], F32) for i in range(2)]
        PT = [k.sb(st, "PT%d" % i, [128, 2, 128], BF16) for i in range(2)]
        k.dma("sp", bT.t[:], I["biasT"], [], [bT])
        k.op("dve", lambda e: e.tensor_scalar(out=negc.t[:], in0=segc.t[:], scalar1=1.0, scalar2=-1.0, op0=ALU.min,
                                              op1=ALU.add), [segc], [negc])
        k.op("dve", lambda e: e.tensor_scalar(out=negc.t[:], in0=negc.t[:], scalar1=30000.0, scalar2=None, op0=ALU.mult),
             [negc], [negc])
        k.op("dve", lambda e: e.tensor_scalar(out=bF.t[:], in0=bT.t[:, :, 0:128], scalar1=negc.t[:, 0:1], scalar2=None,
                                              op0=ALU.add), [bT, negc], [bF])
        for gi in range(3):
            for pr in range(2):
                k.op("pool", lambda e, gi=gi, pr=pr: e.memset(VA[gi][pr].t[:], 1.0), [], [VA[gi][pr]])
        pP, pVp = pbank[0], pbank[1]
        pS = [pbank[2], pbank[3]]
        pO = [pbank[4], pbank[5]]
        cnt = 0
        for cp_ in range(2):
            Wq = k.sb(st, "Wqa%d" % cp_, [128, 8, 384], BF16, n=24)
            Wk = k.sb(st, "Wka%d" % cp_, [128, 8, 384], BF16, n=24)
            Wv = k.sb(st, "Wva%d" % cp_, [128, 8, 384], BF16, n=24)
            for gi in range(3):
                c0 = gi * 256 + cp_ * 128
                for c in range(8):
                    k.dma("pool", Wq.t[:, c, gi * 128:(gi + 1) * 128], w_in_v[:, c, C_QA + c0:C_QA + c0 + 128], [], [Wq.bs[gi * 8 + c]])
                    k.dma("pool", Wk.t[:, c, gi * 128:(gi + 1) * 128], w_in_v[:, c, C_KA + c0:C_KA + c0 + 128], [], [Wk.bs[gi * 8 + c]])
                    k.dma("pool", Wv.t[:, c, gi * 128:(gi + 1) * 128], w_in_v[:, c, C_VA + c0:C_VA + c0 + 128], [], [Wv.bs[gi * 8 + c]])
            for span in range(3):
                par = span % 2
                if cp_ == 1:
                    k.dma("sp", hTs.t[:].rearrange("p c t -> p (c t)"), HTS[span].t[:, :], [HTS[span]], hTs.bs)
                for tl in range(16 if cp_ == 0 else 0):
                    hb, pTb = hbs[tl % 2], pbb[tl % 2]
                    if span == 0:
                        src = I["xp"][NPRE - 2048 + tl * 128:NPRE - 2048 + (tl + 1) * 128, :]
                    else:
                        src = I["xo"][(span - 1) * 2048 + tl * 128:(span - 1) * 2048 + (tl + 1) * 128, :]
                    x = xt[tl % 2]
                    k.dma("sp", x.t[:], src, [], [x])
                    layer_norm(lns, x.t[:], [x], A1.t[:], B1.t[:], [A1, B1], hb.t[:], [hb], "a")
                    for c in range(8):
                        k.op("pe", lambda e, c=c: e.transpose(out=pTb.t[:, c * 128:(c + 1) * 128],
                                                              in_=hb.t[:, c * 128:(c + 1) * 128], identity=identb.t[:]),
                             [hb, identb], [pTb])
                    k.op("act", lambda e, tl=tl: e.activation(out=hTs.t[:, :, tl * 128:(tl + 1) * 128],
                                                              in_=pTb.t[:].rearrange("p (c t) -> p c t", c=8), func=AF.Copy),
                         [pTb], [hTs.bs[tl]])
                if cp_ == 0:
                    k.dma("sp", HTS[span].t[:, :], hTs.t[:].rearrange("p c t -> p (c t)"), hTs.bs, [HTS[span]])
                for gi in range(3):
                    dil = DILS[gi]
                    nb = 16 // dil
                    kt, va = KT[gi][par], VA[gi][par]
                    ktp, vap = KT[gi][1 - par], VA[gi][1 - par]
                    wc = slice(gi * 128, (gi + 1) * 128)
                    for tg in range(4):
                        ts_ = slice(tg * 512, (tg + 1) * 512)
                        for c in range(8):
                            k.op("pe", lambda e, c=c, ts_=ts_, wc=wc: e.matmul(pP.t[:, :], lhsT=Wk.t[:, c, wc], rhs=hTs.t[:, c, ts_],
                                                                                 start=(c == 0), stop=(c == 7)),
                                 [Wk] + hTs.bs[tg * 4:tg * 4 + 4], [pP])
                        k.op("act", lambda e, ts_=ts_, kt=kt: e.activation(out=kt.t[:, ts_], in_=pP.t[:], func=AF.Copy), [pP], [kt])
                        if span > 0:
                            for c in range(8):
                                k.op("pe", lambda e, c=c, ts_=ts_, wc=wc: e.matmul(pP.t[:, :], lhsT=Wq.t[:, c, wc],
                                                                                     rhs=hTs.t[:, c, ts_], start=(c == 0),
                                                                                     stop=(c == 7)),
                                     [Wq] + hTs.bs[tg * 4:tg * 4 + 4], [pP])
                            k.op("act", lambda e, ts_=ts_: e.activation(out=QT.t[:, ts_], in_=pP.t[:], func=AF.Copy, scale=0.125),
                                 [pP], [QT])

                    def cols(r, n, dil=dil):
                        st0 = n * 128 * dil + r
                        return slice(st0, st0 + 127 * dil + 1, dil)
                    for b4 in range(4):
                        for bi in range(4):
                            blk = b4 * 4 + bi
                            r, n = blk // nb, blk % nb
                            for c in range(8):
                                k.op("pe", lambda e, c=c, bi=bi, r=r, n=n, wc=wc: e.matmul(
                                    pVp.t[:, bi * 128:(bi + 1) * 128], lhsT=hTs.t[:, c, cols(r, n)], rhs=Wv.t[:, c, wc],
                                    start=(c == 0), stop=(c == 7)), [Wv] + hTs.bs, [pVp])
                        k.op("act", lambda e, b4=b4, va=va: e.activation(
                            out=va.t[:, b4 * 4:(b4 + 1) * 4, :, 0:64],
                            in_=pVp.t[:].rearrange("p (b h d) -> p b h d", b=4, h=2), func=AF.Copy), [pVp], [va])
                    if span == 0:
                        continue
                    for blk in range(16):
                        r, n = blk // nb, blk % nb
                        first = (span == 1 and n == 0)
                        for hh in range(2):
                            gh = gi * 4 + cp_ * 2 + hh
                            hp = slice(hh * 64, (hh + 1) * 64)
                            ps_, po_ = pS[cnt % 2], pO[cnt % 2]
                            tt, pt = Tt[cnt % 2], PT[cnt % 2]
                            cnt += 1
                            if n > 0:
                                kprev, vprev, bprev = kt, va, blk - 1
                                kpc = cols(r, n - 1)
                            else:
                                kprev, vprev, bprev = ktp, vap, r * nb + nb - 1
                                kpc = cols(r, nb - 1)
                            k.op("pe", lambda e, ps_=ps_, kprev=kprev, kpc=kpc, hp=hp, r=r, n=n: e.matmul(
                                ps_.t[:, 0:128], lhsT=kprev.t[hp, kpc], rhs=QT.t[hp, cols(r, n)], start=True, stop=True),
                                 [kprev, QT], [ps_])
                            k.op("pe", lambda e, ps_=ps_, kt=kt, hp=hp, r=r, n=n: e.matmul(
                                ps_.t[:, 128:256], lhsT=kt.t[hp, cols(r, n)], rhs=QT.t[hp, cols(r, n)], start=True, stop=True),
                                 [kt, QT], [ps_])
                            if first:
                                k.op("dve", lambda e, tt=tt, ps_=ps_, gh=gh: e.tensor_tensor(
                                    out=tt.t[:, 0:128], in0=ps_.t[:, 0:128], in1=bF.t[:, gh, :], op=ALU.add), [ps_, bF], [tt])
                                k.op("dve", lambda e, tt=tt, ps_=ps_, gh=gh: e.tensor_tensor(
                                    out=tt.t[:, 128:256], in0=ps_.t[:, 128:256], in1=bT.t[:, gh, 128:256], op=ALU.add),
                                     [ps_, bT], [tt])
                            else:
                                k.op("dve", lambda e, tt=tt, ps_=ps_, gh=gh: e.tensor_tensor(
                                    out=tt.t[:], in0=ps_.t[:, 0:256], in1=bT.t[:, gh, :], op=ALU.add), [ps_, bT], [tt])
                            k.op("act", lambda e, tt=tt, pt=pt: e.activation(out=pt.t[:].rearrange("p a q -> p (a q)"),
                                                                             in_=tt.t[:], func=AF.Exp), [tt], [pt])
                            k.op("pe", lambda e, po_=po_, vprev=vprev, bprev=bprev, hh=hh, pt=pt: e.matmul(
                                po_.t[0:65, 0:128], lhsT=vprev.t[:, bprev, hh, :], rhs=pt.t[:, 0, :], start=True, stop=False),
                                 [vprev, pt], [po_])
                            k.op("pe", lambda e, po_=po_, va=va, blk=blk, hh=hh, pt=pt: e.matmul(
                                po_.t[0:65, 0:128], lhsT=va.t[:, blk, hh, :], rhs=pt.t[:, 1, :], start=False, stop=True),
                                 [va, pt], [po_])
                            if gi == 0:
                                k.op("dve", lambda e, po_=po_, hh=hh, r=r, n=n: e.tensor_copy(
                                    out=acc.t[0:65, hh, cols(r, n)], in_=po_.t[0:65, 0:128]), [po_], [acc])
                            else:
                                k.op("dve", lambda e, po_=po_, hh=hh, r=r, n=n: e.tensor_tensor(
                                    out=acc.t[0:65, hh, cols(r, n)], in0=acc.t[0:65, hh, cols(r, n)], in1=po_.t[0:65, 0:128],
                                    op=ALU.add), [po_, acc], [acc])
                if span == 0:
                    continue
                k.op("dve", lambda e: e.reciprocal(out=acc.t[64:65, :, :], in_=acc.t[64:65, :, :]), [acc], [acc])
                for hh in range(2):
                    for tg in range(4):
                        ts_ = slice(tg * 512, (tg + 1) * 512)
                        k.op("pe", lambda e, hh=hh, ts_=ts_: e.matmul(pP.t[0:64, :], lhsT=onesf.t[64:65, 0:64],
                                                                      rhs=acc.t[64:65, hh, ts_], start=True, stop=True),
                             [onesf, acc], [pP])
                        k.op("dve", lambda e, hh=hh, ts_=ts_: e.tensor_tensor(out=oaN.t[0:64, hh, ts_], in0=acc.t[0:64, hh, ts_],
                                                                              in1=pP.t[0:64, :], op=ALU.mult), [acc, pP], [oaN])
                    t0 = (span - 1) * 2048
                    k.dma("sp", OAT.t[cp_ * 2 + hh, :, t0:t0 + 2048], oaN.t[0:64, hh, :], [oaN], [OAT])
        k.barrier()
        st.close()

    def phaseS():
        st = ExitStack()
        A1 = k.sb(st, "A1", [128, D], F32)
        B1 = k.sb(st, "B1", [128, D], F32)
        k.dma("sp", A1.t[:], modS.t[:, 1 * D:2 * D], [modS], [A1])
        k.dma("sp", B1.t[:], modS.t[:, 0:D], [modS], [B1])
        lns = ln_scratch(st)
        x = k.sb(st, "xs", [128, D], F32)
        hb = k.sb(st, "hb", [128, D], BF16)
        hT = k.sb(st, "hT", [128, 8, 128], BF16)
        pj = k.sb(st, "pj", [128, 4352], F32)
        wb = [k.sb(st, "wb%d" % i, [128, 8, 512], BF16, n=8) for i in range(2)]
        Wg = load_w(st, "Wg", w_in_v, C_GLR, 16)
        wup = k.sb(st, "wup", [16, 512], BF16)
        bgl = k.sb(st, "bgl", [1, 512], BF16)
        gnw = k.sb(st, "gnw", [128, D], F32)
        bS = k.sb(st, "bS", [16, 3, 128, 4], F32)
        k.dma("pool", wup.t[:], I["w_up"], [], [wup])
        k.dma("pool", bgl.t[:], I["b_gla"], [], [bgl])
        k.dma("sp", gnw.t[:], bcast_rows(I["gnw"], 128), [], [gnw])
        k.dma("sp", bS.t[:].rearrange("p g k h -> p (g k h)"), bcast_rows(I["biasS"], 16), [], [bS])
        for W, step in ((2048, 2), (512, 8), (128, 16)):
            for b0 in range(0, 16, step):
                k.dma_bg("act", O["kvs%d" % W][b0:b0 + step, 0:W - 1, :], I["c%d" % W][b0:b0 + step, 1:W, :])
        k.dma("sp", x.t[:], I["xs"], [], [x])
        layer_norm(lns, x.t[:], [x], A1.t[:], B1.t[:], [A1, B1], hb.t[:], [hb], "s")
        transpose8(hb, hT, pbb[0])
        for blk in range(9):
            w = wb[blk % 2]
            n0 = blk * 512
            n1 = min(4352, n0 + 512)
            for c in range(8):
                k.dma("pool", w.t[:, c, 0:n1 - n0], w_in_v[:, c, n0:n1], [], [w.bs[c]])
            pp = pbank[blk % 2]
            for c in range(8):
                k.op("pe", lambda e, c=c, w=w, pp=pp, nn=n1 - n0: e.matmul(pp.t[:, 0:nn], lhsT=hT.t[:, c, :], rhs=w.t[:, c, 0:nn],
                                                                          start=(c == 0), stop=(c == 7)), [hT, w], [pp])
            k.op("act", lambda e, pp=pp, n0=n0, n1=n1: e.activation(out=pj.t[:, n0:n1], in_=pp.t[:, 0:n1 - n0], func=AF.Copy),
                 [pp], [pj])
        newkv = k.sb(st, "newkv", [128, 3, 512], F32)
        for gi in range(3):
            k.op("dve", lambda e, gi=gi: e.tensor_copy(out=newkv.t[:, gi, 0:256], in_=pj.t[:, C_KA + gi * 256:C_KA + (gi + 1) * 256]),
                 [pj], [newkv])
            k.op("dve", lambda e, gi=gi: e.tensor_copy(out=newkv.t[:, gi, 256:512], in_=pj.t[:, C_VA + gi * 256:C_VA + (gi + 1) * 256]),
                 [pj], [newkv])
            W = WINS[gi]
            cin, cout = I["c%d" % W], O["kvs%d" % W]
            k.dma("sp", cout[:, W - 1, :], newkv.t[0:16, gi, :], [newkv], [dram_out])
        oacc = k.sb(st, "oacc", [16, 4, 64], F32)
        zacc = k.sb(st, "zacc", [16, 4], F32)
        pr1 = k.sb(st, "pr1", [16, 12, 64], F32)
        ss = k.sb(st, "ss", [16, 12], F32)
        psf = k.sb(st, "psf", [16, 12], F32)
        qa_v = pj.t[0:16, C_QA:C_QA + 768].rearrange("p (g d) -> p g d", g=12)
        ka_v = pj.t[0:16, C_KA:C_KA + 768].rearrange("p (g d) -> p g d", g=12)
        va_v = pj.t[0:16, C_VA:C_VA + 768].rearrange("p (g d) -> p g d", g=12)
        k.op("dve", lambda e: e.tensor_tensor(out=pr1.t[:], in0=qa_v, in1=ka_v, op=ALU.mult), [pj], [pr1])
        k.op("dve", lambda e: e.tensor_reduce(out=ss.t[:], in_=pr1.t[:], axis=AX.X, op=ALU.add), [pr1], [ss])
        k.op("act", lambda e: e.activation(out=psf.t[:], in_=ss.t[:], func=AF.Exp, scale=0.125), [ss], [psf])
        k.op("dve", lambda e: e.tensor_tensor(out=pr1.t[:], in0=va_v, in1=psf.t[:].unsqueeze(2).to_broadcast([16, 12, 64]),
                                              op=ALU.mult), [pj, psf], [pr1])
        k.op("dve", lambda e: e.tensor_tensor(out=oacc.t[:], in0=pr1.t[:, 0:4, :], in1=pr1.t[:, 4:8, :], op=ALU.add), [pr1], [oacc])
        k.op("dve", lambda e: e.tensor_tensor(out=oacc.t[:], in0=oacc.t[:], in1=pr1.t[:, 8:12, :], op=ALU.add), [pr1, oacc], [oacc])
        k.op("dve", lambda e: e.tensor_tensor(out=zacc.t[:], in0=psf.t[:, 0:4], in1=psf.t[:, 4:8], op=ALU.add), [psf], [zacc])
        k.op("dve", lambda e: e.tensor_tensor(out=zacc.t[:], in0=zacc.t[:], in1=psf.t[:, 8:12], op=ALU.add), [psf, zacc], [zacc])
        Kt = [k.sb(st, "Kt%d" % i, [16, 16, 512], F32) for i in range(2)]
        prd = k.sb(st, "prd", [16, 16, 256], F32)
        scs = k.sb(st, "scs", [16, 64], F32)
        pex = k.sb(st, "pex", [16, 64], F32)
        red = k.sb(st, "red", [16, 256], F32)
        red4 = k.sb(st, "red4", [16, 4], F32)
        ci_ = 0
        for gi in range(3):
            W, dil = WINS[gi], DILS[gi]
            cin = I["c%d" % W]
            for kc in range(8):
                kt = Kt[ci_ % 2]
                ci_ += 1
                r0 = kc * 16 * dil
                k.dma("sp", kt.t[:], cin[:, r0:r0 + 15 * dil + 1:dil, :], [], [kt])
                qg = pj.t[0:16, C_QA + gi * 256:C_QA + (gi + 1) * 256]
                k.op("dve", lambda e, kt=kt, qg=qg: e.tensor_tensor(out=prd.t[:], in0=kt.t[:, :, 0:256],
                                                                    in1=qg.unsqueeze(1).to_broadcast([16, 16, 256]), op=ALU.mult),
                     [kt, pj], [prd])
                k.op("dve", lambda e: e.tensor_reduce(out=scs.t[:], in_=prd.t[:].rearrange("p k (h d) -> p (k h) d", h=4),
                                                      axis=AX.X, op=ALU.add), [prd], [scs])
                k.op("dve", lambda e, gi=gi, kc=kc: e.scalar_tensor_tensor(
                    out=scs.t[:], in0=scs.t[:], scalar=0.125, in1=bS.t[:, gi, kc * 16:(kc + 1) * 16, :].rearrange("p k h -> p (k h)"),
                    op0=ALU.mult, op1=ALU.add), [scs, bS], [scs])
                k.op("act", lambda e: e.activation(out=pex.t[:], in_=scs.t[:], func=AF.Exp), [scs], [pex])
                k.op("dve", lambda e: e.tensor_reduce(out=red4.t[:], in_=pex.t[:].rearrange("p (k h) -> p h k", h=4), axis=AX.X,
                                                      op=ALU.add), [pex], [red4])
                k.op("dve", lambda e: e.tensor_tensor(out=zacc.t[:], in0=zacc.t[:], in1=red4.t[:], op=ALU.add), [zacc, red4], [zacc])
                k.op("dve", lambda e, kt=kt: e.tensor_tensor(
                    out=prd.t[:].rearrange("p k (h d) -> p k h d", h=4),
                    in0=kt.t[:, :, 256:512].rearrange("p k (h d) -> p k h d", h=4),
                    in1=pex.t[:].rearrange("p (k h) -> p k h", h=4).unsqueeze(3).to_broadcast([16, 16, 4, 64]), op=ALU.mult),
                     [kt, pex], [prd])
                k.op("dve", lambda e: e.tensor_reduce(out=red.t[:], in_=prd.t[:].rearrange("p k n -> p n k"), axis=AX.X, op=ALU.add),
                     [prd], [red])
                k.op("dve", lambda e: e.tensor_tensor(out=oacc.t[:].rearrange("p h d -> p (h d)"),
                                                      in0=oacc.t[:].rearrange("p h d -> p (h d)"), in1=red.t[:], op=ALU.add),
                     [oacc, red], [oacc])
        oas = k.sb(st, "oas", [128, 4, 64], BF16)
        oasT = k.sb(st, "oasT", [64, 4, 128], BF16)
        k.op("pool", lambda e: e.memset(oas.t[:], 0.0), [], [oas])
        k.op("dve", lambda e: e.reciprocal(out=zacc.t[:], in_=zacc.t[:]), [zacc], [zacc])
        k.op("dve", lambda e: e.tensor_tensor(out=oas.t[0:16, :, :], in0=oacc.t[:], in1=zacc.t[:].unsqueeze(2).to_broadcast([16, 4, 64]),
                                              op=ALU.mult), [oacc, zacc], [oas])
        pq_ = pbb[1]
        for j in range(4):
            k.op("pe", lambda e, j=j: e.transpose(out=pq_.t[0:64, j * 128:(j + 1) * 128], in_=oas.t[:, j, :], identity=identb.t[:]),
                 [oas, identb], [pq_])
        k.op("act", lambda e: e.activation(out=oasT.t[:].rearrange("p j t -> p (j t)"), in_=pq_.t[0:64, 0:512], func=AF.Copy),
             [pq_], [oasT])
        for j in range(4):
            k.dma("sp", OAT.t[j, :, SEG:SEG + 128], oasT.t[:, j, :], [oasT], [OAT])
        glrT = k.sb(st, "glrT", [16, 128], BF16)
        ee = k.sb(st, "ee", [128, 512], F32)
        pM, pX = pbank[2], pbank[3]
        for c in range(8):
            k.op("pe", lambda e, c=c: e.matmul(pM.t[0:16, 0:128], lhsT=Wg.t[:, c, :], rhs=hT.t[:, c, :], start=(c == 0), stop=(c == 7)),
                 [hT, Wg], [pM])
        k.op("act", lambda e: e.activation(out=glrT.t[:], in_=pM.t[0:16, 0:128], func=AF.Copy), [pM], [glrT])
        k.op("pe", lambda e: e.matmul(pX.t[:, :], lhsT=glrT.t[:, :], rhs=wup.t[:, :], start=True, stop=False), [glrT, wup], [pX])
        k.op("pe", lambda e: e.matmul(pX.t[:, :], lhsT=onesb.t[0:1, :], rhs=bgl.t[0:1, :], start=False, stop=True), [onesb, bgl], [pX])
        k.op("act", lambda e: e.activation(out=ee.t[:], in_=pX.t[:], func=AF.Exp, scale=-1.0), [pX], [ee])
        k.op("act", lambda e: e.activation(out=ee.t[:], in_=ee.t[:], func=AF.Ln, bias=1.0), [ee], [ee])
        k.op("act", lambda e: e.activation(out=ee.t[:], in_=ee.t[:], func=AF.Exp, scale=-1.0 / 16), [ee], [ee])
        qkdT = k.sb(st, "qkdT", [128, 12, 128], F32)
        qsc = k.sb(st, "qsc", [128, 512], F32)
        k.op("dve", lambda e: e.tensor_scalar(out=qsc.t[:], in0=pj.t[:, C_QB:C_QB + 512], scalar1=128.0 ** -0.5, scalar2=None,
                                              op0=ALU.mult), [pj], [qsc])
        for h in range(4):
            for which, srcap, srcb in ((0, qsc.t[:, h * 128:(h + 1) * 128], qsc),
                                       (1, pj.t[:, C_KB + h * 128:C_KB + (h + 1) * 128], pj),
                                       (2, ee.t[:, h * 128:(h + 1) * 128], ee)):
                pt_ = pbank[4 + (h * 3 + which) % 2]
                k.op("pe", lambda e, pt_=pt_, srcap=srcap: e.transpose(out=pt_.t[:, 0:128], in_=srcap, identity=identf.t[:]),
                     [srcb, identf], [pt_])
                k.op("dve", lambda e, pt_=pt_, which=which, h=h: e.tensor_copy(out=qkdT.t[:, which * 4 + h, :], in_=pt_.t[:, 0:128]),
                     [pt_], [qkdT])
        VS = dscr("VS", [128, D], F32)
        k.dma("sp", VS.t[:, :], pj.t[:, C_VB:C_VB + D], [pj], [VS])
        ogs = k.sb(st, "ogs", [128, 4, 256], F32)
        k.op("pool", lambda e: e.memset(ogs.t[:], 0.0), [], [ogs])
        S0 = [k.sb(st, "S0%d" % i, [128, 4, 256], F32) for i in range(2)]
        vbc = [k.sb(st, "vbc%d" % i, [128, 4, 256], F32) for i in range(2)]
        for b in range(16):
            s0, vb_ = S0[b % 2], vbc[b % 2]
            k.dma("sp", s0.t[:], I["sgla"][b].rearrange("h k v -> k h v"), [], [s0])
            k.dma("sp", vb_.t[:].rearrange("p h v -> p (h v)"), bcast_rows(VS.t[b:b + 1, :], 128), [VS], [vb_])
            for h in range(4):
                k.op("dve", lambda e, s0=s0, h=h, b=b: e.tensor_scalar(out=s0.t[:, h, :], in0=s0.t[:, h, :],
                                                                       scalar1=qkdT.t[:, 8 + h, b:b + 1], scalar2=None, op0=ALU.mult),
                     [s0, qkdT], [s0])
                k.op("dve", lambda e, s0=s0, vb_=vb_, h=h, b=b: e.scalar_tensor_tensor(
                    out=s0.t[:, h, :], in0=vb_.t[:, h, :], scalar=qkdT.t[:, 4 + h, b:b + 1], in1=s0.t[:, h, :], op0=ALU.mult,
                    op1=ALU.add), [s0, vb_, qkdT], [s0])
            k.dma("sp", O["glas"][b].rearrange("h k v -> k h v"), s0.t[:], [s0], [dram_out])
            for h in range(4):
                po = pbank[h // 2]
                k.op("pe", lambda e, po=po, h=h, s0=s0: e.matmul(po.t[:, (h % 2) * 256:(h % 2) * 256 + 256], lhsT=qkdT.t[:, h, :],
                                                                 rhs=s0.t[:, h, :], start=True, stop=True), [qkdT, s0], [po])
            for hf in range(2):
                k.op("dve", lambda e, hf=hf, b=b: e.scalar_tensor_tensor(
                    out=ogs.t[:, hf * 2:hf * 2 + 2, :].rearrange("p h v -> p (h v)"), in0=pbank[hf].t[:, :],
                    scalar=identf.t[:, b:b + 1], in1=ogs.t[:, hf * 2:hf * 2 + 2, :].rearrange("p h v -> p (h v)"),
                    op0=ALU.mult, op1=ALU.add), [pbank[hf], identf, ogs], [ogs])
        ssq = k.sb(st, "ssq", [128, 8], F32)
        obt = k.sb(st, "obt", [128, D], BF16)
        for h in range(4):
            k.op("dve", lambda e, h=h: e.scalar_tensor_tensor(out=ee.t[:, 0:256], in0=ogs.t[:, h, :], scalar=1.0, in1=ogs.t[:, h, :],
                                                              op0=ALU.mult, op1=ALU.mult, accum_out=ssq.t[:, h:h + 1]), [ogs], [ee, ssq])
        k.op("dve", lambda e: e.tensor_scalar(out=ssq.t[:, 0:4], in0=ssq.t[:, 0:4], scalar1=1.0 / 256, scalar2=LN_EPS, op0=ALU.mult,
                                              op1=ALU.add), [ssq], [ssq])
        k.op("act", lambda e: e.activation(out=ssq.t[:, 0:4], in_=ssq.t[:, 0:4], func=AF.Sqrt), [ssq], [ssq])
        k.op("dve", lambda e: e.reciprocal(out=ssq.t[:, 4:8], in_=ssq.t[:, 0:4]), [ssq], [ssq])
        for h in range(4):
            k.op("dve", lambda e, h=h: e.scalar_tensor_tensor(out=obt.t[:, h * 256:(h + 1) * 256], in0=ogs.t[:, h, :],
                                                              scalar=ssq.t[:, 4 + h:5 + h], in1=gnw.t[:, h * 256:(h + 1) * 256],
                                                              op0=ALU.mult, op1=ALU.mult), [ogs, ssq, gnw], [obt])
        k.dma("sp", OB.t[SEG:SEG + 128, :], obt.t[:], [obt], [OB])
        k.barrier()
        st.close()

    def phaseB():
        st = ExitStack()
        A1 = k.sb(st, "A1", [128, D], F32)
        B1 = k.sb(st, "B1", [128, D], F32)
        G1 = k.sb(st, "G1", [128, D], F32)
        l1w = k.sb(st, "l1w", [128, D], F32)
        l1b = k.sb(st, "l1b", [128, D], F32)
        k.dma("sp", l1w.t[:], bcast_rows(I["ln1_w"], 128), [], [l1w])
        k.dma("sp", l1b.t[:], bcast_rows(I["ln1_b"], 128), [], [l1b])
        Wrb = load_w(st, "Wrb", w_in_v, C_RB, 1024)
        Wga = load_w(st, "Wga", w_in_v, C_GA, 1024)
        Wgb = load_w(st, "Wgb", w_in_v, C_GB, 1024)
        Wbb = load_w(st, "Wbb", I["w_br_b"].rearrange("(c p) n -> p c n", p=128), 0, 1024)
        Wo = load_w(st, "Wo", I["w_out"].rearrange("(c p) n -> p c n", p=128), 0, 1024)
        Wba = k.sb(st, "Wba", [64, 4, D], BF16, n=8)
        for j in range(4):
            for hf in range(2):
                k.dma("pool", Wba.t[:, j, hf * 512:(hf + 1) * 512], I["w_br_a"][j * 64:(j + 1) * 64, hf * 512:(hf + 1) * 512], [], [Wba.bs[j * 2 + hf]])
        lns = ln_scratch(st)
        xt = [k.sb(st, "xt%d" % i, [128, D], F32) for i in range(3)]
        hbs = [k.sb(st, "hb%d" % i, [128, D], BF16) for i in range(2)]
        hTs_ = [k.sb(st, "hT%d" % i, [128, 8, 128], BF16) for i in range(2)]
        gats = [[k.sb(st, "gat%d_%d" % (pr, i), [128, D], BF16) for i in range(3)] for pr in range(2)]
        obls = [k.sb(st, "obl%d" % i, [128, D], BF16) for i in range(2)]
        oaTs = [k.sb(st, "oaT%d" % i, [64, 4, 128], BF16) for i in range(2)]
        obT = k.sb(st, "obT", [128, 8, 128], BF16)
        t1 = k.sb(st, "t1", [128, D], F32)
        t2 = k.sb(st, "t2", [128, D], F32)
        mg = k.sb(st, "mg", [128, D], BF16)
        x1o = k.sb(st, "x1o", [128, D], F32)

        def f1(ti):
            smp = ti == NT - 1
            if ti == 0 or smp:
                md = modS if smp else modP
                k.dma("sp", A1.t[:], md.t[:, 1 * D:2 * D], [md], [A1])
                k.dma("sp", B1.t[:], md.t[:, 0:D], [md], [B1])
            x, hb = xt[ti % 3], hbs[ti % 2]
            k.dma("sp", x.t[:], I["xs"] if smp else I["xo"][ti * 128:(ti + 1) * 128, :], [], [x])
            yield from layer_norm_g(lns, x.t[:], [x], A1.t[:], B1.t[:], [A1, B1], hb.t[:], [hb], "b")
            yield

        def f2(ti):
            hb, hT, gat, obl, oaT = hbs[ti % 2], hTs_[ti % 2], gats[ti % 2], obls[ti % 2], oaTs[ti % 2]
            k.dma("sp", obl.t[:], OB.t[ti * 128:(ti + 1) * 128, :], [OB], [obl])
            k.dma("sp", oaT.t[:], OAT.t[:, :, ti * 128:(ti + 1) * 128].rearrange("j d t -> d j t"), [OAT], [oaT])
            transpose8(hb, hT, pbb[0])
            yield
            for wi, (W_, fn) in enumerate(((Wrb, AF.Silu), (Wga, AF.Sigmoid), (Wgb, AF.Sigmoid))):
                for hf in range(2):
                    pp = pbank[hf]
                    for c in range(8):
                        k.op("pe", lambda e, c=c, W_=W_, hf=hf, pp=pp: e.matmul(pp.t[:, :], lhsT=hT.t[:, c, :],
                                                                              rhs=W_.t[:, c, hf * 512:(hf + 1) * 512],
                                                                              start=(c == 0), stop=(c == 7)), [hT, W_], [pp])
                    k.op("act", lambda e, wi=wi, hf=hf, pp=pp, fn=fn: e.activation(out=gat[wi].t[:, hf * 512:(hf + 1) * 512],
                                                                                 in_=pp.t[:], func=fn), [pp], [gat[wi]])
                    yield

        def back(ti):
            smp = ti == NT - 1
            if ti == 0 or smp:
                md = modS if smp else modP
                k.dma("sp", G1.t[:], md.t[:, 2 * D:3 * D], [md], [G1])
            x, gat, obl, oaT = xt[ti % 3], gats[ti % 2], obls[ti % 2], oaTs[ti % 2]
            k.op("dve", lambda e: e.tensor_tensor(out=obl.t[:], in0=obl.t[:], in1=gat[0].t[:], op=ALU.mult), [obl, gat[0]], [obl])
            transpose8(obl, obT, pbb[1])
            yield
            for hf in range(2):
                pa, pb_ = pbank[2 + hf], pbank[4 + hf]
                for j in range(4):
                    k.op("pe", lambda e, j=j, hf=hf, pa=pa: e.matmul(pa.t[:, :], lhsT=oaT.t[:, j, :], rhs=Wba.t[:, j, hf * 512:(hf + 1) * 512],
                                                                     start=(j == 0), stop=(j == 3)), [oaT, Wba], [pa])
                for c in range(8):
                    k.op("pe", lambda e, c=c, hf=hf, pb_=pb_: e.matmul(pb_.t[:, :], lhsT=obT.t[:, c, :], rhs=Wbb.t[:, c, hf * 512:(hf + 1) * 512],
                                                                       start=(c == 0), stop=(c == 7)), [obT, Wbb], [pb_])
                hs = slice(hf * 512, (hf + 1) * 512)
                k.op("dve", lambda e, pa=pa, hs=hs: e.tensor_tensor(out=t1.t[:, hs], in0=pa.t[:], in1=gat[1].t[:, hs], op=ALU.mult),
                     [pa, gat[1]], [t1])
                k.op("dve", lambda e, pb_=pb_, hs=hs: e.tensor_tensor(out=t2.t[:, hs], in0=pb_.t[:], in1=gat[2].t[:, hs], op=ALU.mult),
                     [pb_, gat[2]], [t2])
                yield
            k.op("pool", lambda e: e.tensor_tensor(out=mg.t[:], in0=t1.t[:], in1=t2.t[:], op=ALU.add), [t1, t2], [mg])
            transpose8(mg, obT, pbb[1])
            yield
            for hf in range(2):
                pp = pbank[2 + hf]
                hs = slice(hf * 512, (hf + 1) * 512)
                for c in range(8):
                    k.op("pe", lambda e, c=c, hf=hf, pp=pp: e.matmul(pp.t[:, :], lhsT=obT.t[:, c, :], rhs=Wo.t[:, c, hf * 512:(hf + 1) * 512],
                                                                     start=(c == 0), stop=(c == 7)), [obT, Wo], [pp])
                k.op("dve", lambda e, pp=pp, hs=hs: e.tensor_tensor(out=t1.t[:, hs], in0=pp.t[:], in1=G1.t[:, hs], op=ALU.mult),
                     [pp, G1], [t1])
                yield
            k.op("dve", lambda e, x=x: e.scalar_tensor_tensor(out=t2.t[:], in0=x.t[:], scalar=DN_ALPHA, in1=t1.t[:], op0=ALU.mult,
                                                              op1=ALU.add), [x, t1], [t2])
            yield from layer_norm_g(lns, t2.t[:], [t2], l1w.t[:], l1b.t[:], [l1w, l1b], x1o.t[:], [x1o], "b1")
            k.dma("sp", X1.t[ti * 128:(ti + 1) * 128, :], x1o.t[:], [x1o], [X1])
            yield

        def interleave(gens):
            gens = [g_ for g_ in gens if g_ is not None]
            while gens:
                for g_ in list(gens):
                    try:
                        next(g_)
                    except StopIteration:
                        gens.remove(g_)

        interleave([f1(0)])
        interleave([f1(1), f2(0)])
        for ti in range(NT):
            interleave([f1(ti + 2) if ti + 2 < NT else None, f2(ti + 1) if ti + 1 < NT else None, back(ti)])
        k.barrier()
        st.close()

    def phaseC():
        st = ExitStack()
        A2 = k.sb(st, "A2", [128, D], F32)
        B2 = k.sb(st, "B2", [128, D], F32)
        G2 = k.sb(st, "G2", [128, D], F32)
        l2w = k.sb(st, "l2w", [128, D], F32)
        l2b = k.sb(st, "l2b", [128, D], F32)
        k.dma("sp", l2w.t[:], bcast_rows(I["ln2_w"], 128), [], [l2w])
        k.dma("sp", l2b.t[:], bcast_rows(I["ln2_b"], 128), [], [l2b])
        Wpq = load_w(st, "Wpq", I["w_pq"].rearrange("(c p) n -> p c n", p=128), 0, 2048)
        st_kk = ExitStack()
        kkT = k.sb(st, "kkT", [128, 16, 128], BF16)
        kk = k.sb(st_kk, "kk", [128, 16, 128], BF16)
        k.dma("pool", kk.t[:, 0:8, :], I["pk1"].rearrange("h k d -> k h d"), [], [kk])
        k.dma("pool", kk.t[:, 8:16, :], I["pk2"].rearrange("h k d -> k h d"), [], [kk])
        for which in range(2):
            for h in range(8):
                pq_ = pbb[(which * 8 + h) % 2]
                k.op("pe", lambda e, pq_=pq_, which=which, h=h: e.transpose(out=pq_.t[:, 0:128], in_=kk.t[:, which * 8 + h, :],
                                                                            identity=identb.t[:]), [kk, identb], [pq_])
                k.op("dve", lambda e, pq_=pq_, which=which, h=h: e.tensor_copy(out=kkT.t[:, 2 * h + which, :], in_=pq_.t[:, 0:128]),
                     [pq_], [kkT])
        k.barrier()
        st_kk.close()
        io16 = k.sb(st, "io16", [128, 16], F32)
        thr16 = k.sb(st, "thr16", [128, 16], F32)
        k.dma("sp", io16.t[:], I["iota16"], [], [io16])
        k.op("dve", lambda e: e.tensor_scalar(out=thr16.t[:], in0=io16.t[:], scalar1=16.0, scalar2=None, op0=ALU.mult), [io16], [thr16])
        lns = ln_scratch(st)
        x1 = [k.sb(st, "x1_%d" % i, [128, D], F32) for i in range(2)]
        h2s = [k.sb(st, "h2_%d" % i, [128, D], F32) for i in range(2)]
        eis = [k.sb(st, "ei%d" % i, [128, 128], I32) for i in range(2)]
        gws = [k.sb(st, "gw%d" % i, [128, 8, 16], F32) for i in range(2)]
        h2b = k.sb(st, "h2b", [128, D], BF16)
        h2T = k.sb(st, "h2T", [128, 8, 128], BF16)
        qvT = k.sb(st, "qvT", [128, 16, 128], BF16)
        sA = k.sb(st, "sA", [128, 2048], F32)
        sB = k.sb(st, "sB", [128, 2048], F32)
        v16 = k.sb(st, "v16", [128, 16, 16], F32)
        i16u = k.sb(st, "i16u", [128, 16, 16], U32)
        i16f = k.sb(st, "i16f", [128, 16, 16], F32)
        di1 = k.sb(st, "di1", [128, 8, 16], F32)
        sc16 = k.sb(st, "sc16", [128, 8, 16], F32)
        ciu = k.sb(st, "ciu", [128, 8, 16], U32)
        cif = k.sb(st, "cif", [128, 128], F32)
        a1 = k.sb(st, "a1", [128, 128], F32)
        i1s = k.sb(st, "i1s", [128, 128], F32)
        bsel = k.sb(st, "bsel", [128, 128], F32)
        i2s = k.sb(st, "i2s", [128, 128], F32)
        zs = k.sb(st, "zs", [128, 8], F32)
        dots = k.sb(st, "dots", [128, 128], F32)
        actv = k.sb(st, "actv", [128, 128], F32)
        gco = k.sb(st, "gco", [128, 128], F32)
        GS = 4
        NG = 16
        gb_ = [k.sb(st, "gb%d" % i, [128, 2, D], BF16) for i in range(NG)]
        dg = [k.sb(st, "dg%d" % i, [128, 128], BF16) for i in range(8)]
        junk = k.sb(st, "junkc", [128, D], BF16)
        ff = k.sb(st, "ff", [128, D], F32)
        yt = k.sb(st, "yt", [128, D], F32)
        pF = [pbank[4], pbank[5]]

        def top16(vals, vals2, vout, iout, n, w):
            vv = vals.t[:].rearrange("p (c k) -> p c k", k=w)
            v2 = vals2.t[:].rearrange("p (c k) -> p c k", k=w)
            for c_ in range(n):
                k.op("dve", lambda e, c_=c_: e.max(out=vout.t[:, c_, 0:8], in_=vv[:, c_, :]), [vals], [vout])
                k.op("dve", lambda e, c_=c_: e.max_index(out=iout.t[:, c_, 0:8], in_max=vout.t[:, c_, 0:8], in_values=vv[:, c_, :]),
                     [vals, vout], [iout])
                k.op("dve", lambda e, c_=c_: e.match_replace(out=v2[:, c_, :], in_to_replace=vout.t[:, c_, 0:8],
                                                             in_values=vv[:, c_, :], imm_value=-1e30), [vals, vout], [vals2])
                k.op("dve", lambda e, c_=c_: e.max(out=vout.t[:, c_, 8:16], in_=v2[:, c_, :]), [vals2], [vout])
                k.op("dve", lambda e, c_=c_: e.max_index(out=iout.t[:, c_, 8:16], in_max=vout.t[:, c_, 8:16],
                                                         in_values=v2[:, c_, :]), [vals2, vout], [iout])
                if c_ % 2 == 1:
                    yield

        def front(ti):
            smp = ti == NT - 1
            x, h2, ei, gw = x1[ti % 2], h2s[ti % 2], eis[ti % 2], gws[ti % 2]
            if ti == 0 or smp:
                md = modS if smp else modP
                k.dma("sp", A2.t[:], md.t[:, 4 * D:5 * D], [md], [A2])
                k.dma("sp", B2.t[:], md.t[:, 3 * D:4 * D], [md], [B2])
            k.dma("sp", x.t[:], X1.t[ti * 128:(ti + 1) * 128, :], [X1], [x])
            layer_norm(lns, x.t[:], [x], A2.t[:], B2.t[:], [A2, B2], h2.t[:], [h2], "c")
            yield
            k.op("act", lambda e: e.activation(out=h2b.t[:], in_=h2.t[:], func=AF.Copy), [h2], [h2b])
            transpose8(h2b, h2T, pbb[0])
            yield
            for q4 in range(4):
                pp = pbank[q4 % 2]
                for cc in range(4):
                    ch = q4 * 4 + cc
                    for c in range(8):
                        k.op("pe", lambda e, c=c, cc=cc, ch=ch, pp=pp: e.matmul(pp.t[:, cc * 128:(cc + 1) * 128],
                                                                              lhsT=Wpq.t[:, c, ch * 128:(ch + 1) * 128], rhs=h2T.t[:, c, :],
                                                                              start=(c == 0), stop=(c == 7)), [Wpq, h2T], [pp])
                k.op("act", lambda e, q4=q4, pp=pp: e.activation(out=qvT.t[:, q4 * 4:(q4 + 1) * 4, :].rearrange("p c t -> p (c t)"),
                                                                 in_=pp.t[:], func=AF.Copy), [pp], [qvT])
                yield
            for q4 in range(4):
                pp = pbank[2 + q4 % 2]
                for cc in range(4):
                    ch = q4 * 4 + cc
                    k.op("pe", lambda e, cc=cc, ch=ch, pp=pp: e.matmul(pp.t[:, cc * 128:(cc + 1) * 128], lhsT=qvT.t[:, ch, :],
                                                                       rhs=kkT.t[:, ch, :], start=True, stop=True), [qvT, kkT], [pp])
                k.op("act", lambda e, q4=q4, pp=pp: e.activation(out=sA.t[:, q4 * 512:(q4 + 1) * 512], in_=pp.t[:], func=AF.Copy), [pp], [sA])
            yield
            yield from top16(sA, sB, v16, i16u, 16, 128)
            k.op("dve", lambda e: e.tensor_copy(out=i16f.t[:], in_=i16u.t[:]), [i16u], [i16f])
            v4 = v16.t[:].rearrange("p (h w) k -> p h w k", w=2)
            i4 = i16f.t[:].rearrange("p (h w) k -> p h w k", w=2)
            k.op("dve", lambda e: e.tensor_tensor(out=sB.t[:].rearrange("p (h a b) -> p h a b", h=8, a=16),
                                                  in0=v4[:, :, 0, :].unsqueeze(3).to_broadcast([128, 8, 16, 16]),
                                                  in1=v4[:, :, 1, :].unsqueeze(2).to_broadcast([128, 8, 16, 16]), op=ALU.add),
                 [v16], [sB])
            yield
            yield from top16(sB, sA, sc16, ciu, 8, 256)
            k.op("dve", lambda e: e.tensor_copy(out=cif.t[:], in_=ciu.t[:].rearrange("p h k -> p (h k)")), [ciu], [cif])
            k.op("dve", lambda e: e.tensor_copy(out=di1.t[:, :, 0:1], in_=i4[:, :, 0, 0:1]), [i16f], [di1])
            k.op("dve", lambda e: e.tensor_tensor(out=di1.t[:, :, 1:16], in0=i4[:, :, 0, 1:16], in1=i4[:, :, 0, 0:15], op=ALU.subtract),
                 [i16f], [di1])
            bigv = sA.t[:].rearrange("p (h k j) -> p h k j", h=8, k=16)
            bigf = sA.t[:].rearrange("p (m j) -> p m j", j=16)
            cif4 = cif.t[:].rearrange("p (h k) -> p h k", h=8).unsqueeze(3).to_broadcast([128, 8, 16, 16])
            k.op("dve", lambda e: e.tensor_tensor(out=bigv, in0=cif4, in1=thr16.t[:].unsqueeze(1).unsqueeze(1).to_broadcast([128, 8, 16, 16]),
                                                  op=ALU.is_ge), [cif, thr16], [sA])
            k.op("dve", lambda e: e.tensor_reduce(out=a1.t[:], in_=bigf, axis=AX.X, op=ALU.add), [sA], [a1])
            yield
            k.op("dve", lambda e: e.tensor_tensor(out=bigv, in0=bigv, in1=di1.t[:].unsqueeze(2).to_broadcast([128, 8, 16, 16]), op=ALU.mult),
                 [sA, di1], [sA])
            k.op("dve", lambda e: e.tensor_reduce(out=i1s.t[:], in_=bigf, axis=AX.X, op=ALU.add), [sA], [i1s])
            k.op("dve", lambda e: e.scalar_tensor_tensor(out=bsel.t[:], in0=a1.t[:], scalar=-16.0, in1=cif.t[:], op0=ALU.mult, op1=ALU.add),
                 [a1, cif], [bsel])
            k.op("dve", lambda e: e.tensor_scalar_add(out=bsel.t[:], in0=bsel.t[:], scalar1=16.0), [bsel], [bsel])
            yield
            bs4 = bsel.t[:].rearrange("p (h k) -> p h k", h=8).unsqueeze(3).to_broadcast([128, 8, 16, 16])
            k.op("dve", lambda e: e.tensor_tensor(out=bigv, in0=bs4, in1=io16.t[:].unsqueeze(1).unsqueeze(1).to_broadcast([128, 8, 16, 16]),
                                                  op=ALU.is_equal), [bsel, io16], [sA])
            k.op("dve", lambda e: e.tensor_tensor(out=bigv, in0=bigv, in1=i4[:, :, 1, :].unsqueeze(2).to_broadcast([128, 8, 16, 16]),
                                                  op=ALU.mult), [sA, i16f], [sA])
            k.op("dve", lambda e: e.tensor_reduce(out=i2s.t[:], in_=bigf, axis=AX.X, op=ALU.add), [sA], [i2s])
            yield
            k.op("dve", lambda e: e.scalar_tensor_tensor(out=i1s.t[:], in0=i1s.t[:], scalar=128.0, in1=i2s.t[:], op0=ALU.mult, op1=ALU.add),
                 [i1s, i2s], [i1s])
            k.op("dve", lambda e: e.tensor_copy(out=ei.t[:], in_=i1s.t[:]), [i1s], [ei])
            k.op("dve", lambda e: e.tensor_tensor(out=gw.t[:], in0=sc16.t[:], in1=sc16.t[:, :, 0:1].to_broadcast([128, 8, 16]), op=ALU.subtract),
                 [sc16], [gw])
            k.op("act", lambda e: e.activation(out=gw.t[:], in_=gw.t[:], func=AF.Exp), [gw], [gw])
            k.op("dve", lambda e: e.tensor_reduce(out=zs.t[:], in_=gw.t[:], axis=AX.X, op=ALU.add), [gw], [zs])
            k.op("dve", lambda e: e.reciprocal(out=zs.t[:], in_=zs.t[:]), [zs], [zs])
            k.op("dve", lambda e: e.tensor_tensor(out=gw.t[:], in0=gw.t[:], in1=zs.t[:].unsqueeze(2).to_broadcast([128, 8, 16]), op=ALU.mult),
                 [gw, zs], [gw])
            yield

        def drain(gen):
            if gen is not None:
                for _ in gen:
                    pass

        gcnt = 0
        dcnt = 0
        drain(front(0))
        for ti in range(NT):
            smp = ti == NT - 1
            x, h2, ei, gw = x1[ti % 2], h2s[ti % 2], eis[ti % 2], gws[ti % 2]
            nxt = front(ti + 1) if ti + 1 < NT else None
            if ti == 0 or smp:
                md = modS if smp else modP
                k.dma("sp", G2.t[:], md.t[:, 5 * D:6 * D], [md], [G2])
            ngrp = 128 // GS
            tiles = {}

            def tail(gi_):
                nonlocal dcnt
                cs_ = slice(gi_ * GS, (gi_ + 1) * GS)
                k.op("act", lambda e: e.activation(out=actv.t[:, cs_], in_=dots.t[:, cs_], func=AF.Gelu), [dots], [actv])
                k.op("dve", lambda e: e.tensor_tensor(out=gco.t[:, cs_], in0=actv.t[:, cs_], in1=gw.t[:].rearrange("p h k -> p (h k)")[:, cs_],
                                                      op=ALU.mult), [actv, gw], [gco])
                for c_ in range(gi_ * GS, (gi_ + 1) * GS):
                    gt = tiles.pop(c_)
                    d_ = dg[dcnt % 8]
                    dcnt += 1
                    k.op("act", lambda e, d_=d_, c_=c_: e.activation(out=d_.t[:], in_=identb.t[:], func=AF.Identity, scale=gco.t[:, c_:c_ + 1]),
                         [identb, gco], [d_])
                    for hf in range(2):
                        k.op("pe", lambda e, d_=d_, gt=gt, hf=hf, c_=c_: e.matmul(pF[hf].t[:, :], lhsT=d_.t[:], rhs=gt.t[:, 1, hf * 512:(hf + 1) * 512],
                                                                               start=(c_ == 0), stop=(c_ == 127)), [d_, gt], [pF[hf]])

            for gi_ in range(ngrp):
                for c_ in range(gi_ * GS, (gi_ + 1) * GS):
                    gt = gb_[gcnt % NG]
                    gcnt += 1
                    tiles[c_] = gt
                    k.dma("pool", gt.t[:].rearrange("p a n -> p (a n)"), PUV.t, [ei] + PUV.bs, [gt], indirect=ei.t[:, c_:c_ + 1])
                    k.op("dve", lambda e, gt=gt, c_=c_: e.scalar_tensor_tensor(out=junk.t[:], in0=gt.t[:, 0, :], scalar=1.0, in1=h2.t[:],
                                                                               op0=ALU.mult, op1=ALU.mult, accum_out=dots.t[:, c_:c_ + 1]),
                         [gt, h2], [junk, dots])
                if gi_ > 0:
                    tail(gi_ - 1)
                if nxt is not None:
                    next(nxt, None)
            tail(ngrp - 1)
            drain(nxt)
            for hf in range(2):
                hs = slice(hf * 512, (hf + 1) * 512)
                k.op("dve", lambda e, hf=hf, hs=hs: e.tensor_tensor(out=ff.t[:, hs], in0=pF[hf].t[:], in1=G2.t[:, hs], op=ALU.mult),
                     [pF[hf], G2], [ff])
            k.op("dve", lambda e, x=x: e.scalar_tensor_tensor(out=ff.t[:], in0=x.t[:], scalar=DN_ALPHA, in1=ff.t[:], op0=ALU.mult, op1=ALU.add),
                 [x, ff], [ff])
            layer_norm(lns, ff.t[:], [ff], l2w.t[:], l2b.t[:], [l2w, l2b], yt.t[:], [yt], "c2")
            if smp:
                k.dma("sp", O["ys"], yt.t[0:16, :], [yt], [dram_out])
            else:
                k.dma("sp", O["yo"][ti * 128:(ti + 1) * 128, :], yt.t[:], [yt], [dram_out])
        k.barrier()
        st.close()

    phase0()
    phaseS()
    phaseA()
    phaseG()
    phaseB()
    phaseC()
    k.barrier(final=True)
    g.close()
    k.es.close()
    return nc


_NC = None


def _consts():
    identf = np.eye(128, dtype=np.float32)
    s = np.arange(128)[:, None]
    t = np.arange(128)[None, :]
    tri = (s <= t).astype(np.float32)
    tris = (s > t).astype(np.float32)
    slopes = np.exp2(-8.0 * np.arange(1, 13, dtype=np.float32) / 12).astype(np.float32)
    biasT = np.zeros((128, 12, 256), np.float32)
    ki = np.arange(128)[:, None]
    qi = np.arange(128)[None, :]
    for gi, dil in enumerate(DILS):
        for j in range(4):
            sl = slopes[gi * 4 + j]
            st_prev = qi + 128 - ki
            st_cur = qi - ki
            bp = np.where(st_prev <= 128, -sl * st_prev * dil, -30000.0)
            bc = np.where(st_cur >= 0, -sl * st_cur * dil, -30000.0)
            biasT[:, gi * 4 + j, 0:128] = bp
            biasT[:, gi * 4 + j, 128:256] = bc
    iota16 = np.tile(np.arange(16, dtype=np.float32)[None, :], (128, 1))
    biasS = np.zeros((3, 128, 4), np.float32)
    for gi, dil in enumerate(DILS):
        for j in range(4):
            biasS[gi, :, j] = -slopes[gi * 4 + j] * (128 - np.arange(128)) * dil
    return dict(identf=identf, tri=tri, tris=tris, biasT=biasT, iota16=iota16, biasS=biasS.reshape(1, -1))


def kernel(x_prompt, x_sample, c_prompt, c_sample, cache_kv_w128, cache_kv_w512, cache_kv_w2048, state_gla,
           w_ada, b_ada, w_in, w_gla_up, b_gla, gla_norm_w, w_br_a, w_br_b, w_out, ln1_w, ln1_b,
           w_pq, peer_k1, peer_k2, peer_u, peer_v, ln2_w, ln2_b):
    global _NC
    f = lambda a: np.ascontiguousarray(np.asarray(a, dtype=np.float32))
    if _NC is None:
        _NC = build()
    nc = _NC
    cst = _consts()
    shared = dict(w_ada=f(w_ada[0]), b_ada=f(b_ada), w_in=f(w_in[0]), w_up=f(w_gla_up[0]), b_gla=f(b_gla),
                  gnw=f(gla_norm_w), w_br_a=f(w_br_a[0]), w_br_b=f(w_br_b[0]), w_out=f(w_out[0]), ln1_w=f(ln1_w),
                  ln1_b=f(ln1_b), w_pq=f(w_pq[0]), pk1=f(peer_k1[0]), pk2=f(peer_k2[0]), pu=f(peer_u[0]),
                  pv=f(peer_v[0]), ln2_w=f(ln2_w), ln2_b=f(ln2_b), **cst)
    xpr = f(x_prompt)
    in_maps = []
    for c in range(8):
        b, s = c // 4, c % 4
        xp = np.zeros((NPRE, D), np.float32)
        if s > 0:
            xp[NPRE - s * SEG:] = xpr[b, :s * SEG]
        sq = slice(c * 16, (c + 1) * 16)
        cS = np.zeros((128, D), np.float32)
        cS[:16] = c_sample[sq]
        xs = np.zeros((128, D), np.float32)
        xs[:16] = x_sample[sq, 0]
        m = dict(shared)
        m.update(xo=np.ascontiguousarray(xpr[b, s * SEG:(s + 1) * SEG]), xp=xp,
                 segf=np.full((128, 1), float(s), np.float32), cP=f(c_prompt[b:b + 1]), cS=cS, xs=xs,
                 c128=f(cache_kv_w128[0, sq]).reshape(16, 128, 512), c512=f(cache_kv_w512[0, sq]).reshape(16, 512, 512),
                 c2048=f(cache_kv_w2048[0, sq]).reshape(16, 2048, 512), sgla=f(state_gla[0, sq]))
        in_maps.append(m)
    res = run_bass_kernel_spmd(nc, in_maps, core_ids=list(range(8))).results
    yp = np.stack([np.concatenate([res[b * 4 + s]["yo"] for s in range(4)], 0) for b in range(2)])
    ys = np.concatenate([res[c]["ys"] for c in range(8)], 0).reshape(128, 1, D)
    kvp = [np.stack([res[b * 4 + 3]["kvp%d" % w].reshape(w, 2, 4, 64) for b in range(2)])[None] for w in WINS]
    glap = np.stack([res[b * 4 + 3]["glap"] for b in range(2)])[None]
    kvs = [np.concatenate([res[c]["kvs%d" % w] for c in range(8)], 0).reshape(128, w, 2, 4, 64)[None] for w in WINS]
    glas = np.concatenate([res[c]["glas"] for c in range(8)], 0)[None]
    return (yp, ys, kvp[0], kvp[1], kvp[2], glap, kvs[0], kvs[1], kvs[2], glas)
```

```python
import numpy as np
from contextlib import ExitStack
import concourse.bass as bass
import concourse.mybir as mybir
from concourse.bass_utils import run_bass_kernel_spmd

F32 = mybir.dt.float32
BF16 = mybir.dt.bfloat16
I32 = mybir.dt.int32
U32 = mybir.dt.uint32
AF = mybir.ActivationFunctionType
ALU = mybir.AluOpType
AX = mybir.AxisListType

D = 1024
SEG = 4096
NPRE = 12288
NDS = 72
DN_ALPHA = 2.0 ** 0.25
LN_EPS = 1e-5
C_QA, C_KA, C_VA, C_QB, C_KB, C_VB, C_RB, C_GLR, C_GA, C_GB = 0, 768, 1536, 2304, 2816, 3328, 4352, 5376, 5392, 6416
DILS = (1, 4, 16)
WINS = (128, 512, 2048)


class Buf:
    __slots__ = ("w", "r")

    def __init__(self):
        self.w = None
        self.r = {}


class T:
    def __init__(self, t, n=1):
        self.t = t
        self.bs = [Buf() for _ in range(n)]
        self.b = self.bs[0]


class KB:
    def __init__(self, nc):
        self.nc = nc
        self.E = {"pe": nc.tensor, "act": nc.scalar, "dve": nc.vector, "pool": nc.gpsimd, "sp": nc.sync}
        self.es = ExitStack()
        self.semobj = {}
        self.cnt = {}
        self.known = {}
        for n in self.E:
            self.semobj["e:" + n] = self.es.enter_context(nc.semaphore("s_" + n))
            self.cnt[n] = 0
            self.known[n] = {}
        for j in range(NDS):
            self.semobj["d:%d" % j] = self.es.enter_context(nc.semaphore("d%d" % j))
        self.duse = [0] * NDS
        self.drr = 0
        self.uid = 0

    def sb(self, st, name, shape, dt, n=1):
        self.uid += 1
        return T(st.enter_context(self.nc.sbuf_tensor("%s_%d" % (name, self.uid), shape, dt)), n)

    def ps(self, st, name, shape, dt, n=1):
        self.uid += 1
        return T(st.enter_context(self.nc.psum_tensor("%s_%d" % (name, self.uid), shape, dt)), n)

    def _wait(self, en, deps):
        need = {}
        for kk, v in deps:
            if v > need.get(kk, 0):
                need[kk] = v
        kn = self.known[en]
        for kk, v in need.items():
            if kn.get(kk, 0) >= v:
                continue
            self.E[en].wait_ge(self.semobj[kk], v)
            kn[kk] = v

    def _deps(self, en, reads, writes, is_dma):
        own = None if is_dma else "e:" + en
        deps = []
        for b in reads:
            if b.w is not None:
                deps.append(b.w)
        for b in writes:
            if b.w is not None and b.w[0] != own:
                deps.append(b.w)
            for kk, v in b.r.items():
                if kk != own:
                    deps.append((kk, v))
        return deps

    def _commit(self, ev, reads, writes):
        for b in reads:
            if ev[1] > b.r.get(ev[0], 0):
                b.r[ev[0]] = ev[1]
        for b in writes:
            b.w = ev
            b.r = {}

    @staticmethod
    def _flat(items):
        out = []
        for x in items:
            if isinstance(x, T):
                out.extend(x.bs)
            else:
                out.append(x)
        return out

    def op(self, en, fn, reads=(), writes=()):
        reads = self._flat(reads)
        writes = self._flat(writes)
        self._wait(en, self._deps(en, reads, writes, False))
        self.cnt[en] += 1
        ev = ("e:" + en, self.cnt[en])
        fn(self.E[en]).then_inc(self.semobj[ev[0]], 1)
        self._commit(ev, reads, writes)

    def dma(self, q, out, in_, reads=(), writes=(), indirect=None, **kw):
        reads = self._flat(reads)
        writes = self._flat(writes)
        j = self.drr
        self.drr = (self.drr + 1) % NDS
        key = "d:%d" % j
        deps = self._deps(q, reads, writes, True)
        if self.duse[j] > 0:
            deps.append((key, 16 * self.duse[j]))
        self._wait(q, deps)
        self.duse[j] += 1
        ev = (key, 16 * self.duse[j])
        if indirect is None:
            ins = self.E[q].dma_start(out=out, in_=in_, **kw)
        else:
            ins = self.E[q].indirect_dma_start(out=out, out_offset=None, in_=in_,
                                               in_offset=bass.IndirectOffsetOnAxis(ap=indirect, axis=0))
        ins.then_inc(self.semobj[key], 16)
        self._commit(ev, reads, writes)

    def barrier(self):
        deps = [("e:" + n, c) for n, c in self.cnt.items() if c > 0]
        deps += [("d:%d" % j, 16 * u) for j, u in enumerate(self.duse) if u > 0]
        for en in self.E:
            self._wait(en, [d for d in deps if d[0] != "e:" + en])


def bcast_rows(ap_row, p):
    n = ap_row.shape[-1]
    return bass.AP(ap_row.tensor, ap_row.offset, [[0, p], [1, n]])


def build():
    nc = bass.Bass("TRN2", target_bir_lowering=False)
    k = KB(nc)

    def din(name, shape, dt=F32):
        return nc.dram_tensor(name, shape, dt, kind="ExternalInput").ap()

    def dout(name, shape, dt=F32):
        return nc.dram_tensor(name, shape, dt, kind="ExternalOutput").ap()

    def dscr(name, shape, dt=F32):
        return T(nc.dram_tensor(name, shape, dt, kind="Internal").ap(), 1)

    I = dict(
        xo=din("xo", [SEG, D]), xp=din("xp", [NPRE, D]), segf=din("segf", [128, 1]),
        cP=din("cP", [1, D]), cS=din("cS", [128, D]), xs=din("xs", [128, D]),
        c128=din("c128", [16, 128, 512]), c512=din("c512", [16, 512, 512]), c2048=din("c2048", [16, 2048, 512]),
        sgla=din("sgla", [16, 4, 128, 256]),
        w_ada=din("w_ada", [D, 6 * D]), b_ada=din("b_ada", [1, 6 * D]), w_in=din("w_in", [D, 7440]),
        w_up=din("w_up", [16, 512]), b_gla=din("b_gla", [1, 512]), gnw=din("gnw", [1, D]),
        w_br_a=din("w_br_a", [256, D]), w_br_b=din("w_br_b", [D, D]), w_out=din("w_out", [D, D]),
        ln1_w=din("ln1_w", [1, D]), ln1_b=din("ln1_b", [1, D]), w_pq=din("w_pq", [D, 2048]),
        pk1=din("pk1", [8, 128, 128]), pk2=din("pk2", [8, 128, 128]),
        pu=din("pu", [16384, D]), pv=din("pv", [16384, D]),
        ln2_w=din("ln2_w", [1, D]), ln2_b=din("ln2_b", [1, D]),
        identf=din("identf", [128, 128]), tri=din("tri", [128, 128]), tris=din("tris", [128, 128]),
        biasT=din("biasT", [128, 12, 256]), iota16=din("iota16", [128, 16]), biasS=din("biasS", [1, 3 * 128 * 4]),
    )
    O = dict(
        yo=dout("yo", [SEG, D]), ys=dout("ys", [16, D]),
        kvp128=dout("kvp128", [128, 512]), kvp512=dout("kvp512", [512, 512]), kvp2048=dout("kvp2048", [2048, 512]),
        glap=dout("glap", [4, 128, 256]),
        kvs128=dout("kvs128", [16, 128, 512]), kvs512=dout("kvs512", [16, 512, 512]),
        kvs2048=dout("kvs2048", [16, 2048, 512]), glas=dout("glas", [16, 4, 128, 256]),
    )
    NT = SEG // 128 + 1
    modS = dscr("modS", [128, 6 * D])
    modP = dscr("modP", [128, 6 * D])
    OAT = dscr("OAT", [4, 64, NT * 128], BF16)
    OB = dscr("OB", [NT * 128, D], BF16)
    X1 = dscr("X1", [NT * 128, D], F32)
    PUV = T(nc.dram_tensor("PUV", [16384, 2 * D], BF16, kind="Internal").ap(), 64)
    dram_out = T(None, 1)

    g = ExitStack()
    identf = k.sb(g, "identf", [128, 128], F32)
    identb = k.sb(g, "identb", [128, 128], BF16)
    trif = k.sb(g, "trif", [128, 128], F32)
    trisf = k.sb(g, "trisf", [128, 128], F32)
    onesf = k.sb(g, "onesf", [128, 128], F32)
    onesb = k.sb(g, "onesb", [128, 128], BF16)
    segc = k.sb(g, "segc", [128, 1], F32)
    S = k.sb(g, "S", [128, 4, 256], F32)
    Sb = k.sb(g, "Sb", [128, 4, 256], BF16)
    pbank = [k.ps(g, "pf%d" % i, [128, 512], F32) for i in range(6)]
    pbb = [k.ps(g, "pb%d" % i, [128, 1024], BF16) for i in range(2)]

    k.dma("sp", identf.t[:], I["identf"], [], [identf])
    k.dma("pool", identb.t[:], I["identf"], [], [identb])
    k.dma("sp", trif.t[:], I["tri"], [], [trif])
    k.dma("sp", trisf.t[:], I["tris"], [], [trisf])
    k.dma("sp", segc.t[:], I["segf"], [], [segc])
    k.op("pool", lambda e: e.memset(onesf.t[:], 1.0), [], [onesf])
    k.op("pool", lambda e: e.memset(onesb.t[:], 1.0), [], [onesb])
    k.op("pool", lambda e: e.memset(S.t[:], 0.0), [], [S])
    k.op("pool", lambda e: e.memset(Sb.t[:], 0.0), [], [Sb])

    w_in_v = I["w_in"].rearrange("(c p) n -> p c n", p=128)

    def layer_norm(*a):
        for _ in layer_norm_g(*a):
            pass

    def layer_norm_g(st_tmp, src_ap, src_bufs, A_ap, B_ap, ab_bufs, out_ap, out_bufs, tag):
        st_tmp["n"] += 1
        sel = st_tmp["n"] % 2
        junk, s1, s2, t1 = st_tmp["junk"][sel], st_tmp["s1"][sel], st_tmp["s2"][sel], st_tmp["t1"][sel]
        k.op("act", lambda e: e.activation(out=junk.t[:], in_=src_ap, func=AF.Identity, accum_out=s1.t[:, 0:1]),
             src_bufs, [junk, s1])
        k.op("act", lambda e: e.activation(out=junk.t[:], in_=src_ap, func=AF.Square, accum_out=s2.t[:, 0:1]),
             src_bufs, [junk, s2])
        yield
        k.op("dve", lambda e: e.tensor_scalar(out=t1.t[:, 0:1], in0=s1.t[:, 0:1], scalar1=1.0 / D, scalar2=None,
                                              op0=ALU.mult), [s1], [t1])
        k.op("dve", lambda e: e.tensor_tensor(out=t1.t[:, 1:2], in0=t1.t[:, 0:1], in1=t1.t[:, 0:1], op=ALU.mult),
             [t1], [t1])
        k.op("dve", lambda e: e.scalar_tensor_tensor(out=t1.t[:, 2:3], in0=s2.t[:, 0:1], scalar=1.0 / D,
                                                     in1=t1.t[:, 1:2], op0=ALU.mult, op1=ALU.subtract), [s2, t1], [t1])
        k.op("dve", lambda e: e.tensor_scalar(out=t1.t[:, 2:3], in0=t1.t[:, 2:3], scalar1=LN_EPS, scalar2=None,
                                              op0=ALU.add), [t1], [t1])
        yield
        k.op("act", lambda e: e.activation(out=t1.t[:, 3:4], in_=t1.t[:, 2:3], func=AF.Ln), [t1], [t1])
        k.op("act", lambda e: e.activation(out=t1.t[:, 4:5], in_=t1.t[:, 3:4], func=AF.Exp, scale=-0.5), [t1], [t1])
        k.op("dve", lambda e: e.tensor_scalar(out=t1.t[:, 5:6], in0=t1.t[:, 0:1], scalar1=t1.t[:, 4:5], scalar2=-1.0,
                                              op0=ALU.mult, op1=ALU.mult), [t1], [t1])
        k.op("act", lambda e: e.activation(out=junk.t[:], in_=src_ap, func=AF.Identity, bias=t1.t[:, 5:6],
                                           scale=t1.t[:, 4:5]), src_bufs + [t1], [junk])
        yield
        k.op("dve", lambda e: e.tensor_tensor(out=junk.t[:], in0=junk.t[:], in1=A_ap, op=ALU.mult),
             [junk] + ab_bufs, [junk])
        k.op("dve", lambda e: e.tensor_tensor(out=out_ap, in0=junk.t[:], in1=B_ap, op=ALU.add),
             [junk] + ab_bufs, out_bufs)

    def ln_scratch(st):
        return dict(n=0, junk=[k.sb(st, "lnjunk", [128, D], F32) for _ in range(2)],
                    s1=[k.sb(st, "lns1", [128, 1], F32) for _ in range(2)],
                    s2=[k.sb(st, "lns2", [128, 1], F32) for _ in range(2)],
                    t1=[k.sb(st, "lnt1", [128, 8], F32) for _ in range(2)])

    def transpose8(src, dst, pb, rows=128):
        for c in range(8):
            k.op("pe", lambda e, c=c: e.transpose(out=pb.t[:, c * 128:(c + 1) * 128], in_=src.t[:, c * 128:(c + 1) * 128],
                                                  identity=identb.t[:]), [src, identb], [pb])
        k.op("act", lambda e: e.activation(out=dst.t[:].rearrange("p c t -> p (c t)"), in_=pb.t[:], func=AF.Copy),
             [pb], [dst])

    def load_w(st, name, dram_view, c0, ncols, q="pool"):
        npc = (ncols + 511) // 512
        w = k.sb(st, name, [128, 8, ncols], BF16, n=8 * npc)
        for c in range(8):
            for pi, j0 in enumerate(range(0, ncols, 512)):
                j1 = min(ncols, j0 + 512)
                k.dma(q, w.t[:, c, j0:j1], dram_view[:, c, c0 + j0:c0 + j1], [], [w.bs[c * npc + pi]])
        return w

    def phase0():
        st = ExitStack()
        cs = k.sb(st, "cs", [128, D], F32)
        cp = k.sb(st, "cp", [1, D], F32)
        csT = k.sb(st, "csT", [128, 8, 128], BF16)
        cpT = k.sb(st, "cpT", [128, 8], BF16)
        bada = k.sb(st, "bada", [128, 512], F32)
        wblk = [k.sb(st, "wblk%d" % i, [128, 8, 512], BF16, n=8) for i in range(2)]
        mrow = k.sb(st, "mrow", [1, 512], F32)
        stg = [k.sb(st, "stg%d" % i, [128, 512], F32) for i in range(2)]
        stg2 = [k.sb(st, "stgp%d" % i, [128, 512], F32) for i in range(2)]
        k.dma("sp", cs.t[:], I["cS"], [], [cs])
        k.dma("sp", cp.t[:], I["cP"], [], [cp])
        k.op("act", lambda e: e.activation(out=cs.t[:], in_=cs.t[:], func=AF.Silu), [cs], [cs])
        k.op("act", lambda e: e.activation(out=cp.t[:], in_=cp.t[:], func=AF.Silu), [cp], [cp])
        pf = pbank[0]
        for c in range(8):
            k.op("pe", lambda e, c=c: e.transpose(out=pf.t[:, 0:128], in_=cs.t[:, c * 128:(c + 1) * 128],
                                                  identity=identf.t[:]), [cs, identf], [pf])
            k.op("pe", lambda e, c=c: e.matmul(pf.t[:, 128:129], lhsT=cp.t[0:1, c * 128:(c + 1) * 128],
                                               rhs=onesf.t[0:1, 0:1], start=True, stop=True), [cp, onesf], [pf])
            k.op("dve", lambda e, c=c: e.tensor_copy(out=csT.t[:, c, :], in_=pf.t[:, 0:128]), [pf], [csT])
            k.op("dve", lambda e, c=c: e.tensor_copy(out=cpT.t[:, c:c + 1], in_=pf.t[:, 128:129]), [pf], [cpT])
        w_ada_v = I["w_ada"].rearrange("(c p) n -> p c n", p=128)
        for blk in range(12):
            wb = wblk[blk % 2]
            cols = slice(blk * 512, (blk + 1) * 512)
            for c in range(8):
                k.dma("pool", wb.t[:, c, :], w_ada_v[:, c, cols], [], [wb.bs[c]])
            k.dma("sp", bada.t[:], bcast_rows(I["b_ada"][:, cols], 128), [], [bada])
            p1, p2, p3 = pbank[1], pbank[2], pbank[3]
            for c in range(8):
                k.op("pe", lambda e, c=c: e.matmul(p1.t[:, :], lhsT=csT.t[:, c, :], rhs=wb.t[:, c, :], start=(c == 0),
                                                   stop=(c == 7)), [csT, wb], [p1])
            for c in range(8):
                k.op("pe", lambda e, c=c: e.matmul(p2.t[0:1, :], lhsT=cpT.t[:, c:c + 1], rhs=wb.t[:, c, :], start=(c == 0),
                                                   stop=(c == 7)), [cpT, wb], [p2])
            k.op("act", lambda e: e.activation(out=mrow.t[:], in_=p2.t[0:1, :], func=AF.Copy), [p2], [mrow])
            k.op("pe", lambda e: e.matmul(p3.t[:, :], lhsT=onesf.t[0:1, :], rhs=mrow.t[0:1, :], start=True, stop=True),
                 [onesf, mrow], [p3])
            sa, sp_ = stg[blk % 2], stg2[blk % 2]
            k.op("dve", lambda e: e.tensor_tensor(out=sa.t[:], in0=p1.t[:], in1=bada.t[:], op=ALU.add), [p1, bada], [sa])
            k.op("dve", lambda e: e.tensor_tensor(out=sp_.t[:], in0=p3.t[:], in1=bada.t[:], op=ALU.add), [p3, bada], [sp_])
            if blk in (2, 3, 8, 9):
                k.op("pool", lambda e: e.tensor_scalar_add(out=sa.t[:], in0=sa.t[:], scalar1=1.0), [sa], [sa])
                k.op("pool", lambda e: e.tensor_scalar_add(out=sp_.t[:], in0=sp_.t[:], scalar1=1.0), [sp_], [sp_])
            k.dma("sp", modS.t[:, cols], sa.t[:], [sa], [modS])
            k.dma("sp", modP.t[:, cols], sp_.t[:], [sp_], [modP])
        k.barrier()
        st.close()

    def phaseG():
        st = ExitStack()
        A1 = k.sb(st, "A1", [128, D], F32)
        B1 = k.sb(st, "B1", [128, D], F32)
        k.dma("sp", A1.t[:], modP.t[:, 1 * D:2 * D], [modP], [A1])
        k.dma("sp", B1.t[:], modP.t[:, 0:D], [modP], [B1])
        Wq = load_w(st, "Wq", w_in_v, C_QB, 512)
        Wk = load_w(st, "Wk", w_in_v, C_KB, 512)
        Wv = load_w(st, "Wv", w_in_v, C_VB, 1024)
        Wg = load_w(st, "Wg", w_in_v, C_GLR, 16)
        Wka = load_w(st, "Wka", w_in_v, C_KA, 768)
        Wva = load_w(st, "Wva", w_in_v, C_VA, 768)
        wup = k.sb(st, "wup", [16, 512], BF16)
        bgl = k.sb(st, "bgl", [1, 512], BF16)
        gnw = k.sb(st, "gnw", [128, D], F32)
        k.dma("pool", wup.t[:], I["w_up"], [], [wup])
        k.dma("pool", bgl.t[:], I["b_gla"], [], [bgl])
        k.dma("sp", gnw.t[:], bcast_rows(I["gnw"], 128), [], [gnw])
        lns = ln_scratch(st)
        xt = [k.sb(st, "xt%d" % i, [128, D], F32) for i in range(2)]
        hbs = [k.sb(st, "hb%d" % i, [128, D], BF16) for i in range(2)]
        hTs_ = [k.sb(st, "hT%d" % i, [128, 8, 128], BF16) for i in range(2)]
        ee = k.sb(st, "ee", [128, 512], F32)
        ll = k.sb(st, "ll", [128, 512], F32)
        e3 = k.sb(st, "e3", [128, 512], F32)
        e1 = k.sb(st, "e1", [128, 512], F32)
        khat = k.sb(st, "khat", [128, 512], BF16)
        kchk = k.sb(st, "kchk", [128, 512], BF16)
        qtil = k.sb(st, "qtil", [128, 512], BF16)
        qkT = k.sb(st, "qkT", [128, 8, 128], BF16)
        dec = k.sb(st, "dec", [128, 4], F32)
        scT = k.sb(st, "scT", [128, 4, 128], BF16)
        og = k.sb(st, "og", [128, 4, 256], F32)
        ssq = k.sb(st, "ssq", [128, 8], F32)
        obt = k.sb(st, "obt", [128, D], BF16)
        kvst = [k.sb(st, "kvst%d" % i, [128, 512], F32) for i in range(2)]
        keep = k.sb(st, "keep", [128, 4], F32)
        for j in range(3):
            k.op("dve", lambda e, j=j: e.tensor_scalar(out=keep.t[:, j:j + 1], in0=segc.t[:, 0:1], scalar1=float(j - 2),
                                                       scalar2=0.0, op0=ALU.add, op1=ALU.max), [segc], [keep])
            k.op("dve", lambda e, j=j: e.tensor_scalar(out=keep.t[:, j:j + 1], in0=keep.t[:, j:j + 1], scalar1=1.0,
                                                       scalar2=None, op0=ALU.min), [keep], [keep])
        cvb = [k.sb(st, "cvb%d" % i, [128, 4, D], BF16) for i in range(4)]

        def convert_chunk(ci):
            src, half = (I["pu"], 0) if ci < 32 else (I["pv"], 1)
            cj = ci % 32
            cv = cvb[ci % 4]
            rows = slice(cj * 512, (cj + 1) * 512)
            k.dma("pool", cv.t[:], src[rows, :].rearrange("(p j) n -> p j n", j=4), [], [cv])
            k.dma("sp", PUV.t[rows, half * D:(half + 1) * D].rearrange("(p j) n -> p j n", j=4), cv.t[:], [cv], [PUV.bs[ci]])
        ntile_pre = NPRE // 128
        ntile = ntile_pre + SEG // 128
        pK, pV0, pV1, pM = pbank[1], pbank[2], pbank[3], pbank[5]
        pX, pD = pbank[4], pbank[0]
        pTb, pQK = pbb[0], pbb[1]
        glrTs = [k.sb(st, "glrT%d" % i, [16, 128], BF16) for i in range(2)]
        vbfs = [k.sb(st, "vbf%d" % i, [128, 1024], BF16) for i in range(2)]
        ksbs = [k.sb(st, "ksb%d" % i, [128, 512], F32) for i in range(2)]
        qsbs = [k.sb(st, "qsb%d" % i, [128, 512], F32) for i in range(2)]

        def front1(ti):
            own = ti >= ntile_pre
            to = ti - ntile_pre
            src = I["xo"][to * 128:(to + 1) * 128, :] if own else I["xp"][ti * 128:(ti + 1) * 128, :]
            x = xt[ti % 2]
            hb, hT = hbs[ti % 2], hTs_[ti % 2]
            glrT, vbf, ksb, qsb = glrTs[ti % 2], vbfs[ti % 2], ksbs[ti % 2], qsbs[ti % 2]
            k.dma("sp", x.t[:], src, [], [x])
            yield from layer_norm_g(lns, x.t[:], [x], A1.t[:], B1.t[:], [A1, B1], hb.t[:], [hb], "g")
            if ti % 2 == 0:
                convert_chunk(ti // 2)
            yield

        def front2(ti):
            own = ti >= ntile_pre
            to = ti - ntile_pre
            hb, hT = hbs[ti % 2], hTs_[ti % 2]
            glrT, vbf, ksb, qsb = glrTs[ti % 2], vbfs[ti % 2], ksbs[ti % 2], qsbs[ti % 2]
            transpose8(hb, hT, pTb)
            yield
            for c in range(8):
                k.op("pe", lambda e, c=c: e.matmul(pK.t[:, :], lhsT=hT.t[:, c, :], rhs=Wk.t[:, c, :], start=(c == 0),
                                                   stop=(c == 7)), [hT, Wk], [pK])
            k.op("act", lambda e: e.activation(out=ksb.t[:], in_=pK.t[:], func=AF.Copy), [pK], [ksb])
            yield
            for half, pv in ((0, pV0), (1, pV1)):
                for c in range(8):
                    k.op("pe", lambda e, c=c, half=half, pv=pv: e.matmul(
                        pv.t[:, :], lhsT=hT.t[:, c, :], rhs=Wv.t[:, c, half * 512:(half + 1) * 512], start=(c == 0),
                        stop=(c == 7)), [hT, Wv], [pv])
                yield
            for c in range(8):
                k.op("pe", lambda e, c=c: e.matmul(pM.t[0:16, 0:128], lhsT=Wg.t[:, c, :], rhs=hT.t[:, c, :], start=(c == 0),
                                                   stop=(c == 7)), [hT, Wg], [pM])
            k.op("act", lambda e: e.activation(out=glrT.t[:], in_=pM.t[0:16, 0:128], func=AF.Copy), [pM], [glrT])
            k.op("act", lambda e: e.activation(out=vbf.t[:, 0:512], in_=pV0.t[:], func=AF.Copy), [pV0], [vbf])
            k.op("act", lambda e: e.activation(out=vbf.t[:, 512:1024], in_=pV1.t[:], func=AF.Copy), [pV1], [vbf])
            yield
            if own:
                for c in range(8):
                    k.op("pe", lambda e, c=c: e.matmul(pK.t[:, :], lhsT=hT.t[:, c, :], rhs=Wq.t[:, c, :], start=(c == 0),
                                                       stop=(c == 7)), [hT, Wq], [pK])
                k.op("act", lambda e: e.activation(out=qsb.t[:], in_=pK.t[:], func=AF.Copy, scale=128.0 ** -0.5), [pK], [qsb])
                yield
                for gi in range(3):
                    if SEG - (to + 1) * 128 < WINS[gi]:
                        kv = kvst[gi % 2]
                        for c in range(8):
                            k.op("pe", lambda e, c=c, gi=gi: e.matmul(pM.t[:, 0:256], lhsT=hT.t[:, c, :],
                                                                      rhs=Wka.t[:, c, gi * 256:(gi + 1) * 256],
                                                                      start=(c == 0), stop=(c == 7)), [hT, Wka], [pM])
                        for c in range(8):
                            k.op("pe", lambda e, c=c, gi=gi: e.matmul(pM.t[:, 256:512], lhsT=hT.t[:, c, :],
                                                                      rhs=Wva.t[:, c, gi * 256:(gi + 1) * 256],
                                                                      start=(c == 0), stop=(c == 7)), [hT, Wva], [pM])
                        k.op("act", lambda e, kv=kv: e.activation(out=kv.t[:], in_=pM.t[:], func=AF.Copy), [pM], [kv])
                        r0 = (to + 1) * 128 - (SEG - WINS[gi]) - 128
                        k.dma("sp", O["kvp%d" % WINS[gi]][r0:r0 + 128, :], kv.t[:], [kv], [dram_out])
                        yield

        def back(ti):
            own = ti >= ntile_pre
            to = ti - ntile_pre
            glrT, vbf, ksb, qsb = glrTs[ti % 2], vbfs[ti % 2], ksbs[ti % 2], qsbs[ti % 2]
            k.op("pe", lambda e: e.matmul(pX.t[:, :], lhsT=glrT.t[:, :], rhs=wup.t[:, :], start=True, stop=False),
                 [glrT, wup], [pX])
            k.op("pe", lambda e: e.matmul(pX.t[:, :], lhsT=onesb.t[0:1, :], rhs=bgl.t[0:1, :], start=False, stop=True),
                 [onesb, bgl], [pX])
            k.op("act", lambda e: e.activation(out=ee.t[:], in_=pX.t[:], func=AF.Exp, scale=-1.0), [pX], [ee])
            k.op("act", lambda e: e.activation(out=ll.t[:], in_=ee.t[:], func=AF.Ln, bias=1.0), [ee], [ll])
            yield
            k.op("pe", lambda e: e.matmul(pX.t[:, :], lhsT=trisf.t[:, :], rhs=ll.t[:, :], start=True, stop=True),
                 [trisf, ll], [pX])
            for h in range(4):
                k.op("pe", lambda e, h=h: e.matmul(pD.t[:, h:h + 1], lhsT=ll.t[:, h * 128:(h + 1) * 128],
                                                   rhs=onesf.t[:, 0:1], start=True, stop=True), [ll, onesf], [pD])
            k.op("act", lambda e: e.activation(out=e3.t[:], in_=pX.t[:], func=AF.Exp, scale=-1.0 / 16), [pX], [e3])
            k.op("act", lambda e: e.activation(out=dec.t[:], in_=pD.t[:, 0:4], func=AF.Exp, scale=-1.0 / 16), [pD], [dec])
            yield
            k.op("dve", lambda e: e.tensor_tensor(out=khat.t[:], in0=ksb.t[:], in1=e3.t[:], op=ALU.mult), [ksb, e3], [khat])
            yield
            if own:
                k.op("pe", lambda e: e.matmul(pX.t[:, :], lhsT=trif.t[:, :], rhs=ll.t[:, :], start=True, stop=True),
                     [trif, ll], [pX])
                k.op("act", lambda e: e.activation(out=e1.t[:], in_=pX.t[:], func=AF.Exp, scale=-1.0 / 16), [pX], [e1])
                k.op("act", lambda e: e.activation(out=e3.t[:], in_=pX.t[:], func=AF.Exp, scale=1.0 / 16), [pX], [e3])
                k.op("dve", lambda e: e.tensor_tensor(out=kchk.t[:], in0=ksb.t[:], in1=e3.t[:], op=ALU.mult), [ksb, e3], [kchk])
                k.op("dve", lambda e: e.tensor_tensor(out=qtil.t[:], in0=qsb.t[:], in1=e1.t[:], op=ALU.mult), [qsb, e1], [qtil])
                yield
                for h in range(4):
                    k.op("pe", lambda e, h=h: e.transpose(out=pQK.t[:, h * 128:(h + 1) * 128],
                                                          in_=qtil.t[:, h * 128:(h + 1) * 128], identity=identb.t[:]),
                         [qtil, identb], [pQK])
                    k.op("pe", lambda e, h=h: e.transpose(out=pQK.t[:, (4 + h) * 128:(5 + h) * 128],
                                                          in_=kchk.t[:, h * 128:(h + 1) * 128], identity=identb.t[:]),
                         [kchk, identb], [pQK])
                k.op("act", lambda e: e.activation(out=qkT.t[:].rearrange("p c t -> p (c t)"), in_=pQK.t[:], func=AF.Copy),
                     [pQK], [qkT])
                yield
                for h in range(4):
                    k.op("pe", lambda e, h=h: e.matmul(pX.t[:, h * 128:(h + 1) * 128], lhsT=qkT.t[:, 4 + h, :],
                                                       rhs=qkT.t[:, h, :], start=True, stop=True), [qkT], [pX])
                k.op("dve", lambda e: e.tensor_tensor(
                    out=scT.t[:], in0=pX.t[:].rearrange("p (h t) -> p h t", h=4),
                    in1=trif.t[:].unsqueeze(1).to_broadcast([128, 4, 128]), op=ALU.mult), [pX, trif], [scT])
                yield
                for hp in range(2):
                    for h in (2 * hp, 2 * hp + 1):
                        cs_ = slice((h % 2) * 256, (h % 2) * 256 + 256)
                        k.op("pe", lambda e, h=h, cs_=cs_: e.matmul(pD.t[:, cs_], lhsT=scT.t[:, h, :],
                                                                    rhs=vbf.t[:, h * 256:(h + 1) * 256], start=True, stop=False),
                             [scT, vbf], [pD])
                        k.op("pe", lambda e, h=h, cs_=cs_: e.matmul(pD.t[:, cs_], lhsT=qkT.t[:, h, :], rhs=Sb.t[:, h, :],
                                                                    start=False, stop=True), [qkT, Sb], [pD])
                    k.op("act", lambda e, hp=hp: e.activation(out=og.t[:, 2 * hp:2 * hp + 2, :].rearrange("p h v -> p (h v)"),
                                                              in_=pD.t[:], func=AF.Copy), [pD], [og])
                    yield
                for h in range(4):
                    k.op("dve", lambda e, h=h: e.scalar_tensor_tensor(out=e1.t[:, 0:256], in0=og.t[:, h, :], scalar=1.0,
                                                                      in1=og.t[:, h, :], op0=ALU.mult, op1=ALU.mult,
                                                                      accum_out=ssq.t[:, h:h + 1]), [og], [e1, ssq])
                k.op("dve", lambda e: e.tensor_scalar(out=ssq.t[:, 0:4], in0=ssq.t[:, 0:4], scalar1=1.0 / 256, scalar2=LN_EPS,
                                                      op0=ALU.mult, op1=ALU.add), [ssq], [ssq])
                k.op("act", lambda e: e.activation(out=ssq.t[:, 0:4], in_=ssq.t[:, 0:4], func=AF.Sqrt), [ssq], [ssq])
                k.op("dve", lambda e: e.reciprocal(out=ssq.t[:, 4:8], in_=ssq.t[:, 0:4]), [ssq], [ssq])
                for h in range(4):
                    k.op("dve", lambda e, h=h: e.scalar_tensor_tensor(
                        out=obt.t[:, h * 256:(h + 1) * 256], in0=og.t[:, h, :], scalar=ssq.t[:, 4 + h:5 + h],
                        in1=gnw.t[:, h * 256:(h + 1) * 256], op0=ALU.mult, op1=ALU.mult), [og, ssq, gnw], [obt])
                k.dma("sp", OB.t[to * 128:(to + 1) * 128, :], obt.t[:], [obt], [OB])
                yield
            for hp in range(2):
                for h in (2 * hp, 2 * hp + 1):
                    cs_ = slice((h % 2) * 256, (h % 2) * 256 + 256)
                    k.op("pe", lambda e, h=h, cs_=cs_: e.matmul(pD.t[:, cs_], lhsT=khat.t[:, h * 128:(h + 1) * 128],
                                                                rhs=vbf.t[:, h * 256:(h + 1) * 256], start=True, stop=True),
                         [khat, vbf], [pD])
                for h in (2 * hp, 2 * hp + 1):
                    cs_ = slice((h % 2) * 256, (h % 2) * 256 + 256)
                    k.op("dve", lambda e, h=h, cs_=cs_: e.scalar_tensor_tensor(
                        out=S.t[:, h, :], in0=S.t[:, h, :], scalar=dec.t[:, h:h + 1], in1=pD.t[:, cs_], op0=ALU.mult,
                        op1=ALU.add), [S, dec, pD], [S])
                yield
            if (not own) and (ti + 1) % 32 == 0:
                j = (ti + 1) // 32 - 1
                k.op("dve", lambda e, j=j: e.tensor_scalar(out=S.t[:].rearrange("p h v -> p (h v)"),
                                                           in0=S.t[:].rearrange("p h v -> p (h v)"),
                                                           scalar1=keep.t[:, j:j + 1], scalar2=None, op0=ALU.mult), [S, keep], [S])
            if ti >= ntile_pre - 1:
                k.op("act", lambda e: e.activation(out=Sb.t[:].rearrange("p h v -> p (h v)"),
                                                   in_=S.t[:].rearrange("p h v -> p (h v)"), func=AF.Copy), [S], [Sb])

        def interleave(gens):
            gens = [g_ for g_ in gens if g_ is not None]
            while gens:
                for g_ in list(gens):
                    try:
                        next(g_)
                    except StopIteration:
                        gens.remove(g_)

        interleave([front1(0)])
        interleave([front1(1), front2(0)])
        for ti in range(ntile):
            interleave([front1(ti + 2) if ti + 2 < ntile else None, front2(ti + 1) if ti + 1 < ntile else None, back(ti)])
        k.dma("sp", O["glap"].rearrange("h k v -> k h v"), S.t[:], [S], [dram_out])
        k.barrier()
        st.close()


    def phaseA():
        st = ExitStack()
        A1 = k.sb(st, "A1", [128, D], F32)
        B1 = k.sb(st, "B1", [128, D], F32)
        k.dma("sp", A1.t[:], modP.t[:, 1 * D:2 * D], [modP], [A1])
        k.dma("sp", B1.t[:], modP.t[:, 0:D], [modP], [B1])
        lns = ln_scratch(st)
        xt = [k.sb(st, "xt%d" % i, [128, D], F32) for i in range(2)]
        hbs = [k.sb(st, "hb%d" % i, [128, D], BF16) for i in range(2)]
        hTs = k.sb(st, "hTs", [128, 8, 2048], BF16, n=16)
        HTS = [dscr("HTS%d" % i, [128, 8 * 2048], BF16) for i in range(3)]
        KT = [[k.sb(st, "KT%d%d" % (gi, pr), [128, 2048], BF16) for pr in range(2)] for gi in range(3)]
        VA = [[k.sb(st, "VA%d%d" % (gi, pr), [128, 16, 2, 65], BF16) for pr in range(2)] for gi in range(3)]
        QT = k.sb(st, "QT", [128, 2048], BF16)
        acc = k.sb(st, "acc", [128, 2, 2048], F32)
        oaN = k.sb(st, "oaN", [64, 2, 2048], BF16)
        bT = k.sb(st, "bT", [128, 12, 256], F32)
        bF = k.sb(st, "bF", [128, 12, 128], F32)
        negc = k.sb(st, "negc", [128, 1], F32)
        Tt = [k.sb(st, "Tt%d" % i, [128, 256], F32) for i in range(2)]
        PT = [k.sb(st, "PT%d" % i, [128, 2, 128], BF16) for i in range(2)]
        k.dma("sp", bT.t[:], I["biasT"], [], [bT])
        k.op("dve", lambda e: e.tensor_scalar(out=negc.t[:], in0=segc.t[:], scalar1=1.0, scalar2=-1.0, op0=ALU.min,
                                              op1=ALU.add), [segc], [negc])
        k.op("dve", lambda e: e.tensor_scalar(out=negc.t[:], in0=negc.t[:], scalar1=30000.0, scalar2=None, op0=ALU.mult),
             [negc], [negc])
        k.op("dve", lambda e: e.tensor_scalar(out=bF.t[:], in0=bT.t[:, :, 0:128], scalar1=negc.t[:, 0:1], scalar2=None,
                                              op0=ALU.add), [bT, negc], [bF])
        for gi in range(3):
            for pr in range(2):
                k.op("pool", lambda e, gi=gi, pr=pr: e.memset(VA[gi][pr].t[:], 1.0), [], [VA[gi][pr]])
        pP, pVp = pbank[0], pbank[1]
        pS = [pbank[2], pbank[3]]
        pO = [pbank[4], pbank[5]]
        cnt = 0
        for cp_ in range(2):
            Wq = k.sb(st, "Wqa%d" % cp_, [128, 8, 384], BF16, n=24)
            Wk = k.sb(st, "Wka%d" % cp_, [128, 8, 384], BF16, n=24)
            Wv = k.sb(st, "Wva%d" % cp_, [128, 8, 384], BF16, n=24)
            for gi in range(3):
                c0 = gi * 256 + cp_ * 128
                for c in range(8):
                    k.dma("pool", Wq.t[:, c, gi * 128:(gi + 1) * 128], w_in_v[:, c, C_QA + c0:C_QA + c0 + 128], [], [Wq.bs[gi * 8 + c]])
                    k.dma("pool", Wk.t[:, c, gi * 128:(gi + 1) * 128], w_in_v[:, c, C_KA + c0:C_KA + c0 + 128], [], [Wk.bs[gi * 8 + c]])
                    k.dma("pool", Wv.t[:, c, gi * 128:(gi + 1) * 128], w_in_v[:, c, C_VA + c0:C_VA + c0 + 128], [], [Wv.bs[gi * 8 + c]])
            for span in range(3):
                par = span % 2
                if cp_ == 1:
                    k.dma("sp", hTs.t[:].rearrange("p c t -> p (c t)"), HTS[span].t[:, :], [HTS[span]], hTs.bs)
                def ln_tile(tl, span=span):
                    hb, pTb = hbs[tl % 2], pbb[tl % 2]
                    if span == 0:
                        src = I["xp"][NPRE - 2048 + tl * 128:NPRE - 2048 + (tl + 1) * 128, :]
                    else:
                        src = I["xo"][(span - 1) * 2048 + tl * 128:(span - 1) * 2048 + (tl + 1) * 128, :]
                    x = xt[tl % 2]
                    k.dma("sp", x.t[:], src, [], [x])
                    yield from layer_norm_g(lns, x.t[:], [x], A1.t[:], B1.t[:], [A1, B1], hb.t[:], [hb], "a")
                    yield
                    for c in range(8):
                        k.op("pe", lambda e, c=c: e.transpose(out=pTb.t[:, c * 128:(c + 1) * 128],
                                                              in_=hb.t[:, c * 128:(c + 1) * 128], identity=identb.t[:]),
                             [hb, identb], [pTb])
                    yield
                    k.op("act", lambda e, tl=tl: e.activation(out=hTs.t[:, :, tl * 128:(tl + 1) * 128],
                                                              in_=pTb.t[:].rearrange("p (c t) -> p c t", c=8), func=AF.Copy),
                         [pTb], [hTs.bs[tl]])
                    yield

                for tl in range(0, 16 if cp_ == 0 else 0, 2):
                    pair = [ln_tile(tl), ln_tile(tl + 1)]
                    while pair:
                        for g_ in list(pair):
                            try:
                                next(g_)
                            except StopIteration:
                                pair.remove(g_)
                if cp_ == 0:
                    k.dma("sp", HTS[span].t[:, :], hTs.t[:].rearrange("p c t -> p (c t)"), hTs.bs, [HTS[span]])
                for gi in range(3):
                    dil = DILS[gi]
                    nb = 16 // dil
                    kt, va = KT[gi][par], VA[gi][par]
                    ktp, vap = KT[gi][1 - par], VA[gi][1 - par]
                    wc = slice(gi * 128, (gi + 1) * 128)
                    for tg in range(4):
                        ts_ = slice(tg * 512, (tg + 1) * 512)
                        for c in range(8):
                            k.op("pe", lambda e, c=c, ts_=ts_, wc=wc: e.matmul(pP.t[:, :], lhsT=Wk.t[:, c, wc], rhs=hTs.t[:, c, ts_],
                                                                                 start=(c == 0), stop=(c == 7)),
                                 [Wk] + hTs.bs[tg * 4:tg * 4 + 4], [pP])
                        k.op("act", lambda e, ts_=ts_, kt=kt: e.activation(out=kt.t[:, ts_], in_=pP.t[:], func=AF.Copy), [pP], [kt])
                        if span > 0:
                            for c in range(8):
                                k.op("pe", lambda e, c=c, ts_=ts_, wc=wc: e.matmul(pP.t[:, :], lhsT=Wq.t[:, c, wc],
                                                                                     rhs=hTs.t[:, c, ts_], start=(c == 0),
                                                                                     stop=(c == 7)),
                                     [Wq] + hTs.bs[tg * 4:tg * 4 + 4], [pP])
                            k.op("act", lambda e, ts_=ts_: e.activation(out=QT.t[:, ts_], in_=pP.t[:], func=AF.Copy, scale=0.125),
                                 [pP], [QT])

                    def cols(r, n, dil=dil):
                        st0 = n * 128 * dil + r
                        return slice(st0, st0 + 127 * dil + 1, dil)
                    for b4 in range(4):
                        for bi in range(4):
                            blk = b4 * 4 + bi
                            r, n = blk // nb, blk % nb
                            for c in range(8):
                                k.op("pe", lambda e, c=c, bi=bi, r=r, n=n, wc=wc: e.matmul(
                                    pVp.t[:, bi * 128:(bi + 1) * 128], lhsT=hTs.t[:, c, cols(r, n)], rhs=Wv.t[:, c, wc],
                                    start=(c == 0), stop=(c == 7)), [Wv] + hTs.bs, [pVp])
                        k.op("act", lambda e, b4=b4, va=va: e.activation(
                            out=va.t[:, b4 * 4:(b4 + 1) * 4, :, 0:64],
                            in_=pVp.t[:].rearrange("p (b h d) -> p b h d", b=4, h=2), func=AF.Copy), [pVp], [va])
                    if span == 0:
                        continue
                    for blk in range(16):
                        r, n = blk // nb, blk % nb
                        first = (span == 1 and n == 0)
                        for hh in range(2):
                            gh = gi * 4 + cp_ * 2 + hh
                            hp = slice(hh * 64, (hh + 1) * 64)
                            ps_, po_ = pS[cnt % 2], pO[cnt % 2]
                            tt, pt = Tt[cnt % 2], PT[cnt % 2]
                            cnt += 1
                            if n > 0:
                                kprev, vprev, bprev = kt, va, blk - 1
                                kpc = cols(r, n - 1)
                            else:
                                kprev, vprev, bprev = ktp, vap, r * nb + nb - 1
                                kpc = cols(r, nb - 1)
                            k.op("pe", lambda e, ps_=ps_, kprev=kprev, kpc=kpc, hp=hp, r=r, n=n: e.matmul(
                                ps_.t[:, 0:128], lhsT=kprev.t[hp, kpc], rhs=QT.t[hp, cols(r, n)], start=True, stop=True),
                                 [kprev, QT], [ps_])
                            k.op("pe", lambda e, ps_=ps_, kt=kt, hp=hp, r=r, n=n: e.matmul(
                                ps_.t[:, 128:256], lhsT=kt.t[hp, cols(r, n)], rhs=QT.t[hp, cols(r, n)], start=True, stop=True),
                                 [kt, QT], [ps_])
                            if first:
                                k.op("dve", lambda e, tt=tt, ps_=ps_, gh=gh: e.tensor_tensor(
                                    out=tt.t[:, 0:128], in0=ps_.t[:, 0:128], in1=bF.t[:, gh, :], op=ALU.add), [ps_, bF], [tt])
                                k.op("dve", lambda e, tt=tt, ps_=ps_, gh=gh: e.tensor_tensor(
                                    out=tt.t[:, 128:256], in0=ps_.t[:, 128:256], in1=bT.t[:, gh, 128:256], op=ALU.add),
                                     [ps_, bT], [tt])
                            else:
                                k.op("dve", lambda e, tt=tt, ps_=ps_, gh=gh: e.tensor_tensor(
                                    out=tt.t[:], in0=ps_.t[:, 0:256], in1=bT.t[:, gh, :], op=ALU.add), [ps_, bT], [tt])
                            k.op("act", lambda e, tt=tt, pt=pt: e.activation(out=pt.t[:].rearrange("p a q -> p (a q)"),
                                                                             in_=tt.t[:], func=AF.Exp), [tt], [pt])
                            k.op("pe", lambda e, po_=po_, vprev=vprev, bprev=bprev, hh=hh, pt=pt: e.matmul(
                                po_.t[0:65, 0:128], lhsT=vprev.t[:, bprev, hh, :], rhs=pt.t[:, 0, :], start=True, stop=False),
                                 [vprev, pt], [po_])
                            k.op("pe", lambda e, po_=po_, va=va, blk=blk, hh=hh, pt=pt: e.matmul(
                                po_.t[0:65, 0:128], lhsT=va.t[:, blk, hh, :], rhs=pt.t[:, 1, :], start=False, stop=True),
                                 [va, pt], [po_])
                            if gi == 0:
                                k.op("dve", lambda e, po_=po_, hh=hh, r=r, n=n: e.tensor_copy(
                                    out=acc.t[0:65, hh, cols(r, n)], in_=po_.t[0:65, 0:128]), [po_], [acc])
                            else:
                                k.op("dve", lambda e, po_=po_, hh=hh, r=r, n=n: e.tensor_tensor(
                                    out=acc.t[0:65, hh, cols(r, n)], in0=acc.t[0:65, hh, cols(r, n)], in1=po_.t[0:65, 0:128],
                                    op=ALU.add), [po_, acc], [acc])
                if span == 0:
                    continue
                k.op("dve", lambda e: e.reciprocal(out=acc.t[64:65, :, :], in_=acc.t[64:65, :, :]), [acc], [acc])
                for hh in range(2):
                    for tg in range(4):
                        ts_ = slice(tg * 512, (tg + 1) * 512)
                        k.op("pe", lambda e, hh=hh, ts_=ts_: e.matmul(pP.t[0:64, :], lhsT=onesf.t[64:65, 0:64],
                                                                      rhs=acc.t[64:65, hh, ts_], start=True, stop=True),
                             [onesf, acc], [pP])
                        k.op("dve", lambda e, hh=hh, ts_=ts_: e.tensor_tensor(out=oaN.t[0:64, hh, ts_], in0=acc.t[0:64, hh, ts_],
                                                                              in1=pP.t[0:64, :], op=ALU.mult), [acc, pP], [oaN])
                    t0 = (span - 1) * 2048
                    k.dma("sp", OAT.t[cp_ * 2 + hh, :, t0:t0 + 2048], oaN.t[0:64, hh, :], [oaN], [OAT])
        k.barrier()
        st.close()

    def phaseS():
        st = ExitStack()
        A1 = k.sb(st, "A1", [128, D], F32)
        B1 = k.sb(st, "B1", [128, D], F32)
        k.dma("sp", A1.t[:], modS.t[:, 1 * D:2 * D], [modS], [A1])
        k.dma("sp", B1.t[:], modS.t[:, 0:D], [modS], [B1])
        lns = ln_scratch(st)
        x = k.sb(st, "xs", [128, D], F32)
        hb = k.sb(st, "hb", [128, D], BF16)
        hT = k.sb(st, "hT", [128, 8, 128], BF16)
        pj = k.sb(st, "pj", [128, 4352], F32)
        wb = [k.sb(st, "wb%d" % i, [128, 8, 512], BF16, n=8) for i in range(2)]
        Wg = load_w(st, "Wg", w_in_v, C_GLR, 16)
        wup = k.sb(st, "wup", [16, 512], BF16)
        bgl = k.sb(st, "bgl", [1, 512], BF16)
        gnw = k.sb(st, "gnw", [128, D], F32)
        bS = k.sb(st, "bS", [16, 3, 128, 4], F32)
        k.dma("pool", wup.t[:], I["w_up"], [], [wup])
        k.dma("pool", bgl.t[:], I["b_gla"], [], [bgl])
        k.dma("sp", gnw.t[:], bcast_rows(I["gnw"], 128), [], [gnw])
        k.dma("sp", bS.t[:].rearrange("p g k h -> p (g k h)"), bcast_rows(I["biasS"], 16), [], [bS])
        k.dma("sp", x.t[:], I["xs"], [], [x])
        layer_norm(lns, x.t[:], [x], A1.t[:], B1.t[:], [A1, B1], hb.t[:], [hb], "s")
        transpose8(hb, hT, pbb[0])
        for blk in range(9):
            w = wb[blk % 2]
            n0 = blk * 512
            n1 = min(4352, n0 + 512)
            for c in range(8):
                k.dma("pool", w.t[:, c, 0:n1 - n0], w_in_v[:, c, n0:n1], [], [w.bs[c]])
            pp = pbank[blk % 2]
            for c in range(8):
                k.op("pe", lambda e, c=c, w=w, pp=pp, nn=n1 - n0: e.matmul(pp.t[:, 0:nn], lhsT=hT.t[:, c, :], rhs=w.t[:, c, 0:nn],
                                                                          start=(c == 0), stop=(c == 7)), [hT, w], [pp])
            k.op("act", lambda e, pp=pp, n0=n0, n1=n1: e.activation(out=pj.t[:, n0:n1], in_=pp.t[:, 0:n1 - n0], func=AF.Copy),
                 [pp], [pj])
        newkv = k.sb(st, "newkv", [128, 3, 512], F32)
        for gi in range(3):
            k.op("dve", lambda e, gi=gi: e.tensor_copy(out=newkv.t[:, gi, 0:256], in_=pj.t[:, C_KA + gi * 256:C_KA + (gi + 1) * 256]),
                 [pj], [newkv])
            k.op("dve", lambda e, gi=gi: e.tensor_copy(out=newkv.t[:, gi, 256:512], in_=pj.t[:, C_VA + gi * 256:C_VA + (gi + 1) * 256]),
                 [pj], [newkv])
            W = WINS[gi]
            cin, cout = I["c%d" % W], O["kvs%d" % W]
            k.dma("sp", cout[:, W - 1, :], newkv.t[0:16, gi, :], [newkv], [dram_out])
            for b in range(16):
                k.dma("sp", cout[b, 0:W - 1, :], cin[b, 1:W, :], [], [dram_out])
        oacc = k.sb(st, "oacc", [16, 4, 64], F32)
        zacc = k.sb(st, "zacc", [16, 4], F32)
        pr1 = k.sb(st, "pr1", [16, 12, 64], F32)
        ss = k.sb(st, "ss", [16, 12], F32)
        psf = k.sb(st, "psf", [16, 12], F32)
        qa_v = pj.t[0:16, C_QA:C_QA + 768].rearrange("p (g d) -> p g d", g=12)
        ka_v = pj.t[0:16, C_KA:C_KA + 768].rearrange("p (g d) -> p g d", g=12)
        va_v = pj.t[0:16, C_VA:C_VA + 768].rearrange("p (g d) -> p g d", g=12)
        k.op("dve", lambda e: e.tensor_tensor(out=pr1.t[:], in0=qa_v, in1=ka_v, op=ALU.mult), [pj], [pr1])
        k.op("dve", lambda e: e.tensor_reduce(out=ss.t[:], in_=pr1.t[:], axis=AX.X, op=ALU.add), [pr1], [ss])
        k.op("act", lambda e: e.activation(out=psf.t[:], in_=ss.t[:], func=AF.Exp, scale=0.125), [ss], [psf])
        k.op("dve", lambda e: e.tensor_tensor(out=pr1.t[:], in0=va_v, in1=psf.t[:].unsqueeze(2).to_broadcast([16, 12, 64]),
                                              op=ALU.mult), [pj, psf], [pr1])
        k.op("dve", lambda e: e.tensor_tensor(out=oacc.t[:], in0=pr1.t[:, 0:4, :], in1=pr1.t[:, 4:8, :], op=ALU.add), [pr1], [oacc])
        k.op("dve", lambda e: e.tensor_tensor(out=oacc.t[:], in0=oacc.t[:], in1=pr1.t[:, 8:12, :], op=ALU.add), [pr1, oacc], [oacc])
        k.op("dve", lambda e: e.tensor_tensor(out=zacc.t[:], in0=psf.t[:, 0:4], in1=psf.t[:, 4:8], op=ALU.add), [psf], [zacc])
        k.op("dve", lambda e: e.tensor_tensor(out=zacc.t[:], in0=zacc.t[:], in1=psf.t[:, 8:12], op=ALU.add), [psf, zacc], [zacc])
        Kt = [k.sb(st, "Kt%d" % i, [16, 16, 512], F32) for i in range(2)]
        prd = k.sb(st, "prd", [16, 16, 256], F32)
        scs = k.sb(st, "scs", [16, 64], F32)
        pex = k.sb(st, "pex", [16, 64], F32)
        red = k.sb(st, "red", [16, 256], F32)
        red4 = k.sb(st, "red4", [16, 4], F32)
        ci_ = 0
        for gi in range(3):
            W, dil = WINS[gi], DILS[gi]
            cin = I["c%d" % W]
            for kc in range(8):
                kt = Kt[ci_ % 2]
                ci_ += 1
                r0 = kc * 16 * dil
                k.dma("sp", kt.t[:], cin[:, r0:r0 + 15 * dil + 1:dil, :], [], [kt])
                qg = pj.t[0:16, C_QA + gi * 256:C_QA + (gi + 1) * 256]
                k.op("dve", lambda e, kt=kt, qg=qg: e.tensor_tensor(out=prd.t[:], in0=kt.t[:, :, 0:256],
                                                                    in1=qg.unsqueeze(1).to_broadcast([16, 16, 256]), op=ALU.mult),
                     [kt, pj], [prd])
                k.op("dve", lambda e: e.tensor_reduce(out=scs.t[:], in_=prd.t[:].rearrange("p k (h d) -> p (k h) d", h=4),
                                                      axis=AX.X, op=ALU.add), [prd], [scs])
                k.op("dve", lambda e, gi=gi, kc=kc: e.scalar_tensor_tensor(
                    out=scs.t[:], in0=scs.t[:], scalar=0.125, in1=bS.t[:, gi, kc * 16:(kc + 1) * 16, :].rearrange("p k h -> p (k h)"),
                    op0=ALU.mult, op1=ALU.add), [scs, bS], [scs])
                k.op("act", lambda e: e.activation(out=pex.t[:], in_=scs.t[:], func=AF.Exp), [scs], [pex])
                k.op("dve", lambda e: e.tensor_reduce(out=red4.t[:], in_=pex.t[:].rearrange("p (k h) -> p h k", h=4), axis=AX.X,
                                                      op=ALU.add), [pex], [red4])
                k.op("dve", lambda e: e.tensor_tensor(out=zacc.t[:], in0=zacc.t[:], in1=red4.t[:], op=ALU.add), [zacc, red4], [zacc])
                k.op("dve", lambda e, kt=kt: e.tensor_tensor(
                    out=prd.t[:].rearrange("p k (h d) -> p k h d", h=4),
                    in0=kt.t[:, :, 256:512].rearrange("p k (h d) -> p k h d", h=4),
                    in1=pex.t[:].rearrange("p (k h) -> p k h", h=4).unsqueeze(3).to_broadcast([16, 16, 4, 64]), op=ALU.mult),
                     [kt, pex], [prd])
                k.op("dve", lambda e: e.tensor_reduce(out=red.t[:], in_=prd.t[:].rearrange("p k n -> p n k"), axis=AX.X, op=ALU.add),
                     [prd], [red])
                k.op("dve", lambda e: e.tensor_tensor(out=oacc.t[:].rearrange("p h d -> p (h d)"),
                                                      in0=oacc.t[:].rearrange("p h d -> p (h d)"), in1=red.t[:], op=ALU.add),
                     [oacc, red], [oacc])
        oas = k.sb(st, "oas", [128, 4, 64], BF16)
        oasT = k.sb(st, "oasT", [64, 4, 128], BF16)
        k.op("pool", lambda e: e.memset(oas.t[:], 0.0), [], [oas])
        k.op("dve", lambda e: e.reciprocal(out=zacc.t[:], in_=zacc.t[:]), [zacc], [zacc])
        k.op("dve", lambda e: e.tensor_tensor(out=oas.t[0:16, :, :], in0=oacc.t[:], in1=zacc.t[:].unsqueeze(2).to_broadcast([16, 4, 64]),
                                              op=ALU.mult), [oacc, zacc], [oas])
        pq_ = pbb[1]
        for j in range(4):
            k.op("pe", lambda e, j=j: e.transpose(out=pq_.t[0:64, j * 128:(j + 1) * 128], in_=oas.t[:, j, :], identity=identb.t[:]),
                 [oas, identb], [pq_])
        k.op("act", lambda e: e.activation(out=oasT.t[:].rearrange("p j t -> p (j t)"), in_=pq_.t[0:64, 0:512], func=AF.Copy),
             [pq_], [oasT])
        for j in range(4):
            k.dma("sp", OAT.t[j, :, SEG:SEG + 128], oasT.t[:, j, :], [oasT], [OAT])
        glrT = k.sb(st, "glrT", [16, 128], BF16)
        ee = k.sb(st, "ee", [128, 512], F32)
        pM, pX = pbank[2], pbank[3]
        for c in range(8):
            k.op("pe", lambda e, c=c: e.matmul(pM.t[0:16, 0:128], lhsT=Wg.t[:, c, :], rhs=hT.t[:, c, :], start=(c == 0), stop=(c == 7)),
                 [hT, Wg], [pM])
        k.op("act", lambda e: e.activation(out=glrT.t[:], in_=pM.t[0:16, 0:128], func=AF.Copy), [pM], [glrT])
        k.op("pe", lambda e: e.matmul(pX.t[:, :], lhsT=glrT.t[:, :], rhs=wup.t[:, :], start=True, stop=False), [glrT, wup], [pX])
        k.op("pe", lambda e: e.matmul(pX.t[:, :], lhsT=onesb.t[0:1, :], rhs=bgl.t[0:1, :], start=False, stop=True), [onesb, bgl], [pX])
        k.op("act", lambda e: e.activation(out=ee.t[:], in_=pX.t[:], func=AF.Exp, scale=-1.0), [pX], [ee])
        k.op("act", lambda e: e.activation(out=ee.t[:], in_=ee.t[:], func=AF.Ln, bias=1.0), [ee], [ee])
        k.op("act", lambda e: e.activation(out=ee.t[:], in_=ee.t[:], func=AF.Exp, scale=-1.0 / 16), [ee], [ee])
        qkdT = k.sb(st, "qkdT", [128, 12, 128], F32)
        qsc = k.sb(st, "qsc", [128, 512], F32)
        k.op("dve", lambda e: e.tensor_scalar(out=qsc.t[:], in0=pj.t[:, C_QB:C_QB + 512], scalar1=128.0 ** -0.5, scalar2=None,
                                              op0=ALU.mult), [pj], [qsc])
        for h in range(4):
            for which, srcap, srcb in ((0, qsc.t[:, h * 128:(h + 1) * 128], qsc),
                                       (1, pj.t[:, C_KB + h * 128:C_KB + (h + 1) * 128], pj),
                                       (2, ee.t[:, h * 128:(h + 1) * 128], ee)):
                pt_ = pbank[4 + (h * 3 + which) % 2]
                k.op("pe", lambda e, pt_=pt_, srcap=srcap: e.transpose(out=pt_.t[:, 0:128], in_=srcap, identity=identf.t[:]),
                     [srcb, identf], [pt_])
                k.op("dve", lambda e, pt_=pt_, which=which, h=h: e.tensor_copy(out=qkdT.t[:, which * 4 + h, :], in_=pt_.t[:, 0:128]),
                     [pt_], [qkdT])
        VS = dscr("VS", [128, D], F32)
        k.dma("sp", VS.t[:, :], pj.t[:, C_VB:C_VB + D], [pj], [VS])
        ogs = k.sb(st, "ogs", [128, 4, 256], F32)
        k.op("pool", lambda e: e.memset(ogs.t[:], 0.0), [], [ogs])
        S0 = [k.sb(st, "S0%d" % i, [128, 4, 256], F32) for i in range(2)]
        vbc = [k.sb(st, "vbc%d" % i, [128, 4, 256], F32) for i in range(2)]
        for b in range(16):
            s0, vb_ = S0[b % 2], vbc[b % 2]
            k.dma("sp", s0.t[:], I["sgla"][b].rearrange("h k v -> k h v"), [], [s0])
            k.dma("sp", vb_.t[:].rearrange("p h v -> p (h v)"), bcast_rows(VS.t[b:b + 1, :], 128), [VS], [vb_])
            for h in range(4):
                k.op("dve", lambda e, s0=s0, h=h, b=b: e.tensor_scalar(out=s0.t[:, h, :], in0=s0.t[:, h, :],
                                                                       scalar1=qkdT.t[:, 8 + h, b:b + 1], scalar2=None, op0=ALU.mult),
                     [s0, qkdT], [s0])
                k.op("dve", lambda e, s0=s0, vb_=vb_, h=h, b=b: e.scalar_tensor_tensor(
                    out=s0.t[:, h, :], in0=vb_.t[:, h, :], scalar=qkdT.t[:, 4 + h, b:b + 1], in1=s0.t[:, h, :], op0=ALU.mult,
                    op1=ALU.add), [s0, vb_, qkdT], [s0])
            k.dma("sp", O["glas"][b].rearrange("h k v -> k h v"), s0.t[:], [s0], [dram_out])
            for h in range(4):
                po = pbank[h // 2]
                k.op("pe", lambda e, po=po, h=h, s0=s0: e.matmul(po.t[:, (h % 2) * 256:(h % 2) * 256 + 256], lhsT=qkdT.t[:, h, :],
                                                                 rhs=s0.t[:, h, :], start=True, stop=True), [qkdT, s0], [po])
            for hf in range(2):
                k.op("dve", lambda e, hf=hf, b=b: e.scalar_tensor_tensor(
                    out=ogs.t[:, hf * 2:hf * 2 + 2, :].rearrange("p h v -> p (h v)"), in0=pbank[hf].t[:, :],
                    scalar=identf.t[:, b:b + 1], in1=ogs.t[:, hf * 2:hf * 2 + 2, :].rearrange("p h v -> p (h v)"),
                    op0=ALU.mult, op1=ALU.add), [pbank[hf], identf, ogs], [ogs])
        ssq = k.sb(st, "ssq", [128, 8], F32)
        obt = k.sb(st, "obt", [128, D], BF16)
        for h in range(4):
            k.op("dve", lambda e, h=h: e.scalar_tensor_tensor(out=ee.t[:, 0:256], in0=ogs.t[:, h, :], scalar=1.0, in1=ogs.t[:, h, :],
                                                              op0=ALU.mult, op1=ALU.mult, accum_out=ssq.t[:, h:h + 1]), [ogs], [ee, ssq])
        k.op("dve", lambda e: e.tensor_scalar(out=ssq.t[:, 0:4], in0=ssq.t[:, 0:4], scalar1=1.0 / 256, scalar2=LN_EPS, op0=ALU.mult,
                                              op1=ALU.add), [ssq], [ssq])
        k.op("act", lambda e: e.activation(out=ssq.t[:, 0:4], in_=ssq.t[:, 0:4], func=AF.Sqrt), [ssq], [ssq])
        k.op("dve", lambda e: e.reciprocal(out=ssq.t[:, 4:8], in_=ssq.t[:, 0:4]), [ssq], [ssq])
        for h in range(4):
            k.op("dve", lambda e, h=h: e.scalar_tensor_tensor(out=obt.t[:, h * 256:(h + 1) * 256], in0=ogs.t[:, h, :],
                                                              scalar=ssq.t[:, 4 + h:5 + h], in1=gnw.t[:, h * 256:(h + 1) * 256],
                                                              op0=ALU.mult, op1=ALU.mult), [ogs, ssq, gnw], [obt])
        k.dma("sp", OB.t[SEG:SEG + 128, :], obt.t[:], [obt], [OB])
        k.barrier()
        st.close()

    def phaseB():
        st = ExitStack()
        A1 = k.sb(st, "A1", [128, D], F32)
        B1 = k.sb(st, "B1", [128, D], F32)
        G1 = k.sb(st, "G1", [128, D], F32)
        l1w = k.sb(st, "l1w", [128, D], F32)
        l1b = k.sb(st, "l1b", [128, D], F32)
        k.dma("sp", l1w.t[:], bcast_rows(I["ln1_w"], 128), [], [l1w])
        k.dma("sp", l1b.t[:], bcast_rows(I["ln1_b"], 128), [], [l1b])
        Wrb = load_w(st, "Wrb", w_in_v, C_RB, 1024)
        Wga = load_w(st, "Wga", w_in_v, C_GA, 1024)
        Wgb = load_w(st, "Wgb", w_in_v, C_GB, 1024)
        Wbb = load_w(st, "Wbb", I["w_br_b"].rearrange("(c p) n -> p c n", p=128), 0, 1024)
        Wo = load_w(st, "Wo", I["w_out"].rearrange("(c p) n -> p c n", p=128), 0, 1024)
        Wba = k.sb(st, "Wba", [64, 4, D], BF16, n=8)
        for j in range(4):
            for hf in range(2):
                k.dma("pool", Wba.t[:, j, hf * 512:(hf + 1) * 512], I["w_br_a"][j * 64:(j + 1) * 64, hf * 512:(hf + 1) * 512], [], [Wba.bs[j * 2 + hf]])
        lns = ln_scratch(st)
        xt = [k.sb(st, "xt%d" % i, [128, D], F32) for i in range(3)]
        hbs = [k.sb(st, "hb%d" % i, [128, D], BF16) for i in range(2)]
        hTs_ = [k.sb(st, "hT%d" % i, [128, 8, 128], BF16) for i in range(2)]
        gats = [[k.sb(st, "gat%d_%d" % (pr, i), [128, D], BF16) for i in range(3)] for pr in range(2)]
        obls = [k.sb(st, "obl%d" % i, [128, D], BF16) for i in range(2)]
        oaTs = [k.sb(st, "oaT%d" % i, [64, 4, 128], BF16) for i in range(2)]
        obT = k.sb(st, "obT", [128, 8, 128], BF16)
        t1 = k.sb(st, "t1", [128, D], F32)
        t2 = k.sb(st, "t2", [128, D], F32)
        mg = k.sb(st, "mg", [128, D], BF16)
        x1o = k.sb(st, "x1o", [128, D], F32)

        def f1(ti):
            smp = ti == NT - 1
            if ti == 0 or smp:
                md = modS if smp else modP
                k.dma("sp", A1.t[:], md.t[:, 1 * D:2 * D], [md], [A1])
                k.dma("sp", B1.t[:], md.t[:, 0:D], [md], [B1])
            x, hb = xt[ti % 3], hbs[ti % 2]
            k.dma("sp", x.t[:], I["xs"] if smp else I["xo"][ti * 128:(ti + 1) * 128, :], [], [x])
            yield from layer_norm_g(lns, x.t[:], [x], A1.t[:], B1.t[:], [A1, B1], hb.t[:], [hb], "b")
            yield

        def f2(ti):
            hb, hT, gat, obl, oaT = hbs[ti % 2], hTs_[ti % 2], gats[ti % 2], obls[ti % 2], oaTs[ti % 2]
            k.dma("sp", obl.t[:], OB.t[ti * 128:(ti + 1) * 128, :], [OB], [obl])
            k.dma("sp", oaT.t[:], OAT.t[:, :, ti * 128:(ti + 1) * 128].rearrange("j d t -> d j t"), [OAT], [oaT])
            transpose8(hb, hT, pbb[0])
            yield
            for wi, (W_, fn) in enumerate(((Wrb, AF.Silu), (Wga, AF.Sigmoid), (Wgb, AF.Sigmoid))):
                for hf in range(2):
                    pp = pbank[hf]
                    for c in range(8):
                        k.op("pe", lambda e, c=c, W_=W_, hf=hf, pp=pp: e.matmul(pp.t[:, :], lhsT=hT.t[:, c, :],
                                                                              rhs=W_.t[:, c, hf * 512:(hf + 1) * 512],
                                                                              start=(c == 0), stop=(c == 7)), [hT, W_], [pp])
                    k.op("act", lambda e, wi=wi, hf=hf, pp=pp, fn=fn: e.activation(out=gat[wi].t[:, hf * 512:(hf + 1) * 512],
                                                                                 in_=pp.t[:], func=fn), [pp], [gat[wi]])
                    yield

        def back(ti):
            smp = ti == NT - 1
            if ti == 0 or smp:
                md = modS if smp else modP
                k.dma("sp", G1.t[:], md.t[:, 2 * D:3 * D], [md], [G1])
            x, gat, obl, oaT = xt[ti % 3], gats[ti % 2], obls[ti % 2], oaTs[ti % 2]
            k.op("dve", lambda e: e.tensor_tensor(out=obl.t[:], in0=obl.t[:], in1=gat[0].t[:], op=ALU.mult), [obl, gat[0]], [obl])
            transpose8(obl, obT, pbb[1])
            yield
            for hf in range(2):
                pa, pb_ = pbank[2 + hf], pbank[4 + hf]
                for j in range(4):
                    k.op("pe", lambda e, j=j, hf=hf, pa=pa: e.matmul(pa.t[:, :], lhsT=oaT.t[:, j, :], rhs=Wba.t[:, j, hf * 512:(hf + 1) * 512],
                                                                     start=(j == 0), stop=(j == 3)), [oaT, Wba], [pa])
                for c in range(8):
                    k.op("pe", lambda e, c=c, hf=hf, pb_=pb_: e.matmul(pb_.t[:, :], lhsT=obT.t[:, c, :], rhs=Wbb.t[:, c, hf * 512:(hf + 1) * 512],
                                                                       start=(c == 0), stop=(c == 7)), [obT, Wbb], [pb_])
                hs = slice(hf * 512, (hf + 1) * 512)
                k.op("dve", lambda e, pa=pa, hs=hs: e.tensor_tensor(out=t1.t[:, hs], in0=pa.t[:], in1=gat[1].t[:, hs], op=ALU.mult),
                     [pa, gat[1]], [t1])
                k.op("dve", lambda e, pb_=pb_, hs=hs: e.tensor_tensor(out=t2.t[:, hs], in0=pb_.t[:], in1=gat[2].t[:, hs], op=ALU.mult),
                     [pb_, gat[2]], [t2])
                yield
            k.op("pool", lambda e: e.tensor_tensor(out=mg.t[:], in0=t1.t[:], in1=t2.t[:], op=ALU.add), [t1, t2], [mg])
            transpose8(mg, obT, pbb[1])
            yield
            for hf in range(2):
                pp = pbank[2 + hf]
                hs = slice(hf * 512, (hf + 1) * 512)
                for c in range(8):
                    k.op("pe", lambda e, c=c, hf=hf, pp=pp: e.matmul(pp.t[:, :], lhsT=obT.t[:, c, :], rhs=Wo.t[:, c, hf * 512:(hf + 1) * 512],
                                                                     start=(c == 0), stop=(c == 7)), [obT, Wo], [pp])
                k.op("dve", lambda e, pp=pp, hs=hs: e.tensor_tensor(out=t1.t[:, hs], in0=pp.t[:], in1=G1.t[:, hs], op=ALU.mult),
                     [pp, G1], [t1])
                yield
            k.op("dve", lambda e, x=x: e.scalar_tensor_tensor(out=t2.t[:], in0=x.t[:], scalar=DN_ALPHA, in1=t1.t[:], op0=ALU.mult,
                                                              op1=ALU.add), [x, t1], [t2])
            yield from layer_norm_g(lns, t2.t[:], [t2], l1w.t[:], l1b.t[:], [l1w, l1b], x1o.t[:], [x1o], "b1")
            k.dma("sp", X1.t[ti * 128:(ti + 1) * 128, :], x1o.t[:], [x1o], [X1])
            yield

        def interleave(gens):
            gens = [g_ for g_ in gens if g_ is not None]
            while gens:
                for g_ in list(gens):
                    try:
                        next(g_)
                    except StopIteration:
                        gens.remove(g_)

        interleave([f1(0)])
        interleave([f1(1), f2(0)])
        for ti in range(NT):
            interleave([f1(ti + 2) if ti + 2 < NT else None, f2(ti + 1) if ti + 1 < NT else None, back(ti)])
        k.barrier()
        st.close()

    def phaseC():
        st = ExitStack()
        A2 = k.sb(st, "A2", [128, D], F32)
        B2 = k.sb(st, "B2", [128, D], F32)
        G2 = k.sb(st, "G2", [128, D], F32)
        l2w = k.sb(st, "l2w", [128, D], F32)
        l2b = k.sb(st, "l2b", [128, D], F32)
        k.dma("sp", l2w.t[:], bcast_rows(I["ln2_w"], 128), [], [l2w])
        k.dma("sp", l2b.t[:], bcast_rows(I["ln2_b"], 128), [], [l2b])
        Wpq = load_w(st, "Wpq", I["w_pq"].rearrange("(c p) n -> p c n", p=128), 0, 2048)
        st_kk = ExitStack()
        kkT = k.sb(st, "kkT", [128, 16, 128], BF16)
        kk = k.sb(st_kk, "kk", [128, 16, 128], BF16)
        k.dma("pool", kk.t[:, 0:8, :], I["pk1"].rearrange("h k d -> k h d"), [], [kk])
        k.dma("pool", kk.t[:, 8:16, :], I["pk2"].rearrange("h k d -> k h d"), [], [kk])
        for which in range(2):
            for h in range(8):
                pq_ = pbb[(which * 8 + h) % 2]
                k.op("pe", lambda e, pq_=pq_, which=which, h=h: e.transpose(out=pq_.t[:, 0:128], in_=kk.t[:, which * 8 + h, :],
                                                                            identity=identb.t[:]), [kk, identb], [pq_])
                k.op("dve", lambda e, pq_=pq_, which=which, h=h: e.tensor_copy(out=kkT.t[:, 2 * h + which, :], in_=pq_.t[:, 0:128]),
                     [pq_], [kkT])
        k.barrier()
        st_kk.close()
        io16 = k.sb(st, "io16", [128, 16], F32)
        thr16 = k.sb(st, "thr16", [128, 16], F32)
        k.dma("sp", io16.t[:], I["iota16"], [], [io16])
        k.op("dve", lambda e: e.tensor_scalar(out=thr16.t[:], in0=io16.t[:], scalar1=16.0, scalar2=None, op0=ALU.mult), [io16], [thr16])
        lns = ln_scratch(st)
        x1 = [k.sb(st, "x1_%d" % i, [128, D], F32) for i in range(2)]
        h2s = [k.sb(st, "h2_%d" % i, [128, D], F32) for i in range(2)]
        eis = [k.sb(st, "ei%d" % i, [128, 128], I32) for i in range(2)]
        gws = [k.sb(st, "gw%d" % i, [128, 8, 16], F32) for i in range(2)]
        h2b = k.sb(st, "h2b", [128, D], BF16)
        h2T = k.sb(st, "h2T", [128, 8, 128], BF16)
        qvT = k.sb(st, "qvT", [128, 16, 128], BF16)
        sA = k.sb(st, "sA", [128, 2048], F32)
        sB = k.sb(st, "sB", [128, 2048], F32)
        v16 = k.sb(st, "v16", [128, 16, 16], F32)
        i16u = k.sb(st, "i16u", [128, 16, 16], U32)
        i16f = k.sb(st, "i16f", [128, 16, 16], F32)
        di1 = k.sb(st, "di1", [128, 8, 16], F32)
        sc16 = k.sb(st, "sc16", [128, 8, 16], F32)
        ciu = k.sb(st, "ciu", [128, 8, 16], U32)
        cif = k.sb(st, "cif", [128, 128], F32)
        a1 = k.sb(st, "a1", [128, 128], F32)
        i1s = k.sb(st, "i1s", [128, 128], F32)
        bsel = k.sb(st, "bsel", [128, 128], F32)
        i2s = k.sb(st, "i2s", [128, 128], F32)
        zs = k.sb(st, "zs", [128, 8], F32)
        dots = k.sb(st, "dots", [128, 128], F32)
        actv = k.sb(st, "actv", [128, 128], F32)
        gco = k.sb(st, "gco", [128, 128], F32)
        GS = 4
        NG = 16
        gb_ = [k.sb(st, "gb%d" % i, [128, 2, D], BF16) for i in range(NG)]
        dg = [k.sb(st, "dg%d" % i, [128, 128], BF16) for i in range(8)]
        junk = k.sb(st, "junkc", [128, D], BF16)
        ff = k.sb(st, "ff", [128, D], F32)
        yt = k.sb(st, "yt", [128, D], F32)
        pF = [pbank[4], pbank[5]]

        def top16(vals, vals2, vout, iout, n, w):
            vv = vals.t[:].rearrange("p (c k) -> p c k", k=w)
            v2 = vals2.t[:].rearrange("p (c k) -> p c k", k=w)
            for c_ in range(n):
                k.op("dve", lambda e, c_=c_: e.max(out=vout.t[:, c_, 0:8], in_=vv[:, c_, :]), [vals], [vout])
                k.op("dve", lambda e, c_=c_: e.max_index(out=iout.t[:, c_, 0:8], in_max=vout.t[:, c_, 0:8], in_values=vv[:, c_, :]),
                     [vals, vout], [iout])
                k.op("dve", lambda e, c_=c_: e.match_replace(out=v2[:, c_, :], in_to_replace=vout.t[:, c_, 0:8],
                                                             in_values=vv[:, c_, :], imm_value=-1e30), [vals, vout], [vals2])
                k.op("dve", lambda e, c_=c_: e.max(out=vout.t[:, c_, 8:16], in_=v2[:, c_, :]), [vals2], [vout])
                k.op("dve", lambda e, c_=c_: e.max_index(out=iout.t[:, c_, 8:16], in_max=vout.t[:, c_, 8:16],
                                                         in_values=v2[:, c_, :]), [vals2, vout], [iout])
                if c_ % 2 == 1:
                    yield

        def front(ti):
            smp = ti == NT - 1
            x, h2, ei, gw = x1[ti % 2], h2s[ti % 2], eis[ti % 2], gws[ti % 2]
            if ti == 0 or smp:
                md = modS if smp else modP
                k.dma("sp", A2.t[:], md.t[:, 4 * D:5 * D], [md], [A2])
                k.dma("sp", B2.t[:], md.t[:, 3 * D:4 * D], [md], [B2])
            k.dma("sp", x.t[:], X1.t[ti * 128:(ti + 1) * 128, :], [X1], [x])
            layer_norm(lns, x.t[:], [x], A2.t[:], B2.t[:], [A2, B2], h2.t[:], [h2], "c")
            yield
            k.op("act", lambda e: e.activation(out=h2b.t[:], in_=h2.t[:], func=AF.Copy), [h2], [h2b])
            transpose8(h2b, h2T, pbb[0])
            yield
            for q4 in range(4):
                pp = pbank[q4 % 2]
                for cc in range(4):
                    ch = q4 * 4 + cc
                    for c in range(8):
                        k.op("pe", lambda e, c=c, cc=cc, ch=ch, pp=pp: e.matmul(pp.t[:, cc * 128:(cc + 1) * 128],
                                                                              lhsT=Wpq.t[:, c, ch * 128:(ch + 1) * 128], rhs=h2T.t[:, c, :],
                                                                              start=(c == 0), stop=(c == 7)), [Wpq, h2T], [pp])
                k.op("act", lambda e, q4=q4, pp=pp: e.activation(out=qvT.t[:, q4 * 4:(q4 + 1) * 4, :].rearrange("p c t -> p (c t)"),
                                                                 in_=pp.t[:], func=AF.Copy), [pp], [qvT])
                yield
            for q4 in range(4):
                pp = pbank[2 + q4 % 2]
                for cc in range(4):
                    ch = q4 * 4 + cc
                    k.op("pe", lambda e, cc=cc, ch=ch, pp=pp: e.matmul(pp.t[:, cc * 128:(cc + 1) * 128], lhsT=qvT.t[:, ch, :],
                                                                       rhs=kkT.t[:, ch, :], start=True, stop=True), [qvT, kkT], [pp])
                k.op("act", lambda e, q4=q4, pp=pp: e.activation(out=sA.t[:, q4 * 512:(q4 + 1) * 512], in_=pp.t[:], func=AF.Copy), [pp], [sA])
            yield
            yield from top16(sA, sB, v16, i16u, 16, 128)
            k.op("dve", lambda e: e.tensor_copy(out=i16f.t[:], in_=i16u.t[:]), [i16u], [i16f])
            v4 = v16.t[:].rearrange("p (h w) k -> p h w k", w=2)
            i4 = i16f.t[:].rearrange("p (h w) k -> p h w k", w=2)
            k.op("dve", lambda e: e.tensor_tensor(out=sB.t[:].rearrange("p (h a b) -> p h a b", h=8, a=16),
                                                  in0=v4[:, :, 0, :].unsqueeze(3).to_broadcast([128, 8, 16, 16]),
                                                  in1=v4[:, :, 1, :].unsqueeze(2).to_broadcast([128, 8, 16, 16]), op=ALU.add),
                 [v16], [sB])
            yield
            yield from top16(sB, sA, sc16, ciu, 8, 256)
            k.op("dve", lambda e: e.tensor_copy(out=cif.t[:], in_=ciu.t[:].rearrange("p h k -> p (h k)")), [ciu], [cif])
            k.op("dve", lambda e: e.tensor_copy(out=di1.t[:, :, 0:1], in_=i4[:, :, 0, 0:1]), [i16f], [di1])
            k.op("dve", lambda e: e.tensor_tensor(out=di1.t[:, :, 1:16], in0=i4[:, :, 0, 1:16], in1=i4[:, :, 0, 0:15], op=ALU.subtract),
                 [i16f], [di1])
            bigv = sA.t[:].rearrange("p (h k j) -> p h k j", h=8, k=16)
            bigf = sA.t[:].rearrange("p (m j) -> p m j", j=16)
            cif4 = cif.t[:].rearrange("p (h k) -> p h k", h=8).unsqueeze(3).to_broadcast([128, 8, 16, 16])
            k.op("dve", lambda e: e.tensor_tensor(out=bigv, in0=cif4, in1=thr16.t[:].unsqueeze(1).unsqueeze(1).to_broadcast([128, 8, 16, 16]),
                                                  op=ALU.is_ge), [cif, thr16], [sA])
            k.op("dve", lambda e: e.tensor_reduce(out=a1.t[:], in_=bigf, axis=AX.X, op=ALU.add), [sA], [a1])
            yield
            k.op("dve", lambda e: e.tensor_tensor(out=bigv, in0=bigv, in1=di1.t[:].unsqueeze(2).to_broadcast([128, 8, 16, 16]), op=ALU.mult),
                 [sA, di1], [sA])
            k.op("dve", lambda e: e.tensor_reduce(out=i1s.t[:], in_=bigf, axis=AX.X, op=ALU.add), [sA], [i1s])
            k.op("dve", lambda e: e.scalar_tensor_tensor(out=bsel.t[:], in0=a1.t[:], scalar=-16.0, in1=cif.t[:], op0=ALU.mult, op1=ALU.add),
                 [a1, cif], [bsel])
            k.op("dve", lambda e: e.tensor_scalar_add(out=bsel.t[:], in0=bsel.t[:], scalar1=16.0), [bsel], [bsel])
            yield
            bs4 = bsel.t[:].rearrange("p (h k) -> p h k", h=8).unsqueeze(3).to_broadcast([128, 8, 16, 16])
            k.op("dve", lambda e: e.tensor_tensor(out=bigv, in0=bs4, in1=io16.t[:].unsqueeze(1).unsqueeze(1).to_broadcast([128, 8, 16, 16]),
                                                  op=ALU.is_equal), [bsel, io16], [sA])
            k.op("dve", lambda e: e.tensor_tensor(out=bigv, in0=bigv, in1=i4[:, :, 1, :].unsqueeze(2).to_broadcast([128, 8, 16, 16]),
                                                  op=ALU.mult), [sA, i16f], [sA])
            k.op("dve", lambda e: e.tensor_reduce(out=i2s.t[:], in_=bigf, axis=AX.X, op=ALU.add), [sA], [i2s])
            yield
            k.op("dve", lambda e: e.scalar_tensor_tensor(out=i1s.t[:], in0=i1s.t[:], scalar=128.0, in1=i2s.t[:], op0=ALU.mult, op1=ALU.add),
                 [i1s, i2s], [i1s])
            k.op("dve", lambda e: e.tensor_copy(out=ei.t[:], in_=i1s.t[:]), [i1s], [ei])
            k.op("dve", lambda e: e.tensor_tensor(out=gw.t[:], in0=sc16.t[:], in1=sc16.t[:, :, 0:1].to_broadcast([128, 8, 16]), op=ALU.subtract),
                 [sc16], [gw])
            k.op("act", lambda e: e.activation(out=gw.t[:], in_=gw.t[:], func=AF.Exp), [gw], [gw])
            k.op("dve", lambda e: e.tensor_reduce(out=zs.t[:], in_=gw.t[:], axis=AX.X, op=ALU.add), [gw], [zs])
            k.op("dve", lambda e: e.reciprocal(out=zs.t[:], in_=zs.t[:]), [zs], [zs])
            k.op("dve", lambda e: e.tensor_tensor(out=gw.t[:], in0=gw.t[:], in1=zs.t[:].unsqueeze(2).to_broadcast([128, 8, 16]), op=ALU.mult),
                 [gw, zs], [gw])
            yield

        def drain(gen):
            if gen is not None:
                for _ in gen:
                    pass

        gcnt = 0
        dcnt = 0
        drain(front(0))
        for ti in range(NT):
            smp = ti == NT - 1
            x, h2, ei, gw = x1[ti % 2], h2s[ti % 2], eis[ti % 2], gws[ti % 2]
            nxt = front(ti + 1) if ti + 1 < NT else None
            if ti == 0 or smp:
                md = modS if smp else modP
                k.dma("sp", G2.t[:], md.t[:, 5 * D:6 * D], [md], [G2])
            ngrp = 128 // GS
            tiles = {}

            def tail(gi_):
                nonlocal dcnt
                cs_ = slice(gi_ * GS, (gi_ + 1) * GS)
                k.op("act", lambda e: e.activation(out=actv.t[:, cs_], in_=dots.t[:, cs_], func=AF.Gelu), [dots], [actv])
                k.op("dve", lambda e: e.tensor_tensor(out=gco.t[:, cs_], in0=actv.t[:, cs_], in1=gw.t[:].rearrange("p h k -> p (h k)")[:, cs_],
                                                      op=ALU.mult), [actv, gw], [gco])
                for c_ in range(gi_ * GS, (gi_ + 1) * GS):
                    gt = tiles.pop(c_)
                    d_ = dg[dcnt % 8]
                    dcnt += 1
                    k.op("act", lambda e, d_=d_, c_=c_: e.activation(out=d_.t[:], in_=identb.t[:], func=AF.Identity, scale=gco.t[:, c_:c_ + 1]),
                         [identb, gco], [d_])
                    for hf in range(2):
                        k.op("pe", lambda e, d_=d_, gt=gt, hf=hf, c_=c_: e.matmul(pF[hf].t[:, :], lhsT=d_.t[:], rhs=gt.t[:, 1, hf * 512:(hf + 1) * 512],
                                                                               start=(c_ == 0), stop=(c_ == 127)), [d_, gt], [pF[hf]])

            for gi_ in range(ngrp):
                for c_ in range(gi_ * GS, (gi_ + 1) * GS):
                    gt = gb_[gcnt % NG]
                    gcnt += 1
                    tiles[c_] = gt
                    k.dma("pool", gt.t[:].rearrange("p a n -> p (a n)"), PUV.t, [ei] + PUV.bs, [gt], indirect=ei.t[:, c_:c_ + 1])
                    k.op("dve", lambda e, gt=gt, c_=c_: e.scalar_tensor_tensor(out=junk.t[:], in0=gt.t[:, 0, :], scalar=1.0, in1=h2.t[:],
                                                                               op0=ALU.mult, op1=ALU.mult, accum_out=dots.t[:, c_:c_ + 1]),
                         [gt, h2], [junk, dots])
                if gi_ > 0:
                    tail(gi_ - 1)
                if nxt is not None:
                    next(nxt, None)
            tail(ngrp - 1)
            drain(nxt)
            for hf in range(2):
                hs = slice(hf * 512, (hf + 1) * 512)
                k.op("dve", lambda e, hf=hf, hs=hs: e.tensor_tensor(out=ff.t[:, hs], in0=pF[hf].t[:], in1=G2.t[:, hs], op=ALU.mult),
                     [pF[hf], G2], [ff])
            k.op("dve", lambda e, x=x: e.scalar_tensor_tensor(out=ff.t[:], in0=x.t[:], scalar=DN_ALPHA, in1=ff.t[:], op0=ALU.mult, op1=ALU.add),
                 [x, ff], [ff])
            layer_norm(lns, ff.t[:], [ff], l2w.t[:], l2b.t[:], [l2w, l2b], yt.t[:], [yt], "c2")
            if smp:
                k.dma("sp", O["ys"], yt.t[0:16, :], [yt], [dram_out])
            else:
                k.dma("sp", O["yo"][ti * 128:(ti + 1) * 128, :], yt.t[:], [yt], [dram_out])
        k.barrier()
        st.close()

    phase0()
    phaseS()
    phaseA()
    phaseG()
    phaseB()
    phaseC()
    k.barrier()
    g.close()
    k.es.close()
    return nc


_NC = None


def _consts():
    identf = np.eye(128, dtype=np.float32)
    s = np.arange(128)[:, None]
    t = np.arange(128)[None, :]
    tri = (s <= t).astype(np.float32)
    tris = (s > t).astype(np.float32)
    slopes = np.exp2(-8.0 * np.arange(1, 13, dtype=np.float32) / 12).astype(np.float32)
    biasT = np.zeros((128, 12, 256), np.float32)
    ki = np.arange(128)[:, None]
    qi = np.arange(128)[None, :]
    for gi, dil in enumerate(DILS):
        for j in range(4):
            sl = slopes[gi * 4 + j]
            st_prev = qi + 128 - ki
            st_cur = qi - ki
            bp = np.where(st_prev <= 128, -sl * st_prev * dil, -30000.0)
            bc = np.where(st_cur >= 0, -sl * st_cur * dil, -30000.0)
            biasT[:, gi * 4 + j, 0:128] = bp
            biasT[:, gi * 4 + j, 128:256] = bc
    iota16 = np.tile(np.arange(16, dtype=np.float32)[None, :], (128, 1))
    biasS = np.zeros((3, 128, 4), np.float32)
    for gi, dil in enumerate(DILS):
        for j in range(4):
            biasS[gi, :, j] = -slopes[gi * 4 + j] * (128 - np.arange(128)) * dil
    return dict(identf=identf, tri=tri, tris=tris, biasT=biasT, iota16=iota16, biasS=biasS.reshape(1, -1))


def kernel(x_prompt, x_sample, c_prompt, c_sample, cache_kv_w128, cache_kv_w512, cache_kv_w2048, state_gla,
           w_ada, b_ada, w_in, w_gla_up, b_gla, gla_norm_w, w_br_a, w_br_b, w_out, ln1_w, ln1_b,
           w_pq, peer_k1, peer_k2, peer_u, peer_v, ln2_w, ln2_b):
    global _NC
    f = lambda a: np.ascontiguousarray(np.asarray(a, dtype=np.float32))
    if _NC is None:
        _NC = build()
    nc = _NC
    cst = _consts()
    shared = dict(w_ada=f(w_ada[0]), b_ada=f(b_ada), w_in=f(w_in[0]), w_up=f(w_gla_up[0]), b_gla=f(b_gla),
                  gnw=f(gla_norm_w), w_br_a=f(w_br_a[0]), w_br_b=f(w_br_b[0]), w_out=f(w_out[0]), ln1_w=f(ln1_w),
                  ln1_b=f(ln1_b), w_pq=f(w_pq[0]), pk1=f(peer_k1[0]), pk2=f(peer_k2[0]), pu=f(peer_u[0]),
                  pv=f(peer_v[0]), ln2_w=f(ln2_w), ln2_b=f(ln2_b), **cst)
    xpr = f(x_prompt)
    in_maps = []
    for c in range(8):
        b, s = c // 4, c % 4
        xp = np.zeros((NPRE, D), np.float32)
        if s > 0:
            xp[NPRE - s * SEG:] = xpr[b, :s * SEG]
        sq = slice(c * 16, (c + 1) * 16)
        cS = np.zeros((128, D), np.float32)
        cS[:16] = c_sample[sq]
        xs = np.zeros((128, D), np.float32)
        xs[:16] = x_sample[sq, 0]
        m = dict(shared)
        m.update(xo=np.ascontiguousarray(xpr[b, s * SEG:(s + 1) * SEG]), xp=xp,
                 segf=np.full((128, 1), float(s), np.float32), cP=f(c_prompt[b:b + 1]), cS=cS, xs=xs,
                 c128=f(cache_kv_w128[0, sq]).reshape(16, 128, 512), c512=f(cache_kv_w512[0, sq]).reshape(16, 512, 512),
                 c2048=f(cache_kv_w2048[0, sq]).reshape(16, 2048, 512), sgla=f(state_gla[0, sq]))
        in_maps.append(m)
    res = run_bass_kernel_spmd(nc, in_maps, core_ids=list(range(8))).results
    yp = np.stack([np.concatenate([res[b * 4 + s]["yo"] for s in range(4)], 0) for b in range(2)])
    ys = np.concatenate([res[c]["ys"] for c in range(8)], 0).reshape(128, 1, D)
    kvp = [np.stack([res[b * 4 + 3]["kvp%d" % w].reshape(w, 2, 4, 64) for b in range(2)])[None] for w in WINS]
    glap = np.stack([res[b * 4 + 3]["glap"] for b in range(2)])[None]
    kvs = [np.concatenate([res[c]["kvs%d" % w] for c in range(8)], 0).reshape(128, w, 2, 4, 64)[None] for w in WINS]
    glas = np.concatenate([res[c]["glas"] for c in range(8)], 0)[None]
    return (yp, ys, kvp[0], kvp[1], kvp[2], glap, kvs[0], kvs[1], kvs[2], glas)
```
